# Optimizing a Trainium2 kernel written in Bass

```python
import jax, jax.numpy as jnp
from jax import lax
import numpy as np

D_MODEL = 1024
BATCH = 16
SEQ = 4096
DEPTH = 2

N_MIXERS = 2
PLE_DIM = 256
D_FF = 4 * D_MODEL
EPS = 1e-6

HG_KDIM = 128
HG_HEADS = D_MODEL // HG_KDIM
HG_VDIM = D_MODEL // HG_HEADS
HG_QK = HG_HEADS * HG_KDIM
HG_V = HG_HEADS * HG_VDIM
HG_CHUNK = 64
N_HGRN = (DEPTH + 1) // 2

MLA_HEADS = 16
MLA_Q_RANK = 384
MLA_KV_RANK = 256
MLA_NOPE = 128
MLA_ROPE = 64
MLA_V = 128
ROPE_BASE = 10000.0
Q_BLOCK = 128
N_MLA = DEPTH // 2

kernel_name = "hybrid_hgrn2_mla_encoder"


def rms_norm(x, gain):
    xf = x.astype(jnp.float32)
    y = xf * lax.rsqrt(jnp.mean(xf * xf, axis=-1, keepdims=True) + EPS)
    return (y * gain.astype(jnp.float32)).astype(x.dtype)


def gla_chunk_scan(q, k, v, g):
    b, h, s, dk = q.shape
    dv = v.shape[-1]
    nc = s // HG_CHUNK

    def to_chunks(t):
        return jnp.moveaxis(t.reshape(b, h, nc, HG_CHUNK, t.shape[-1]), 2, 0)

    qc, kc, vc, gc = to_chunks(q), to_chunks(k), to_chunks(v), to_chunks(g)
    mask = jnp.tril(jnp.ones((HG_CHUNK, HG_CHUNK), dtype=bool))

    def step(state, inp):
        qi, ki, vi, gi = inp
        G = jnp.cumsum(gi.astype(jnp.float32), axis=-2)
        G_last = G[..., -1:, :]
        q_dec = qi * jnp.exp(G)
        k_inv = ki * jnp.exp(-G)
        k_end = ki * jnp.exp(G_last - G)
        attn = jnp.where(mask, jnp.einsum('bhik,bhjk->bhij', q_dec, k_inv), 0.0)
        o = (jnp.einsum('bhij,bhjv->bhiv', attn, vi)
             + jnp.einsum('bhik,bhkv->bhiv', q_dec, state))
        new_state = (jnp.exp(G_last[..., 0, :])[..., None] * state
                     + jnp.einsum('bhjk,bhjv->bhkv', k_end, vi))
        return new_state, o

    state0 = jnp.zeros((b, h, dk, dv), jnp.float32)
    _, o = lax.scan(step, state0, (qc, kc, vc, gc))
    return jnp.moveaxis(o, 0, 2).reshape(b, h, s, dv).astype(v.dtype)


def hgrn2_mixer(x, w_in, o_norm, w_out, lb):
    b, s, _ = x.shape
    proj = x @ w_in
    q, f_fw, f_bw, inp, gate = jnp.split(
        proj, [HG_QK, 2 * HG_QK, 3 * HG_QK, 3 * HG_QK + HG_V], axis=-1)

    def heads(t):
        return jnp.transpose(t.reshape(b, s, HG_HEADS, -1), (0, 2, 1, 3))

    q = heads(q) * (HG_KDIM ** -0.5)
    v = heads(inp)
    lbh = lb.astype(jnp.float32).reshape(HG_HEADS, 1, HG_KDIM)

    def gates(f_logit):
        f = lbh + (1.0 - lbh) * jax.nn.sigmoid(heads(f_logit).astype(jnp.float32))
        return 1.0 - f, jnp.log(f)

    k_f, g_f = gates(f_fw)
    k_b, g_b = gates(f_bw)
    flip = lambda t: jnp.flip(t, axis=2)
    o_f = gla_chunk_scan(q, k_f, v, g_f)
    o_b = flip(gla_chunk_scan(flip(q), flip(k_b), flip(v), flip(g_b)))
    o = rms_norm(o_f + o_b, o_norm[:, None, :])
    o = jnp.transpose(o, (0, 2, 1, 3)).reshape(b, s, HG_V)
    return (o * jax.nn.silu(gate)) @ w_out


def rope_angles(positions, dim):
    inv = 1.0 / (ROPE_BASE ** (jnp.arange(0, dim, 2, dtype=jnp.float32) / dim))
    ang = positions.astype(jnp.float32)[..., None] * inv
    return jnp.cos(ang), jnp.sin(ang)


def apply_rope(t, cos, sin):
    t1, t2 = jnp.split(t, 2, axis=-1)
    cos = cos.astype(t.dtype)
    sin = sin.astype(t.dtype)
    return jnp.concatenate([t1 * cos - t2 * sin, t1 * sin + t2 * cos], axis=-1)


def mla_mixer(x, positions, w_in, q_norm, w_uq, kv_norm, w_ukv, w_o):
    b, s, _ = x.shape
    proj = x @ w_in
    cq, ckv, k_rope = jnp.split(proj, [MLA_Q_RANK, MLA_Q_RANK + MLA_KV_RANK], axis=-1)
    cq = rms_norm(cq, q_norm)
    ckv = rms_norm(ckv, kv_norm)
    q = (cq @ w_uq).reshape(b, s, MLA_HEADS, MLA_NOPE + MLA_ROPE)
    q_nope, q_rope = q[..., :MLA_NOPE], q[..., MLA_NOPE:]
    cos, sin = rope_angles(positions, MLA_ROPE)
    q_rope = apply_rope(q_rope, cos[:, :, None, :], sin[:, :, None, :])
    k_rope = apply_rope(k_rope, cos, sin)
    w_ukv_h = w_ukv.reshape(MLA_KV_RANK, MLA_HEADS, MLA_NOPE + MLA_V)
    w_uk, w_uv = w_ukv_h[..., :MLA_NOPE], w_ukv_h[..., MLA_NOPE:]
    q_lat = jnp.einsum('bshn,chn->bshc', q_nope, w_uk)
    scale = (MLA_NOPE + MLA_ROPE) ** -0.5
    nb = s // Q_BLOCK

    def blockify(t):
        return jnp.moveaxis(t.reshape(b, nb, Q_BLOCK, *t.shape[2:]), 1, 0)

    def attend(blk):
        ql, qr = blk
        scores = (jnp.einsum('bqhc,bkc->bhqk', ql, ckv)
                  + jnp.einsum('bqhr,bkr->bhqk', qr, k_rope)) * scale
        probs = jax.nn.softmax(scores.astype(jnp.float32), axis=-1).astype(ckv.dtype)
        return jnp.einsum('bhqk,bkc->bqhc', probs, ckv)

    o_lat = lax.map(attend, (blockify(q_lat), blockify(q_rope)))
    o_lat = jnp.moveaxis(o_lat, 0, 1).reshape(b, s, MLA_HEADS, MLA_KV_RANK)
    o = jnp.einsum('bshc,chv->bshv', o_lat, w_uv).reshape(b, s, MLA_HEADS * MLA_V)
    return o @ w_o


def squared_relu_mlp(x, w1, w2):
    return jnp.square(jax.nn.relu(x @ w1)) @ w2


def setup_inputs(seed: int = 0) -> dict:
    key = jax.random.key(seed)
    ks = jax.random.split(key, 32)
    f32 = jnp.float32

    def nrm(k, shape, fan_in):
        return jax.random.normal(k, shape, f32) * (fan_in ** -0.5)

    def gain(k, shape):
        return 1.0 + 0.05 * jax.random.normal(k, shape, f32)

    x = jax.random.normal(ks[0], (BATCH, SEQ, D_MODEL), f32)
    p = jax.random.normal(ks[1], (DEPTH, BATCH, SEQ, PLE_DIM), f32)
    positions = (jnp.arange(SEQ, dtype=jnp.int32)[None, :]
                 + jax.random.randint(ks[2], (BATCH, 1), 0, SEQ, dtype=jnp.int32))
    return {
        "x": x,
        "p": p,
        "positions": positions,
        "pre_mix_norm": gain(ks[3], (DEPTH, D_MODEL)),
        "post_mix_norm": gain(ks[4], (DEPTH, D_MODEL)),
        "pre_mlp_norm": gain(ks[5], (DEPTH, D_MODEL)),
        "post_mlp_norm": gain(ks[6], (DEPTH, D_MODEL)),
        "w_mlp_in": nrm(ks[7], (DEPTH, D_MODEL, D_FF), D_MODEL),
        "w_mlp_out": nrm(ks[8], (DEPTH, D_FF, D_MODEL), D_FF),
        "w_ple_proj": nrm(ks[9], (DEPTH, PLE_DIM, D_MODEL), PLE_DIM),
        "w_ple_gate": nrm(ks[10], (DEPTH, D_MODEL, D_MODEL), D_MODEL),
        "ple_norm": gain(ks[11], (DEPTH, D_MODEL)),
        "hg_lb_logits": 0.1 * jax.random.normal(ks[12], (DEPTH + 1, HG_QK), f32),
        "hg_w_in": nrm(ks[13], (N_HGRN, D_MODEL, 3 * HG_QK + 2 * HG_V), D_MODEL),
        "hg_o_norm": gain(ks[14], (N_HGRN, HG_HEADS, HG_VDIM)),
        "hg_w_out": nrm(ks[15], (N_HGRN, HG_V, D_MODEL), HG_V),
        "mla_w_in": nrm(ks[16], (N_MLA, D_MODEL, MLA_Q_RANK + MLA_KV_RANK + MLA_ROPE), D_MODEL),
        "mla_q_norm": gain(ks[17], (N_MLA, MLA_Q_RANK)),
        "mla_w_uq": nrm(ks[18], (N_MLA, MLA_Q_RANK, MLA_HEADS * (MLA_NOPE + MLA_ROPE)), MLA_Q_RANK),
        "mla_kv_norm": gain(ks[19], (N_MLA, MLA_KV_RANK)),
        "mla_w_ukv": nrm(ks[20], (N_MLA, MLA_KV_RANK, MLA_HEADS * (MLA_NOPE + MLA_V)), MLA_KV_RANK),
        "mla_w_o": nrm(ks[21], (N_MLA, MLA_HEADS * MLA_V, D_MODEL), MLA_HEADS * MLA_V),
    }


def reference(x, p, positions, pre_mix_norm, post_mix_norm, pre_mlp_norm, post_mlp_norm,
              w_mlp_in, w_mlp_out, w_ple_proj, w_ple_gate, ple_norm,
              hg_lb_logits, hg_w_in, hg_o_norm, hg_w_out,
              mla_w_in, mla_q_norm, mla_w_uq, mla_kv_norm, mla_w_ukv, mla_w_o):
    lb_all = jnp.cumsum(jax.nn.softmax(hg_lb_logits.astype(jnp.float32), axis=0), axis=0)
    h = x
    for i in range(DEPTH):
        a = rms_norm(h, pre_mix_norm[i])
        j = i // N_MIXERS
        if i % N_MIXERS == 0:
            m = hgrn2_mixer(a, hg_w_in[j], hg_o_norm[j], hg_w_out[j], lb_all[i])
        else:
            m = mla_mixer(a, positions, mla_w_in[j], mla_q_norm[j], mla_w_uq[j],
                          mla_kv_norm[j], mla_w_ukv[j], mla_w_o[j])
        h = h + rms_norm(m, post_mix_norm[i])
        f = squared_relu_mlp(rms_norm(h, pre_mlp_norm[i]), w_mlp_in[i], w_mlp_out[i])
        h = h + rms_norm(f, post_mlp_norm[i])
        e = p[i].astype(h.dtype) @ w_ple_proj[i]
        g = jax.nn.sigmoid(h @ w_ple_gate[i])
        h = h + rms_norm(g * e, ple_norm[i])
    return h
```

```python
from contextlib import ExitStack
import numpy as np
import concourse.bass as bass
import concourse.mybir as mybir
from concourse.bass_utils import run_bass_kernel_spmd

F32 = mybir.dt.float32
BF16 = mybir.dt.bfloat16
I32 = mybir.dt.int32
AF = mybir.ActivationFunctionType
ALU = mybir.AluOpType

ENGS = ("pe", "act", "dve", "pool", "sp")

D = 1024
SEQ = 4096
NB = 512
NBLK = SEQ // NB
EPS = 1e-6
NCORES = 8


class Tile:
    __slots__ = ("name", "h", "writers", "readers", "dsem", "dcount", "space", "dwaited")

    def __init__(self, name, h, space):
        self.name = name
        self.h = h
        self.space = space
        self.writers = []
        self.readers = []
        self.dsem = None
        self.dcount = 0

    def __getitem__(self, idx):
        return self.h[idx]


class Instr:
    __slots__ = ("eng", "fn", "idx", "waits", "marked", "is_dma", "dtile", "dval", "vc", "cnt")

    def __init__(self, eng, fn, idx):
        self.eng = eng
        self.fn = fn
        self.idx = idx
        self.waits = []
        self.marked = False
        self.is_dma = False
        self.dtile = None
        self.dval = 0
        self.vc = None
        self.cnt = 0


class Sched:
    def __init__(self, nc, gctx):
        self.nc = nc
        self.gctx = gctx
        self.pctx = None
        self.esem = {e: gctx.enter_context(nc.semaphore("s_" + e)) for e in ENGS}
        self.base = {e: 0 for e in ENGS}
        self.tiles = []
        self.live = []
        self._reset()
        self.n_instr = 0

    def _reset(self):
        self.prog = {e: [] for e in ENGS}
        self.vc = {e: {p: -1 for p in ENGS} for e in ENGS}
        for e in ENGS:
            self.vc[e]["dma"] = {}
        for t in self.live:
            t.writers = []
            t.readers = []
        self.live = []

    def begin(self):
        self.pctx = ExitStack()

    def sbuf(self, name, shape, dt, glob=False):
        c = self.gctx if glob else self.pctx
        self.uid = getattr(self, "uid", 0) + 1
        name = f"{name}_{self.uid}"
        h = c.enter_context(self.nc.sbuf_tensor(name, list(shape), dt))
        return Tile(name, h, "sb")

    def psum(self, name, shape, dt=F32, glob=False):
        c = self.gctx if glob else self.pctx
        h = c.enter_context(self.nc.psum_tensor(name, list(shape), dt))
        return Tile(name, h, "ps")

    def dram(self, name, shape, dt, kind="Internal"):
        h = self.nc.dram_tensor(name, list(shape), dt, kind=kind)
        return Tile(name, h.ap(), "dr")

    def _dep(self, ins, d):
        if d is ins:
            return
        e = ins.eng
        if d.is_dma:
            known = self.vc[e]["dma"]
            tot = d.dtile.dcount
            if known.get(id(d.dtile), 0) >= tot:
                return
            known[id(d.dtile)] = tot
            ins.waits.append((d.dtile, tot))
            return
        p = d.eng
        if p == "pe" and e == "pe":
            return
        if self.vc[e][p] >= d.idx:
            return
        ins.waits.append(d)
        d.marked = True
        self.vc[e][p] = d.idx
        for k, v in d.vc.items():
            if k == "dma":
                known = self.vc[e]["dma"]
                for kk, vv in v.items():
                    if known.get(kk, 0) < vv:
                        known[kk] = vv
            elif self.vc[e][k] < v:
                self.vc[e][k] = v

    def op(self, eng, fn, reads=(), writes=(), acc=()):
        ins = Instr(eng, fn, len(self.prog[eng]))
        for t in reads:
            for w in t.writers:
                self._dep(ins, w)
            if t.space == "ps":
                for r in t.readers:
                    if r.eng != eng:
                        self._dep(ins, r)
        for t in writes:
            for w in t.writers:
                self._dep(ins, w)
            for r in t.readers:
                self._dep(ins, r)
        for t in acc:
            for r in t.readers:
                self._dep(ins, r)
        snap = {}
        for k, v in self.vc[eng].items():
            snap[k] = dict(v) if k == "dma" else v
        ins.vc = snap
        for t in reads:
            if not t.readers and not t.writers:
                self.live.append(t)
            t.readers.append(ins)
        for t in writes:
            if not t.readers and not t.writers:
                self.live.append(t)
            t.writers = [ins]
            t.readers = []
        for t in acc:
            if not t.readers and not t.writers:
                self.live.append(t)
            if t.readers:
                t.writers = [ins]
                t.readers = []
            else:
                t.writers.append(ins)
        self.prog[eng].append(ins)
        return ins

    def dma(self, eng, out_t, out_ap, in_t, in_ap, group=False, xw=(), sem_tile=None):
        st = out_t if out_t.space == "sb" else (in_t if in_t.space == "sb" else out_t)
        if sem_tile is not None:
            st = sem_tile

        def fn(e, out_ap=out_ap, in_ap=in_ap):
            return e.dma_start(out=out_ap, in_=in_ap)

        if group:
            ins = self.op(eng, fn, reads=[in_t], acc=[out_t], writes=list(xw))
        else:
            ins = self.op(eng, fn, reads=[in_t], writes=[out_t] + list(xw))
        ins.is_dma = True
        ins.dtile = st
        if st.dsem is None:
            st.dsem = self.gctx.enter_context(self.nc.semaphore("d_" + st.name))
            self.tiles.append(st)
        st.dcount += 16
        ins.dval = st.dcount
        return ins

    def mm(self, ps_t, out_ap, l_t, l_ap, r_t, r_ap, start, stop):
        def fn(e):
            return e.matmul(out_ap, lhsT=l_ap, rhs=r_ap, start=start, stop=stop)
        if start:
            return self.op("pe", fn, reads=[l_t, r_t], writes=[ps_t])
        return self.op("pe", fn, reads=[l_t, r_t], acc=[ps_t])

    def mmg(self, ps_t, out_ap, l_t, l_ap, r_t, r_ap, start, stop):
        def fn(e):
            return e.matmul(out_ap, lhsT=l_ap, rhs=r_ap, start=start, stop=stop)
        return self.op("pe", fn, reads=[l_t, r_t], acc=[ps_t])

    def tr(self, ps_t, out_ap, in_t, in_ap, id_t, id_ap):
        def fn(e):
            return e.transpose(out=out_ap, in_=in_ap, identity=id_ap)
        return self.op("pe", fn, reads=[in_t, id_t], acc=[ps_t])

    def act(self, out_t, out_ap, in_t, in_ap, func, bias=None, scale=None, rd=()):
        kw = {}
        if bias is not None:
            kw["bias"] = bias
        if scale is not None:
            kw["scale"] = scale

        def fn(e):
            return e.activation(out=out_ap, in_=in_ap, func=func, **kw)
        return self.op("act", fn, reads=[in_t] + list(rd), writes=[out_t])

    def tsc(self, eng, out_t, out_ap, in_t, in_ap, s1, s2, op0, op1=None, rd=()):
        def fn(e):
            if op1 is None:
                return e.tensor_scalar(out=out_ap, in0=in_ap, scalar1=s1, scalar2=None, op0=op0)
            return e.tensor_scalar(out=out_ap, in0=in_ap, scalar1=s1, scalar2=s2, op0=op0, op1=op1)
        return self.op(eng, fn, reads=[in_t] + list(rd), writes=[out_t])

    def stt(self, out_t, out_ap, a_t, a_ap, scalar, b_t, b_ap, op0, op1, rd=()):
        def fn(e):
            return e.scalar_tensor_tensor(out=out_ap, in0=a_ap, scalar=scalar, in1=b_ap, op0=op0, op1=op1)
        return self.op("dve", fn, reads=[a_t, b_t] + list(rd), writes=[out_t])

    def tt(self, eng, out_t, out_ap, a_t, a_ap, b_t, b_ap, op):
        def fn(e):
            return e.tensor_tensor(out=out_ap, in0=a_ap, in1=b_ap, op=op)
        return self.op(eng, fn, reads=[a_t, b_t], writes=[out_t])

    def cp(self, eng, out_t, out_ap, in_t, in_ap):
        if eng == "act":
            return self.act(out_t, out_ap, in_t, in_ap, AF.Copy)

        def fn(e):
            return e.tensor_copy(out=out_ap, in_=in_ap)
        return self.op(eng, fn, reads=[in_t], writes=[out_t])

    def memset(self, eng, t, ap, val):
        def fn(e):
            return e.memset(ap, val)
        return self.op(eng, fn, writes=[t])

    def flush(self, final=False):
        nc = self.nc
        esem = self.esem
        for e in ENGS:
            lst = self.prog[e]
            for ins in reversed(lst):
                if not ins.is_dma:
                    ins.marked = True
                    break
            c = self.base[e]
            for ins in lst:
                if ins.is_dma:
                    continue
                if ins.marked:
                    c += 1
                    ins.cnt = c
            self.base[e] = c
            self.n_instr += len(lst)

        def ev(d):
            if isinstance(d, tuple):
                return d[0].dsem, d[1]
            return esem[d.eng], d.cnt

        dma_tails = [(t.dsem, t.dcount) for t in self.tiles if t.dcount]
        eng_tails = [(esem[e], self.base[e]) for e in ENGS if self.base[e] > 0]

        with nc.Block() as block:
            def run(eng_obj, name):
                for ins in self.prog[name]:
                    for d in ins.waits:
                        s, v = ev(d)
                        eng_obj.wait_ge(s, v)
                    bi = ins.fn(eng_obj)
                    if ins.is_dma:
                        bi.then_inc(ins.dtile.dsem, 16)
                    elif ins.marked:
                        bi.then_inc(esem[name], 1)
                for s, v in eng_tails:
                    if s is not esem[name]:
                        eng_obj.wait_ge(s, v)
                for s, v in dma_tails:
                    eng_obj.wait_ge(s, v)

            @block.tensor
            def _(e):
                run(e, "pe")

            @block.scalar
            def _(e):
                run(e, "act")

            @block.vector
            def _(e):
                run(e, "dve")

            @block.gpsimd
            def _(e):
                run(e, "pool")

            @block.sync
            def _(e):
                run(e, "sp")

        self._reset()
        if self.pctx is not None:
            self.pctx.close()
            self.pctx = None


def lay_oc(W, cw=128):
    K, N = W.shape
    KC, OC = K // 128, N // cw
    return np.ascontiguousarray(W.reshape(KC, 128, OC, cw).transpose(2, 1, 0, 3).reshape(OC, 128, KC * cw))


def vec_cols(v):
    C = v.shape[0] // 128
    return np.ascontiguousarray(v.reshape(C, 128).T)


V_PRE_MIX, V_POST_MIX, V_PRE_MLP, V_POST_MLP, V_PLE = 0, 16, 32, 48, 64
V_ONORM = 80
V_QNORM = 88
V_KVNORM = 91
V_LBL = 93
NVEC = 117

C_ID = 0
C_MF = 128
C_MB = 256
C_SCAN = 384
C_INVF = 896
NCONST = 897


def make_consts():
    c = np.zeros((128, NCONST), np.float32)
    c[:, C_ID:C_ID + 128] = np.eye(128, dtype=np.float32)
    j = np.arange(128)[:, None]
    i = np.arange(128)[None, :]
    same = (j // 64) == (i // 64)
    c[:, C_MF:C_MF + 128] = (same & (i >= j)).astype(np.float32)
    c[:, C_MB:C_MB + 128] = (same & (i <= j)).astype(np.float32)
    sm = np.ones(512, np.float32)
    sm[::64] = 0.0
    c[:, C_SCAN:C_SCAN + 512] = sm[None, :]
    inv = (1.0 / (10000.0 ** (np.arange(0, 64, 2, dtype=np.float32) / 64.0))).astype(np.float32)
    c[0:32, C_INVF] = inv
    c[32:64, C_INVF] = inv
    return c


def prep_weights(inp):
    w = {}
    w["hg_w_in"] = lay_oc(inp["hg_w_in"][0])
    w["hg_w_out"] = lay_oc(inp["hg_w_out"][0])
    for l in range(2):
        w[f"w1_{l}"] = lay_oc(inp["w_mlp_in"][l])
        w[f"w2_{l}"] = lay_oc(inp["w_mlp_out"][l])
        w[f"wpp_{l}"] = lay_oc(inp["w_ple_proj"][l])
        w[f"wpg_{l}"] = lay_oc(inp["w_ple_gate"][l])
    win = inp["mla_w_in"][0]
    kr = win[:, 640:704]
    kr_sw = np.concatenate([kr[:, 32:64], kr[:, 0:32]], axis=1)
    w["mla_w_in"] = lay_oc(np.concatenate([win, kr_sw], axis=1))
    wuq = inp["mla_w_uq"][0].reshape(384, 16, 192)
    nope = wuq[:, :, 0:128]
    rope = wuq[:, :, 128:192]
    rope_sw = np.concatenate([rope[:, :, 32:64], rope[:, :, 0:32]], axis=2)
    wq = np.concatenate([nope, rope, rope_sw], axis=2).reshape(384, 16 * 256)
    w["mla_w_uq"] = lay_oc(wq, cw=256)
    w["mla_w_ukv"] = lay_oc(inp["mla_w_ukv"][0], cw=256)
    w["mla_w_o"] = lay_oc(inp["mla_w_o"][0])
    vecs = np.zeros((128, NVEC), np.float32)
    for l in range(2):
        vecs[:, V_PRE_MIX + 8 * l:V_PRE_MIX + 8 * l + 8] = vec_cols(inp["pre_mix_norm"][l])
        vecs[:, V_POST_MIX + 8 * l:V_POST_MIX + 8 * l + 8] = vec_cols(inp["post_mix_norm"][l])
        vecs[:, V_PRE_MLP + 8 * l:V_PRE_MLP + 8 * l + 8] = vec_cols(inp["pre_mlp_norm"][l])
        vecs[:, V_POST_MLP + 8 * l:V_POST_MLP + 8 * l + 8] = vec_cols(inp["post_mlp_norm"][l])
        vecs[:, V_PLE + 8 * l:V_PLE + 8 * l + 8] = vec_cols(inp["ple_norm"][l])
    vecs[:, V_ONORM:V_ONORM + 8] = inp["hg_o_norm"][0].T
    vecs[:, V_QNORM:V_QNORM + 3] = vec_cols(inp["mla_q_norm"][0])
    vecs[:, V_KVNORM:V_KVNORM + 2] = vec_cols(inp["mla_kv_norm"][0])
    for j in range(3):
        vecs[:, V_LBL + 8 * j:V_LBL + 8 * j + 8] = vec_cols(inp["hg_lb_logits"][j])
    w["vecs"] = vecs
    w["consts"] = make_consts()
    return w


WSHAPES = {
    "hg_w_in": [40, 128, 1024], "hg_w_out": [8, 128, 1024],
    "w1_0": [32, 128, 1024], "w2_0": [8, 128, 4096], "wpp_0": [8, 128, 256], "wpg_0": [8, 128, 1024],
    "w1_1": [32, 128, 1024], "w2_1": [8, 128, 4096], "wpp_1": [8, 128, 256], "wpg_1": [8, 128, 1024],
    "mla_w_in": [6, 128, 1024], "mla_w_uq": [16, 128, 768], "mla_w_ukv": [16, 128, 512],
    "mla_w_o": [8, 128, 2048],
}


def build(nseq=2, stop_after="F", dump=()):
    nc = bass.Bass("TRN2", target_bir_lowering=False)
    T = nseq * SEQ
    nblk = nseq * NBLK
    with ExitStack() as g:
        S = Sched(nc, g)
        x_d = S.dram("x", [nseq, SEQ, D], F32, kind="ExternalInput")
        p_d = S.dram("p", [2, nseq, SEQ, 256], F32, kind="ExternalInput")
        pos_d = S.dram("pos", [nseq, SEQ], I32, kind="ExternalInput")
        vecs_d = S.dram("vecs", [128, NVEC], F32, kind="ExternalInput")
        consts_d = S.dram("consts", [128, NCONST], F32, kind="ExternalInput")
        wf = {k: S.dram(k, shp, F32, kind="ExternalInput") for k, shp in WSHAPES.items()}
        out_d = S.dram("out", [nseq, SEQ, D], F32, kind="ExternalOutput")

        def scratch(name, shape, dt):
            return S.dram(name, shape, dt, kind="ExternalOutput" if name in dump else "Internal")

        wb = {k: S.dram(k + "_b", shp, BF16) for k, shp in WSHAPES.items()}
        HT = scratch("HT", [D, T], F32)
        AT = scratch("AT", [D, T], BF16)
        YT = scratch("YT", [2048, T], BF16)
        QN = scratch("QN", [16, 128, T], BF16)
        QR = scratch("QR", [16, 65, T], BF16)

        cst = S.sbuf("cst", [128, NCONST], F32, glob=True)
        vec = S.sbuf("vec", [128, NVEC], F32, glob=True)
        identb = S.sbuf("identb", [128, 128], BF16, glob=True)
        identf = S.sbuf("identf", [128, 128], F32, glob=True)
        onesb = S.sbuf("onesb", [128, 128], BF16, glob=True)
        epsb = S.sbuf("epsb", [128, 1], F32, glob=True)
        lbt = S.sbuf("lbt", [128, 16], F32, glob=True)
        ps = [S.psum(f"ps{i}", [128, 512], F32, glob=True) for i in range(8)]

        def hview(dr, t0, n=NB):
            return dr[:, t0:t0 + n].rearrange("(k p) t -> p k t", p=128)

        S.begin()
        S.dma("sp", cst, cst[:], consts_d, consts_d[:])
        S.dma("sp", vec, vec[:], vecs_d, vecs_d[:])
        toks = [Tile(f"tok{i}", None, "dr") for i in range(2)]
        ndw = 0
        import os as _os
        for k in WSHAPES:
            if "w" in _os.environ.get("DEV_SKIP", ""):
                break
            n0, _, el = WSHAPES[k]
            step = max(1, min(n0, 4096 // el))
            for o in range(0, n0, step):
                o2 = min(n0, o + step)
                S.dma("pool", wb[k], wb[k][o:o2], wf[k], wf[k][o:o2], group=True, xw=[toks[ndw % 2]], sem_tile=toks[ndw % 2])
                ndw += 1
        S.cp("dve", identb, identb[:], cst, cst[:, C_ID:C_ID + 128])
        S.cp("dve", identf, identf[:], cst, cst[:, C_ID:C_ID + 128])
        S.memset("dve", onesb, onesb[:], 1.0)
        S.memset("dve", epsb, epsb[:], EPS)
        ex = S.sbuf("ex", [128, 24], F32)
        sm = S.sbuf("sm", [128, 8], F32)
        S.act(ex, ex[:], vec, vec[:, V_LBL:V_LBL + 24], AF.Exp)
        S.tt("dve", sm, sm[:], ex, ex[:, 0:8], ex, ex[:, 8:16], ALU.add)
        S.tt("dve", sm, sm[:], sm, sm[:], ex, ex[:, 16:24], ALU.add)
        S.op("dve", lambda e: e.reciprocal(out=sm[:], in_=sm[:]), reads=[sm], writes=[sm])
        S.tt("dve", lbt, lbt[:, 0:8], ex, ex[:, 0:8], sm, sm[:], ALU.mult)
        S.tsc("dve", lbt, lbt[:, 8:16], lbt, lbt[:, 0:8], -1.0, 1.0, ALU.mult, ALU.add)
        if stop_after == "W":
            dbg = S.sbuf("dbg", [128, 16], F32)
            S.cp("dve", dbg, dbg[:], lbt, lbt[:])
            S.dma("sp", out_d, out_d[0, 0:128, 0:16], dbg, dbg[:])
        import os as _os
        if not _os.environ.get("DEV_NOFLUSHW"):
            S.flush()
        if stop_after == "W":
            return nc

        def rstd_from_sq(sq_t, sq_aps, nfeat, bank, rstd_t, rstd_ap, kparts=None):
            n = len(sq_aps)
            for c, ap in enumerate(sq_aps):
                kp = 128 if kparts is None else kparts[c]
                S.mm(bank, bank[:, :], onesb, onesb[0:kp, :], sq_t, ap, start=(c == 0), stop=(c == n - 1))
            S.act(rstd_t, rstd_ap, bank, bank[:, :], AF.Ln, bias=epsb[:, 0:1], scale=1.0 / nfeat, rd=[epsb])
            S.act(rstd_t, rstd_ap, rstd_t, rstd_ap, AF.Exp, scale=-0.5)

        if not _os.environ.get("DEV_NOFLUSHW"):
            S.begin()
        xin = [S.sbuf(f"xin{i}", [128, 4, D], F32) for i in range(2)]
        hTs = [S.sbuf(f"hT{i}", [128, 8, NB], F32) for i in range(2)]
        sqs = [S.sbuf(f"sq{i}", [128, 8, NB], BF16) for i in range(2)]
        aTs = [S.sbuf(f"aT{i}", [128, 8, NB], BF16) for i in range(2)]
        rstds = [S.sbuf(f"rstd{i}", [128, NB], F32) for i in range(2)]

        def load_x(bi):
            s, b = divmod(bi, NBLK)
            t = xin[bi % 2]
            if "x" in _os.environ.get("DEV_SKIP", ""):
                return
            S.dma("sp", t, t[:], x_d, x_d[s, b * NB:(b + 1) * NB, :].rearrange("(j p) d -> p j d", p=128))

        import os as _os
        nblkA = int(_os.environ.get("DEV_NBLK", nblk))
        skipA = _os.environ.get("DEV_SKIP", "")
        load_x(0)
        for bi in range(nblkA):
            if bi + 1 < nblkA:
                load_x(bi + 1)
            xt, hT, sq, aT, rstd = xin[bi % 2], hTs[bi % 2], sqs[bi % 2], aTs[bi % 2], rstds[bi % 2]
            for kc in range(8):
                bank = ps[kc % 4]
                for j in range(4):
                    if "t" in skipA:
                        continue
                    if j == 0:
                        S.op("pe", (lambda o, i_: (lambda e: e.transpose(out=o, in_=i_, identity=identf[:])))(bank[:, 0:128], xt[:, 0, kc * 128:(kc + 1) * 128]), reads=[xt, identf], writes=[bank])
                    else:
                        S.tr(bank, bank[:, j * 128:(j + 1) * 128], xt, xt[:, j, kc * 128:(kc + 1) * 128], identf, identf[:])
                if "c" not in skipA:
                    S.cp("dve", hT, hT[:, kc, :], bank, bank[:, :])
                if "s" not in skipA:
                    S.act(sq, sq[:, kc, :], bank, bank[:, :], AF.Square)
            if "h" not in skipA:
                S.dma("sp", HT, hview(HT, bi * NB), hT, hT[:], group=True)
            if "n" in skipA:
                continue
            rstd_from_sq(sq, [sq[:, c, :] for c in range(8)], D, ps[4 + bi % 2], rstd, rstd[:])
            for kc in range(8):
                S.stt(aT, aT[:, kc, :], hT, hT[:, kc, :], vec[:, V_PRE_MIX + kc:V_PRE_MIX + kc + 1], rstd, rstd[:], ALU.mult, ALU.mult, rd=[vec])
            if "a" not in skipA:
                S.dma("sp", AT, hview(AT, bi * NB), aT, aT[:], group=True)
        S.flush()
        if stop_after == "A":
            return nc

        S.begin()
        HSCALE = 128.0 ** -0.5
        wts = [S.sbuf(f"wB{i}", [128, 5, 1024], BF16) for i in range(2)]
        ats = [S.sbuf(f"atB{i}", [128, 8, NB], BF16) for i in range(2)]
        qdec = [S.sbuf(f"qdec{d}", [128, SEQ], BF16) for d in range(2)]
        kend_tok = [S.sbuf(f"kend{d}", [128, 32, 128], BF16) for d in range(2)]
        vtok = S.sbuf("vtok", [128, 32, 128], BF16)
        attnT = [S.sbuf(f"attnT{d}", [128, 32, 128], BF16) for d in range(2)]
        sgT = S.sbuf("sgT", [128, SEQ], BF16)
        egl = [S.sbuf(f"egl{d}", [128, 64], F32) for d in range(2)]
        Sbf = [S.sbuf(f"Sbf{d}", [128, 64, 128], BF16) for d in range(2)]
        Sst = [[S.sbuf(f"Sst{d}_{i}", [128, 128], F32) for i in range(2)] for d in range(2)]
        tA = [S.sbuf(f"tA{d}", [128, NB], F32) for d in range(2)]
        tG = [S.sbuf(f"tG{d}", [128, NB], F32) for d in range(2)]
        tP = [S.sbuf(f"tP{d}", [128, NB], F32) for d in range(2)]
        tX = [S.sbuf(f"tX{d}", [128, NB], F32) for d in range(2)]
        tEa = [S.sbuf(f"tEa{d}", [128, NB], F32) for d in range(2)]
        tEb = [S.sbuf(f"tEb{d}", [128, NB], F32) for d in range(2)]
        tEe = [S.sbuf(f"tEe{d}", [128, NB], F32) for d in range(2)]
        kinvT = [S.sbuf(f"kinvT{d}", [128, NB], BF16) for d in range(2)]
        kendT = [S.sbuf(f"kendT{d}", [128, NB], BF16) for d in range(2)]
        qsb = S.sbuf("qsb", [128, NB], F32)
        osq = [S.sbuf(f"osq{i}", [128, NB], BF16) for i in range(2)]
        orstd = S.sbuf("orstd", [128, NB], F32)
        otmp = S.sbuf("otmp", [128, NB], F32)
        yblk = [S.sbuf(f"yblk{i}", [128, NB], BF16) for i in range(2)]
        scanm = cst[:, C_SCAN:C_SCAN + NB]

        def c3(ap):
            return ap.rearrange("p (c t) -> p c t", t=64)

        it = 0
        abi = 0
        for s in range(nseq):
            for h in range(8):
                wt = wts[it % 2]
                for j, oc in enumerate([h, 8 + h, 16 + h, 24 + h, 32 + h]):
                    S.dma("sp", wt, wt[:, j, :], wb["hg_w_in"], wb["hg_w_in"][oc], group=True)
                def load_at(b):
                    at = ats[(abi0 + b) % 2]
                    S.dma("sp", at, at[:], AT, hview(AT, s * SEQ + b * NB))

                def proj(b):
                    at = ats[(abi0 + b) % 2]
                    for j, bank in ((1, ps[0]), (2, ps[1]), (0, ps[2]), (4, ps[3])):
                        for kc in range(8):
                            S.mm(bank, bank[:, :], wt, wt[:, j, kc * 128:(kc + 1) * 128], at, at[:, kc, :], start=(kc == 0), stop=(kc == 7))
                    for jj in range(4):
                        for kc in range(8):
                            S.mmg(ps[4], ps[4][:, jj * 128:(jj + 1) * 128], at, at[:, kc, jj * 128:(jj + 1) * 128], wt, wt[:, 3, kc * 128:(kc + 1) * 128], start=(kc == 0), stop=(kc == 7))

                abi0 = abi
                load_at(0)
                proj(0)
                for b in range(NBLK):
                    abi += 1
                    if b + 1 < NBLK:
                        load_at(b + 1)
                    blk = slice(b * NB, (b + 1) * NB)
                    for d in range(2):
                        S.act(tA[d], tA[d][:], ps[d], ps[d][:, :], AF.Sigmoid, scale=-1.0)
                    S.act(sgT, sgT[:, blk], ps[3], ps[3][:, :], AF.Silu)
                    S.cp("dve", vtok, vtok[:, b * 4:(b + 1) * 4, :], ps[4], ps[4][:, :].rearrange("p (j v) -> p j v", v=128))
                    S.cp("dve", qsb, qsb[:], ps[2], ps[2][:, :])
                    if b + 1 < NBLK:
                        proj(b + 1)
                    for d in range(2):
                        S.tsc("dve", tA[d], tA[d][:], tA[d], tA[d][:], lbt[:, 8 + h:9 + h], None, ALU.mult, rd=[lbt])
                    for d in range(2):
                        S.act(tG[d], tG[d][:], tA[d], tA[d][:], AF.Ln, bias=1.0, scale=-1.0)
                    for d in range(2):
                        S.op("dve", (lambda o, m, g_: (lambda e: e.tensor_tensor_scan(out=o, data0=m, data1=g_, initial=0.0, op0=ALU.mult, op1=ALU.add)))(tP[d][:], scanm, tG[d][:]), reads=[cst, tG[d]], writes=[tP[d]])
                    Tb0 = c3(tP[0][:])[:, :, 63:64].to_broadcast([128, 8, 64])
                    Tb1 = c3(tP[1][:])[:, :, 63:64].to_broadcast([128, 8, 64])
                    S.tt("dve", tX[0], c3(tX[0][:]), tP[0], Tb0, tP[0], c3(tP[0][:]), ALU.subtract)
                    S.tt("dve", tEe[1], tEe[1][:], tP[1], tP[1][:], tG[1], tG[1][:], ALU.subtract)
                    S.tt("dve", tX[1], c3(tX[1][:]), tP[1], Tb1, tEe[1], c3(tEe[1][:]), ALU.subtract)
                    Gt = (tP[0], tX[1])
                    S.act(tEa[0], tEa[0][:], Gt[0], Gt[0][:], AF.Exp)
                    S.act(tEb[0], tEb[0][:], Gt[0], Gt[0][:], AF.Exp, scale=-1.0)
                    S.act(tEe[0], tEe[0][:], tX[0], tX[0][:], AF.Exp)
                    S.act(tEe[1], tEe[1][:], tEe[1], tEe[1][:], AF.Exp)
                    S.act(tEa[1], tEa[1][:], Gt[1], Gt[1][:], AF.Exp)
                    S.act(tEb[1], tEb[1][:], Gt[1], Gt[1][:], AF.Exp, scale=-1.0)
                    for d in range(2):
                        S.stt(qdec[d], qdec[d][:, blk], qsb, qsb[:], HSCALE, tEa[d], tEa[d][:], ALU.mult, ALU.mult)
                        S.tt("dve", kinvT[d], kinvT[d][:], tA[d], tA[d][:], tEb[d], tEb[d][:], ALU.mult)
                        S.tt("pool", kendT[d], kendT[d][:], tA[d], tA[d][:], tEe[d], tEe[d][:], ALU.mult)
                        col = 63 if d == 0 else 0
                        S.cp("pool", egl[d], egl[d][:, b * 8:(b + 1) * 8], tEa[d], c3(tEa[d][:])[:, :, col])
                    for d in range(2):
                        bk = ps[6 + d]
                        for jj in range(4):
                            S.mmg(bk, bk[:, jj * 128:(jj + 1) * 128], kendT[d], kendT[d][:, jj * 128:(jj + 1) * 128], identb, identb[:], start=True, stop=True)
                        S.cp("act", kend_tok[d], kend_tok[d][:, b * 4:(b + 1) * 4, :], bk, bk[:, :].rearrange("p (j v) -> p j v", v=128))
                    for d in range(2):
                        for jj in range(4):
                            tk = slice(jj * 128, (jj + 1) * 128)
                            tq = slice(b * NB + jj * 128, b * NB + (jj + 1) * 128)
                            S.mmg(ps[5], ps[5][:, tk], kinvT[d], kinvT[d][:, tk], qdec[d], qdec[d][:, tq], start=True, stop=True)
                        mcol = C_MF if d == 0 else C_MB
                        S.tt("dve", attnT[d], attnT[d][:, b * 4:(b + 1) * 4, :], ps[5], ps[5][:, :].rearrange("p (j v) -> p j v", v=128),
                             cst, cst[:, mcol:mcol + 128].unsqueeze(1).to_broadcast([128, 4, 128]), ALU.mult)
                zi = [0, 0]
                for d in range(2):
                    S.memset("pool", Sst[d][0], Sst[d][0][:], 0.0)
                for step in range(64):
                    for d in range(2):
                        c = step if d == 0 else 63 - step
                        ub = ps[2 * (step % 2) + d]
                        usl = slice(0, 128)
                        rows = slice((c % 2) * 64, (c % 2) * 64 + 64)
                        S.mmg(ub, ub[:, usl], kend_tok[d], kend_tok[d][rows, c // 2, :], vtok, vtok[rows, c // 2, :], start=True, stop=True)
                        zc = Sst[d][zi[d] % 2]
                        zn = Sst[d][(zi[d] + 1) % 2]
                        zi[d] += 1
                        S.cp("act" if d == 0 else "pool", Sbf[d], Sbf[d][:, c, :], zc, zc[:])
                        S.stt(zn, zn[:], zc, zc[:], egl[d][:, c:c + 1], ub, ub[:, usl], ALU.mult, ALU.add, rd=[egl[d]])
                def out_mm(b):
                    ob = ps[4 + b % 2]
                    for jj in range(4):
                        tt_ = b * 4 + jj
                        osl = slice(jj * 128, (jj + 1) * 128)
                        S.mmg(ob, ob[:, osl], vtok, vtok[:, tt_, :], attnT[0], attnT[0][:, tt_, :], start=True, stop=False)
                        S.mmg(ob, ob[:, osl], vtok, vtok[:, tt_, :], attnT[1], attnT[1][:, tt_, :], start=False, stop=False)
                        for cc in range(2):
                            c = tt_ * 2 + cc
                            csl = slice(jj * 128 + cc * 64, jj * 128 + cc * 64 + 64)
                            qsl = slice(c * 64, (c + 1) * 64)
                            S.mmg(ob, ob[:, csl], Sbf[0], Sbf[0][:, c, :], qdec[0], qdec[0][:, qsl], start=False, stop=False)
                            S.mmg(ob, ob[:, csl], Sbf[1], Sbf[1][:, c, :], qdec[1], qdec[1][:, qsl], start=False, stop=(cc == 1))
                    S.act(osq[b % 2], osq[b % 2][:], ob, ob[:, :], AF.Square)

                def out_norm(b):
                    ob = ps[4 + b % 2]
                    rstd_from_sq(osq[b % 2], [osq[b % 2][:]], 128, ps[6 + b % 2], orstd, orstd[:])
                    S.stt(otmp, otmp[:], ob, ob[:, :], vec[:, V_ONORM + h:V_ONORM + h + 1], orstd, orstd[:], ALU.mult, ALU.mult, rd=[vec])
                    yb = yblk[b % 2]
                    S.tt("pool", yb, yb[:], otmp, otmp[:], sgT, sgT[:, b * NB:(b + 1) * NB], ALU.mult)
                    S.dma("sp", YT, YT[h * 128:(h + 1) * 128, s * SEQ + b * NB:s * SEQ + (b + 1) * NB], yb, yb[:], group=True)

                for b in range(NBLK):
                    out_mm(b)
                    if b >= 1:
                        out_norm(b - 1)
                out_norm(NBLK - 1)
                it += 1
        S.flush()
        if stop_after == "B":
            return nc

        def post_mixer(layer, wmix, kcm, last):
            S.begin()
            nring = 5
            ring = [S.sbuf(f"wr{i}", [128, 4096], BF16) for i in range(nring)]
            yin = S.sbuf("yin", [128, kcm, NB], BF16)
            hTl = [S.sbuf(f"hTl{i}", [128, 8, NB], F32) for i in range(2)]
            pin = S.sbuf("pin", [128, 4, 256], F32)
            pT = S.sbuf("pT", [128, 2, NB], BF16)
            mt = S.sbuf("mt", [128, 8, NB], F32)
            mtX = S.sbuf("mtX", [128, 8, NB], F32)
            sqc = S.sbuf("sqc", [128, 8, NB], BF16)
            rs = S.sbuf("rs", [128, NB], F32)
            a2 = S.sbuf("a2", [128, 8, NB], BF16)
            hbf = S.sbuf("hbf", [128, 8, NB], BF16)
            hid = S.sbuf("hid", [128, 32, NB], BF16)
            rl = [S.sbuf(f"rl{i}", [128, NB], F32) for i in range(2)]
            gsb = [S.sbuf(f"gsb{i}", [128, NB], F32) for i in range(2)]
            tmpn = [S.sbuf(f"tmpn{i}", [128, NB], F32) for i in range(2)]
            if not last:
                aTn = S.sbuf("aTn", [128, 8, NB], BF16)
            else:
                osg = [S.sbuf(f"osg{i}", [128, NB], F32) for i in range(2)]
            per = 4096 // (kcm * 128)

            order = [("C1", 0), ("C3", 0)]
            for b in range(nblk):
                if b + 1 < nblk:
                    order.append(("C1", b + 1))
                order.append(("C4", b))
                if b + 1 < nblk:
                    order.append(("C3", b + 1))
                order.append(("C5", b))
            chunks = []
            for st, b in order:
                if st == "C1":
                    for o in range(0, 8, per):
                        chunks.append((wmix, o, per, kcm * 128))
                elif st == "C3":
                    for o in range(0, 32, 4):
                        chunks.append((wb[f"w1_{layer}"], o, 4, 1024))
                elif st == "C4":
                    for o in range(8):
                        chunks.append((wb[f"w2_{layer}"], o, 1, 4096))
                else:
                    chunks.append((wb[f"wpp_{layer}"], 0, 8, 256))
                    for o in range(0, 8, 4):
                        chunks.append((wb[f"wpg_{layer}"], o, 4, 1024))
            loaded = {}
            gpos = [0]

            def wload(gi):
                if gi >= len(chunks) or gi in loaded:
                    return
                wt_, o0, n, el = chunks[gi]
                r = ring[gi % nring]
                S.dma("sp", r, r[:, 0:n * el].rearrange("p (o k) -> p o k", o=n), wt_, wt_[o0:o0 + n].rearrange("o p k -> p o k"))
                loaded[gi] = r

            def wget():
                gi = gpos[0]
                gpos[0] += 1
                for a in range(gi, gi + nring - 2):
                    wload(a)
                return loaded.pop(gi)

            chains = []

            def pop_one():
                while chains:
                    try:
                        next(chains[0][2])
                        chains[0][3] += 1
                        return
                    except StopIteration:
                        chains.pop(0)

            def ensure_steps(kind, b, n):
                while True:
                    tgt = [c for c in chains if c[0] == kind and c[1] == b]
                    if not tgt or tgt[0][3] >= n:
                        return
                    pop_one()

            def force(kind, b):
                while any(c[0] == kind and c[1] == b for c in chains):
                    pop_one()

            def drain():
                while chains:
                    pop_one()

            def ld_h(b):
                S.dma("sp", hTl[b % 2], hTl[b % 2][:], HT, hview(HT, b * NB))

            def ld_y(b):
                S.dma("sp", yin, yin[:], YT, hview(YT[0:kcm * 128], b * NB))

            def ld_p(b):
                s_, bb = divmod(b, NBLK)
                S.dma("sp", pin, pin[:], p_d, p_d[layer, s_, bb * NB:(bb + 1) * NB, :].rearrange("(j p) d -> p j d", p=128))

            def squares(src_t, src_ap):
                for c in range(8):
                    S.act(sqc, sqc[:, c, :], src_t, src_ap(c), AF.Square)

            def rstd_now():
                rstd_from_sq(sqc, [sqc[:, c, :] for c in range(8)], D, ps[7], rs, rs[:])

            def resid_apply(src_t, gcol, hT):
                for c in range(8):
                    tn = tmpn[c % 2]
                    S.tt("pool" if c % 2 == 0 else "dve", tn, tn[:], src_t, src_t[:, c, :], rs, rs[:], ALU.mult)
                    S.stt(hT, hT[:, c, :], tn, tn[:], vec[:, gcol + c:gcol + c + 1], hT, hT[:, c, :], ALU.mult, ALU.add, rd=[vec])

            def bf_apply(hT, gcol, dst):
                for c in range(8):
                    S.stt(dst, dst[:, c, :], hT, hT[:, c, :], vec[:, gcol + c:gcol + c + 1], rs, rs[:], ALU.mult, ALU.mult, rd=[vec])

            def chain_X(b):
                hT = hTl[b % 2]
                ld_h(b)
                squares(mtX, lambda c: mtX[:, c, :])
                yield
                rstd_now()
                resid_apply(mtX, V_POST_MIX + 8 * layer, hT)
                squares(hT, lambda c: hT[:, c, :])
                yield
                rstd_now()
                bf_apply(hT, V_PRE_MLP + 8 * layer, a2)

            def chain_F(b):
                hT = hTl[b % 2]
                squares(mt, lambda c: mt[:, c, :])
                yield
                rstd_now()
                resid_apply(mt, V_POST_MLP + 8 * layer, hT)
                for c in range(8):
                    S.cp("act", hbf, hbf[:, c, :], hT, hT[:, c, :])

            def chain_T(b):
                hT = hTl[b % 2]
                squares(mt, lambda c: mt[:, c, :])
                yield
                rstd_now()
                resid_apply(mt, V_PLE + 8 * layer, hT)
                if not last:
                    S.dma("sp", HT, hview(HT, b * NB), hT, hT[:], group=True)
                    squares(hT, lambda c: hT[:, c, :])
                    yield
                    rstd_now()
                    bf_apply(hT, V_PRE_MIX + 8 * (layer + 1), aTn)
                    S.dma("sp", AT, hview(AT, b * NB), aTn, aTn[:], group=True)
                else:
                    yield
                    s_, bb = divmod(b, NBLK)
                    k = 0
                    for j in range(4):
                        for half in range(2):
                            bank = ps[4 + k % 2]
                            for q in range(4):
                                kc = half * 4 + q
                                if q == 0:
                                    S.op("pe", (lambda o, i_: (lambda e: e.transpose(out=o, in_=i_, identity=identf[:])))(bank[:, 0:128], hT[:, kc, j * 128:(j + 1) * 128]), reads=[hT, identf], writes=[bank])
                                else:
                                    S.tr(bank, bank[:, q * 128:(q + 1) * 128], hT, hT[:, kc, j * 128:(j + 1) * 128], identf, identf[:])
                            og = osg[k % 2]
                            S.cp("dve", og, og[:], bank, bank[:, :])
                            S.dma("sp", out_d, out_d[s_, bb * NB + j * 128:bb * NB + (j + 1) * 128, half * 512:(half + 1) * 512], og, og[:])
                            k += 1
                            if k % 4 == 0:
                                yield

            def start_chain(kind, b, gen):
                chains.append([kind, b, gen, 0])

            def st_C1(b):
                oc = 0
                for _ in range(8 // per):
                    r = wget()
                    for o in range(per):
                        bank = ps[oc % 4]
                        for kc in range(kcm):
                            S.mm(bank, bank[:, :], r, r[:, o * kcm * 128 + kc * 128:o * kcm * 128 + (kc + 1) * 128], yin, yin[:, kc, :], start=(kc == 0), stop=(kc == kcm - 1))
                        S.cp("act", mtX, mtX[:, oc, :], bank, bank[:, :])
                        oc += 1
                if b + 1 < nblk:
                    ld_y(b + 1)
                start_chain("X", b, chain_X(b))
                if b >= 2:
                    ensure_steps("T", b - 2, 2)

            def st_C3(b):
                force("X", b)
                oc = 0
                for _ in range(8):
                    r = wget()
                    for o in range(4):
                        bank = ps[oc % 4]
                        for kc in range(8):
                            S.mm(bank, bank[:, :], r, r[:, o * 1024 + kc * 128:o * 1024 + (kc + 1) * 128], a2, a2[:, kc, :], start=(kc == 0), stop=(kc == 7))
                        rr = rl[oc % 2]
                        S.act(rr, rr[:], bank, bank[:, :], AF.Relu)
                        S.tt("pool", hid, hid[:, oc, :], rr, rr[:], rr, rr[:], ALU.mult)
                        oc += 1
                        if oc % 4 == 0:
                            pop_one()

            def st_C4(b):
                if b >= 1:
                    ensure_steps("T", b - 1, 2)
                for oc in range(8):
                    r = wget()
                    bank = ps[oc % 4]
                    for kc in range(32):
                        S.mm(bank, bank[:, :], r, r[:, kc * 128:(kc + 1) * 128], hid, hid[:, kc, :], start=(kc == 0), stop=(kc == 31))
                    S.cp("act", mt, mt[:, oc, :], bank, bank[:, :])
                    if oc % 2 == 0:
                        pop_one()
                start_chain("F", b, chain_F(b))
                force_first("F", b)

            def force_first(kind, b):
                tgt = [c for c in chains if c[0] == kind and c[1] == b][0]
                while chains and chains[0] is not tgt:
                    pop_one()
                if chains and chains[0] is tgt:
                    pop_one()

            def st_pT(b):
                for kc in range(2):
                    bank = ps[6]
                    for j in range(4):
                        if j == 0:
                            S.op("pe", (lambda o, i_: (lambda e: e.transpose(out=o, in_=i_, identity=identf[:])))(bank[:, 0:128], pin[:, 0, kc * 128:(kc + 1) * 128]), reads=[pin, identf], writes=[bank])
                        else:
                            S.tr(bank, bank[:, j * 128:(j + 1) * 128], pin, pin[:, j, kc * 128:(kc + 1) * 128], identf, identf[:])
                    S.cp("dve", pT, pT[:, kc, :], bank, bank[:, :])

            def st_C5(b):
                force("F", b)
                if b >= 1:
                    force("T", b - 1)
                st_pT(b)
                if b + 1 < nblk:
                    ld_p(b + 1)
                rp = wget()
                oc = 0
                for _ in range(2):
                    r = wget()
                    for o in range(4):
                        bg = ps[oc % 2]
                        be = ps[2 + oc % 2]
                        for kc in range(8):
                            S.mm(bg, bg[:, :], r, r[:, o * 1024 + kc * 128:o * 1024 + (kc + 1) * 128], hbf, hbf[:, kc, :], start=(kc == 0), stop=(kc == 7))
                        for kc in range(2):
                            S.mm(be, be[:, :], rp, rp[:, oc * 256 + kc * 128:oc * 256 + (kc + 1) * 128], pT, pT[:, kc, :], start=(kc == 0), stop=(kc == 1))
                        gg = gsb[oc % 2]
                        S.act(gg, gg[:], bg, bg[:, :], AF.Sigmoid)
                        S.tt("dve", mt, mt[:, oc, :], be, be[:, :], gg, gg[:], ALU.mult)
                        oc += 1
                start_chain("T", b, chain_T(b))
                force_first("T", b)

            ld_y(0)
            ld_p(0)
            for st, b in order:
                {"C1": st_C1, "C3": st_C3, "C4": st_C4, "C5": st_C5}[st](b)
            drain()
            S.flush()

        post_mixer(0, wb["hg_w_out"], 8, last=False)
        if stop_after == "C":
            return nc

        CKV = scratch("CKV", [256, T], BF16)
        KR = scratch("KR", [64, T], BF16)
        KSS = scratch("KSS", [1, T], F32)
        S.begin()
        ASCALE = 192.0 ** -0.5
        win_t = S.sbuf("win_t", [128, 6, 1024], BF16)
        wq_t = S.sbuf("wq_t", [128, 16, 768], BF16)
        S.dma("sp", win_t, win_t[:], wb["mla_w_in"], wb["mla_w_in"][:].rearrange("o p k -> p o k"))
        S.dma("sp", wq_t, wq_t[:], wb["mla_w_uq"], wb["mla_w_uq"][:].rearrange("o p k -> p o k"))
        atD = [S.sbuf(f"atD{i}", [128, 8, NB], BF16) for i in range(2)]
        posi = S.sbuf("posi", [64, NB], I32)
        ang = S.sbuf("ang", [64, NB], F32)
        angn = S.sbuf("angn", [64, NB], F32)
        angi = S.sbuf("angi", [64, NB], I32)
        cosT = S.sbuf("cosT", [64, NB], F32)
        sinT = S.sbuf("sinT", [64, NB], F32)
        cq_sb = S.sbuf("cq_sb", [128, 3, NB], F32)
        cq_sq = S.sbuf("cq_sq", [128, 3, NB], BF16)
        cqn = S.sbuf("cqn", [128, 3, NB], BF16)
        kv_sb = S.sbuf("kv_sb", [128, 2, NB], F32)
        kv_sq = S.sbuf("kv_sq", [128, 2, NB], BF16)
        ckv_b = [S.sbuf(f"ckv_b{i}", [128, 2, NB], BF16) for i in range(2)]
        rsD = S.sbuf("rsD", [128, NB], F32)
        kr_b = [S.sbuf(f"kr_b{i}", [64, NB], BF16) for i in range(2)]
        kr_sq = S.sbuf("kr_sq", [64, NB], BF16)
        kss_b = [S.sbuf(f"kss_b{i}", [65, NB], F32) for i in range(2)]
        r1 = [S.sbuf(f"r1_{i}", [64, NB], F32) for i in range(2)]
        r2 = [S.sbuf(f"r2_{i}", [64, NB], F32) for i in range(2)]
        qn_b = [S.sbuf(f"qn_b{i}", [128, NB], BF16) for i in range(3)]
        qn_sq = [S.sbuf(f"qn_sq{i}", [128, NB], BF16) for i in range(2)]
        qr_sq = [S.sbuf(f"qr_sq{i}", [64, NB], BF16) for i in range(2)]
        qr_b = [S.sbuf(f"qr_b{i}", [65, NB], BF16) for i in range(3)]
        qnl = [S.sbuf(f"qnl{i}", [65, NB], F32) for i in range(2)]
        TWO_PI_HI = 6.28125
        TWO_PI_LO = 0.0019353071795864769
        hi = 0
        for bi in range(nblk):
            s, b = divmod(bi, NBLK)
            t0 = bi * NB
            at = atD[bi % 2]
            S.dma("sp", at, at[:], AT, hview(AT, t0))
            S.dma("sp", posi, posi[:], pos_d, pos_d[s:s + 1, b * NB:(b + 1) * NB].partition_broadcast(64))
            S.cp("dve", ang, ang[:], posi, posi[:])
            S.tsc("dve", ang, ang[:], ang, ang[:], cst[0:64, C_INVF:C_INVF + 1], None, ALU.mult, rd=[cst])
            for tab, shift in ((sinT, 0.0), (cosT, float(np.pi / 2))):
                S.tsc("dve", angn, angn[:], ang, ang[:], shift, float(1.0 / (2 * np.pi)), ALU.add, ALU.mult)
                S.cp("dve", angi, angi[:], angn, angn[:])
                S.cp("dve", angn, angn[:], angi, angi[:])
                S.tsc("dve", tab, tab[:], ang, ang[:], shift, None, ALU.add)
                S.stt(tab, tab[:], angn, angn[:], -TWO_PI_HI, tab, tab[:], ALU.mult, ALU.add)
                S.stt(tab, tab[:], angn, angn[:], -TWO_PI_LO, tab, tab[:], ALU.mult, ALU.add)
                S.tsc("dve", tab, tab[:], tab, tab[:], float(np.pi), float(-np.pi), ALU.min, ALU.max)
                S.act(tab, tab[:], tab, tab[:], AF.Sin)
            S.tsc("dve", sinT, sinT[0:32, :], sinT, sinT[0:32, :], -1.0, None, ALU.mult)
            for oc in range(3):
                bank = ps[oc % 4]
                for kc in range(8):
                    S.mm(bank, bank[:, :], win_t, win_t[:, oc, kc * 128:(kc + 1) * 128], at, at[:, kc, :], start=(kc == 0), stop=(kc == 7))
                S.cp("dve", cq_sb, cq_sb[:, oc, :], bank, bank[:, :])
                S.act(cq_sq, cq_sq[:, oc, :], bank, bank[:, :], AF.Square)
            rstd_from_sq(cq_sq, [cq_sq[:, c, :] for c in range(3)], 384, ps[7], rsD, rsD[:])
            for c in range(3):
                S.stt(cqn, cqn[:, c, :], cq_sb, cq_sb[:, c, :], vec[:, V_QNORM + c:V_QNORM + c + 1], rsD, rsD[:], ALU.mult, ALU.mult, rd=[vec])
            for oc in range(2):
                bank = ps[oc % 4]
                for kc in range(8):
                    S.mm(bank, bank[:, :], win_t, win_t[:, 3 + oc, kc * 128:(kc + 1) * 128], at, at[:, kc, :], start=(kc == 0), stop=(kc == 7))
                S.cp("dve", kv_sb, kv_sb[:, oc, :], bank, bank[:, :])
                S.act(kv_sq, kv_sq[:, oc, :], bank, bank[:, :], AF.Square)
            rstd_from_sq(kv_sq, [kv_sq[:, c, :] for c in range(2)], 256, ps[7], rsD, rsD[:])
            ck = ckv_b[bi % 2]
            for c in range(2):
                S.stt(ck, ck[:, c, :], kv_sb, kv_sb[:, c, :], vec[:, V_KVNORM + c:V_KVNORM + c + 1], rsD, rsD[:], ALU.mult, ALU.mult, rd=[vec])
            S.dma("sp", CKV, hview(CKV, t0), ck, ck[:], group=True)
            for half, bank in ((0, ps[2]), (1, ps[3])):
                for kc in range(8):
                    S.mm(bank, bank[0:64, :], win_t, win_t[:, 5, kc * 128 + half * 64:kc * 128 + half * 64 + 64], at, at[:, kc, :], start=(kc == 0), stop=(kc == 7))
            krb = kr_b[bi % 2]
            S.tt("dve", r1[0], r1[0][:], ps[2], ps[2][0:64, :], cosT, cosT[:], ALU.mult)
            S.tt("dve", r2[0], r2[0][:], ps[3], ps[3][0:64, :], sinT, sinT[:], ALU.mult)
            S.tt("pool", krb, krb[:], r1[0], r1[0][:], r2[0], r2[0][:], ALU.add)
            S.act(kr_sq, kr_sq[:], ps[2], ps[2][0:64, :], AF.Square)
            S.dma("sp", KR, KR[:, t0:t0 + NB], krb, krb[:], group=True)
            S.mm(ps[7], ps[7][0:65, :], onesb, onesb[0:64, 0:65], kr_sq, kr_sq[:], start=True, stop=True)
            kb = kss_b[bi % 2]
            S.cp("dve", kb, kb[64:65, :], ps[7], ps[7][64:65, :])
            S.dma("sp", KSS, KSS[0:1, t0:t0 + NB], kb, kb[64:65, :], group=True)
            prev_tail = None
            for h in range(16):
                bn, br, bs = ps[(3 * hi) % 6], ps[(3 * hi + 1) % 6], ps[(3 * hi + 2) % 6]
                for kc in range(3):
                    S.mm(bn, bn[:, :], wq_t, wq_t[:, h, kc * 256:kc * 256 + 128], cqn, cqn[:, kc, :], start=(kc == 0), stop=(kc == 2))
                for kc in range(3):
                    S.mm(br, br[0:64, :], wq_t, wq_t[:, h, kc * 256 + 128:kc * 256 + 192], cqn, cqn[:, kc, :], start=(kc == 0), stop=(kc == 2))
                for kc in range(3):
                    S.mm(bs, bs[0:64, :], wq_t, wq_t[:, h, kc * 256 + 192:kc * 256 + 256], cqn, cqn[:, kc, :], start=(kc == 0), stop=(kc == 2))
                qn = qn_b[hi % 3]
                qr = qr_b[hi % 3]
                S.cp("act", qn, qn[:], bn, bn[:, :])
                S.act(qn_sq[hi % 2], qn_sq[hi % 2][:], bn, bn[:, :], AF.Square)
                S.act(qr_sq[hi % 2], qr_sq[hi % 2][:], br, br[0:64, :], AF.Square)
                S.tt("dve", r1[hi % 2], r1[hi % 2][:], br, br[0:64, :], cosT, cosT[:], ALU.mult)
                S.tt("dve", r2[hi % 2], r2[hi % 2][:], bs, bs[0:64, :], sinT, sinT[:], ALU.mult)
                S.tt("pool", qr, qr[0:64, :], r1[hi % 2], r1[hi % 2][:], r2[hi % 2], r2[hi % 2][:], ALU.add)
                def qtail(h=h, hi=hi, qn=qn, qr=qr):
                    bq = ps[6 + hi % 2]
                    S.mm(bq, bq[0:65, :], onesb, onesb[:, 0:65], qn_sq[hi % 2], qn_sq[hi % 2][:], start=True, stop=False)
                    S.mm(bq, bq[0:65, :], onesb, onesb[0:64, 0:65], qr_sq[hi % 2], qr_sq[hi % 2][:], start=False, stop=True)
                    ql = qnl[hi % 2]
                    S.act(ql, ql[64:65, :], bq, bq[64:65, :], AF.Ln)
                    S.act(ql, ql[64:65, :], ql, ql[64:65, :], AF.Exp, scale=0.5)
                    S.tsc("dve", qr, qr[64:65, :], ql, ql[64:65, :], -1.0, None, ALU.mult)
                    S.dma("sp", QN, QN[h, :, t0:t0 + NB], qn, qn[:], group=True)
                    S.dma("sp", QR, QR[h, :, t0:t0 + NB], qr, qr[:], group=True)
                if prev_tail is not None:
                    prev_tail()
                prev_tail = qtail
                hi += 1
            prev_tail()
        S.flush()
        if stop_after == "D":
            return nc

        S.begin()
        wkv_t = S.sbuf("wkv_t", [128, 16, 512], BF16)
        S.dma("sp", wkv_t, wkv_t[:], wb["mla_w_ukv"], wb["mla_w_ukv"][:].rearrange("o p k -> p o k"))
        ckv_s = S.sbuf("ckv_s", [128, 2, SEQ], BF16)
        krs = [S.sbuf(f"krs{i}", [65, SEQ], BF16) for i in range(2)]
        kss_s = S.sbuf("kss_s", [65, SEQ], F32)
        knT = [S.sbuf(f"knT{i}", [128, SEQ], BF16) for i in range(2)]
        kn_sq = S.sbuf("kn_sq", [128, NB], BF16)
        vh = [S.sbuf(f"vh{i}", [128, 32, 132], BF16) for i in range(2)]
        ktot = S.sbuf("ktot", [65, SEQ], F32)
        kmax = S.sbuf("kmax", [65, 2], F32)
        qnE = [S.sbuf(f"qnE{i}", [128, NB], BF16) for i in range(2)]
        qrE = [S.sbuf(f"qrE{i}", [65, NB], BF16) for i in range(2)]
        oT_b = [S.sbuf(f"oT_b{i}", [128, NB], BF16) for i in range(2)]
        for i in range(2):
            S.memset("pool", vh[i], vh[i][:], 1.0)
        pT4 = [S.sbuf(f"pT4_{i}", [128, NB], BF16) for i in range(4)]
        accD = [[S.sbuf(f"accD{i}_{j}", [128, NB], F32) for j in range(4)] for i in range(2)]
        accP = [S.sbuf(f"accP{i}", [128, NB], F32) for i in range(2)]
        onesf = S.sbuf("onesf", [128, 128], F32)
        recE = [S.sbuf(f"recE{i}", [128, NB], F32) for i in range(2)]
        S.memset("dve", onesf, onesf[:], 1.0)

        def prep_head(h, slot):
            kn, v, kr = knT[slot], vh[slot], krs[slot]
            for b in range(NBLK):
                bank = ps[6 + b % 2]
                for kc in range(2):
                    S.mm(bank, bank[:, :], wkv_t, wkv_t[:, h, kc * 256:kc * 256 + 128], ckv_s, ckv_s[:, kc, b * NB:(b + 1) * NB], start=(kc == 0), stop=(kc == 1))
                S.cp("dve", kn, kn[:, b * NB:(b + 1) * NB], bank, bank[:, :])
                S.tt("pool", kn_sq, kn_sq[:], kn, kn[:, b * NB:(b + 1) * NB], kn, kn[:, b * NB:(b + 1) * NB], ALU.mult)
                bq = ps[6 + (b + 1) % 2]
                S.mm(bq, bq[0:65, :], onesb, onesb[:, 0:65], kn_sq, kn_sq[:], start=True, stop=True)
                S.tt("dve", ktot, ktot[64:65, b * NB:(b + 1) * NB], bq, bq[64:65, :], kss_s, kss_s[64:65, b * NB:(b + 1) * NB], ALU.add)
            S.op("dve", (lambda o, i_: (lambda e: e.reduce_max(out=o, in_=i_, axis=mybir.AxisListType.X)))(kmax[64:65, 0:1], ktot[64:65, :]), reads=[ktot], writes=[kmax])
            S.act(kmax, kmax[64:65, 1:2], kmax, kmax[64:65, 0:1], AF.Ln)
            S.act(kmax, kmax[64:65, 1:2], kmax, kmax[64:65, 1:2], AF.Exp, scale=0.5)
            S.memset("pool", kr, kr[64:65, :], 1.0)
            S.tsc("dve", kr, kr[64:65, :], kr, kr[64:65, :], kmax[64:65, 1:2], None, ALU.mult, rd=[kmax])
            for tt_ in range(32):
                bank = ps[6 + (tt_ // 4) % 2]
                sl = slice((tt_ % 4) * 128, (tt_ % 4 + 1) * 128)
                for kc in range(2):
                    S.mmg(bank, bank[:, sl], ckv_s, ckv_s[:, kc, tt_ * 128:(tt_ + 1) * 128], wkv_t, wkv_t[:, h, kc * 256 + 128:kc * 256 + 256], start=(kc == 0), stop=(kc == 1))
                if tt_ % 4 == 3:
                    S.cp("dve", v, v[:, tt_ - 3:tt_ + 1, 0:128], bank, bank[:, :].rearrange("p (j v) -> p j v", v=128))

        def load_q(s, h, qb, slot):
            t0 = s * SEQ + qb * NB
            S.dma("sp", qnE[slot], qnE[slot][:], QN, QN[h, :, t0:t0 + NB])
            S.dma("sp", qrE[slot], qrE[slot][:], QR, QR[h, :, t0:t0 + NB])

        LOOK = 2
        hcount = 0
        for s in range(nseq):
            S.dma("sp", ckv_s, ckv_s[:], CKV, hview(CKV, s * SEQ, SEQ))
            S.dma("sp", kss_s, kss_s[64:65, :], KSS, KSS[0:1, s * SEQ:(s + 1) * SEQ])
            for i in range(2):
                S.dma("sp", krs[i], krs[i][0:64, :], KR, KR[:, s * SEQ:(s + 1) * SEQ])
            items = [(h, qb, kt) for h in range(16) for qb in range(NBLK) for kt in range(32)]
            nit = len(items)

            def hslot(h):
                return (hcount + h) % 2

            def qslot(h, qb):
                return (h * NBLK + qb) % 2

            def qk(i_):
                h, qb, kt = items[i_]
                sb_ = ps[i_ % 4]
                ks = slice(kt * 128, (kt + 1) * 128)
                kn, kr = knT[hslot(h)], krs[hslot(h)]
                qn, qr = qnE[qslot(h, qb)], qrE[qslot(h, qb)]
                S.mm(sb_, sb_[:, :], kn, kn[:, ks], qn, qn[:], start=True, stop=False)
                S.mm(sb_, sb_[:, :], kr, kr[0:65, ks], qr, qr[0:65, :], start=False, stop=True)

            deferred = []
            prep_head(0, hslot(0))
            load_q(s, 0, 0, qslot(0, 0))
            for j in range(LOOK):
                qk(j)
            for i_, it in enumerate(items):
                h, qb, kt = it
                qbi = h * NBLK + qb
                if kt == 0:
                    nh, nqb = (h, qb + 1) if qb + 1 < NBLK else (h + 1, 0)
                    if nh < 16:
                        load_q(s, nh, nqb, qslot(nh, nqb))
                if kt == 4 and qb == 3 and h + 1 < 16:
                    prep_head(h + 1, hslot(h + 1))
                if i_ + LOOK < nit:
                    qk(i_ + LOOK)
                sb_ = ps[i_ % 4]
                pt = pT4[i_ % 4]
                S.act(pt, pt[:], sb_, sb_[:, :], AF.Exp, scale=ASCALE)
                v = vh[hslot(h)]
                ob = ps[4 + qbi % 2]
                S.mm(ob, ob[:, :], v, v[:, kt, 0:128], pt, pt[:], start=(kt == 0), stop=(kt == 31))
                if False:
                    pass
                else:
                    ad_ = accD[qbi % 2][kt % 4]
                    if kt < 4:
                        S.cp("dve", ad_, ad_[:], pt, pt[:])
                    else:
                        S.tt("dve", ad_, ad_[:], ad_, ad_[:], pt, pt[:], ALU.add)
                if kt == 3 and deferred:
                    for f in deferred:
                        f()
                    deferred = []
                if kt == 31:
                    def fin(h=h, qb=qb, s=s, qbi=qbi):
                        t0 = s * SEQ + qb * NB
                        ad_, ap_, ob = accD[qbi % 2], accP[qbi % 2], ps[4 + qbi % 2]
                        rec = recE[qbi % 2]
                        oT = oT_b[qbi % 2]
                        bd = ps[6 + qbi % 2]
                        S.tt("pool", ad_[0], ad_[0][:], ad_[0], ad_[0][:], ad_[1], ad_[1][:], ALU.add)
                        S.tt("pool", ad_[2], ad_[2][:], ad_[2], ad_[2][:], ad_[3], ad_[3][:], ALU.add)
                        S.tt("pool", ad_[0], ad_[0][:], ad_[0], ad_[0][:], ad_[2], ad_[2][:], ALU.add)
                        S.mm(bd, bd[:, :], onesf, onesf[:], ad_[0], ad_[0][:], start=True, stop=True)
                        S.act(rec, rec[:], bd, bd[:, :], AF.Ln)
                        S.act(rec, rec[:], rec, rec[:], AF.Exp, scale=-1.0)
                        S.tt("dve", oT, oT[:], ob, ob[:, :], rec, rec[:], ALU.mult)
                        S.dma("sp", YT, YT[h * 128:(h + 1) * 128, t0:t0 + NB], oT, oT[:], group=True)
                    deferred.append(fin)
            for f in deferred:
                f()
            hcount += 16
        S.flush()
        if stop_after == "E":
            return nc

        post_mixer(1, wb["mla_w_o"], 16, last=True)
    return nc


_CACHE = {}


def kernel(x, p, positions, pre_mix_norm, post_mix_norm, pre_mlp_norm, post_mlp_norm,
           w_mlp_in, w_mlp_out, w_ple_proj, w_ple_gate, ple_norm,
           hg_lb_logits, hg_w_in, hg_o_norm, hg_w_out,
           mla_w_in, mla_q_norm, mla_w_uq, mla_kv_norm, mla_w_ukv, mla_w_o):
    inp = dict(x=x, p=p, positions=positions, pre_mix_norm=pre_mix_norm, post_mix_norm=post_mix_norm,
               pre_mlp_norm=pre_mlp_norm, post_mlp_norm=post_mlp_norm, w_mlp_in=w_mlp_in,
               w_mlp_out=w_mlp_out, w_ple_proj=w_ple_proj, w_ple_gate=w_ple_gate, ple_norm=ple_norm,
               hg_lb_logits=hg_lb_logits, hg_w_in=hg_w_in, hg_o_norm=hg_o_norm, hg_w_out=hg_w_out,
               mla_w_in=mla_w_in, mla_q_norm=mla_q_norm, mla_w_uq=mla_w_uq, mla_kv_norm=mla_kv_norm,
               mla_w_ukv=mla_w_ukv, mla_w_o=mla_w_o)
    inp = {k: np.asarray(v) for k, v in inp.items()}
    w = prep_weights(inp)
    nseq = 2
    if "nc" not in _CACHE:
        _CACHE["nc"] = build(nseq=nseq)
    nc = _CACHE["nc"]
    in_maps = []
    for c in range(NCORES):
        m = dict(w)
        m["x"] = np.ascontiguousarray(inp["x"][c * nseq:(c + 1) * nseq], dtype=np.float32)
        m["p"] = np.ascontiguousarray(inp["p"][:, c * nseq:(c + 1) * nseq], dtype=np.float32)
        m["pos"] = np.ascontiguousarray(inp["positions"][c * nseq:(c + 1) * nseq], dtype=np.int32)
        in_maps.append(m)
    res = run_bass_kernel_spmd(nc, in_maps, core_ids=list(range(NCORES)))
    return np.concatenate([r["out"] for r in res.results], axis=0).astype(np.float32)
```

```python
from contextlib import ExitStack
import numpy as np
import concourse.bass as bass
import concourse.mybir as mybir
from concourse.bass_utils import run_bass_kernel_spmd

F32 = mybir.dt.float32
BF16 = mybir.dt.bfloat16
I32 = mybir.dt.int32
AF = mybir.ActivationFunctionType
ALU = mybir.AluOpType

ENGS = ("pe", "act", "dve", "pool", "sp")

D = 1024
SEQ = 4096
NB = 512
NBLK = SEQ // NB
EPS = 1e-6
NCORES = 8


class Tile:
    __slots__ = ("name", "h", "writers", "readers", "dsem", "dcount", "space", "dwaited")

    def __init__(self, name, h, space):
        self.name = name
        self.h = h
        self.space = space
        self.writers = []
        self.readers = []
        self.dsem = None
        self.dcount = 0

    def __getitem__(self, idx):
        return self.h[idx]


class Instr:
    __slots__ = ("eng", "fn", "idx", "waits", "marked", "is_dma", "dtile", "dval", "vc", "cnt")

    def __init__(self, eng, fn, idx):
        self.eng = eng
        self.fn = fn
        self.idx = idx
        self.waits = []
        self.marked = False
        self.is_dma = False
        self.dtile = None
        self.dval = 0
        self.vc = None
        self.cnt = 0


class Sched:
    def __init__(self, nc, gctx):
        self.nc = nc
        self.gctx = gctx
        self.pctx = None
        self.esem = {e: gctx.enter_context(nc.semaphore("s_" + e)) for e in ENGS}
        self.base = {e: 0 for e in ENGS}
        self.tiles = []
        self.live = []
        self._reset()
        self.n_instr = 0

    def _reset(self):
        self.prog = {e: [] for e in ENGS}
        self.vc = {e: {p: -1 for p in ENGS} for e in ENGS}
        for e in ENGS:
            self.vc[e]["dma"] = {}
        for t in self.live:
            t.writers = []
            t.readers = []
        self.live = []

    def begin(self):
        self.pctx = ExitStack()

    def sbuf(self, name, shape, dt, glob=False):
        c = self.gctx if glob else self.pctx
        self.uid = getattr(self, "uid", 0) + 1
        name = f"{name}_{self.uid}"
        h = c.enter_context(self.nc.sbuf_tensor(name, list(shape), dt))
        return Tile(name, h, "sb")

    def psum(self, name, shape, dt=F32, glob=False):
        c = self.gctx if glob else self.pctx
        h = c.enter_context(self.nc.psum_tensor(name, list(shape), dt))
        return Tile(name, h, "ps")

    def dram(self, name, shape, dt, kind="Internal"):
        h = self.nc.dram_tensor(name, list(shape), dt, kind=kind)
        return Tile(name, h.ap(), "dr")

    def _dep(self, ins, d):
        if d is ins:
            return
        e = ins.eng
        if d.is_dma:
            known = self.vc[e]["dma"]
            tot = d.dtile.dcount
            if known.get(id(d.dtile), 0) >= tot:
                return
            known[id(d.dtile)] = tot
            ins.waits.append((d.dtile, tot))
            return
        p = d.eng
        if p == "pe" and e == "pe":
            return
        if self.vc[e][p] >= d.idx:
            return
        ins.waits.append(d)
        d.marked = True
        self.vc[e][p] = d.idx
        for k, v in d.vc.items():
            if k == "dma":
                known = self.vc[e]["dma"]
                for kk, vv in v.items():
                    if known.get(kk, 0) < vv:
                        known[kk] = vv
            elif self.vc[e][k] < v:
                self.vc[e][k] = v

    def op(self, eng, fn, reads=(), writes=(), acc=(), dma=False):
        ins = Instr(eng, fn, len(self.prog[eng]))
        relax = (not dma) and eng in ("act", "dve")

        def skip(d):
            return relax and d.eng == eng and not d.is_dma
        for t in reads:
            for w in t.writers:
                self._dep(ins, w)
            if t.space == "ps":
                for r in t.readers:
                    if r.eng != eng:
                        self._dep(ins, r)
        for t in writes:
            for w in t.writers:
                if not skip(w):
                    self._dep(ins, w)
            for r in t.readers:
                if not skip(r):
                    self._dep(ins, r)
        for t in acc:
            for r in t.readers:
                if not skip(r):
                    self._dep(ins, r)
        snap = {}
        for k, v in self.vc[eng].items():
            snap[k] = dict(v) if k == "dma" else v
        ins.vc = snap
        for t in reads:
            if not t.readers and not t.writers:
                self.live.append(t)
            t.readers.append(ins)
        for t in writes:
            if not t.readers and not t.writers:
                self.live.append(t)
            t.writers = [ins]
            t.readers = []
        for t in acc:
            if not t.readers and not t.writers:
                self.live.append(t)
            if t.readers:
                t.writers = [ins]
                t.readers = []
            else:
                t.writers.append(ins)
        self.prog[eng].append(ins)
        return ins

    def dma(self, eng, out_t, out_ap, in_t, in_ap, group=False, xw=(), sem_tile=None):
        st = out_t if out_t.space == "sb" else (in_t if in_t.space == "sb" else out_t)
        if sem_tile is not None:
            st = sem_tile

        def fn(e, out_ap=out_ap, in_ap=in_ap):
            return e.dma_start(out=out_ap, in_=in_ap)

        if group:
            ins = self.op(eng, fn, reads=[in_t], acc=[out_t], writes=list(xw), dma=True)
        else:
            ins = self.op(eng, fn, reads=[in_t], writes=[out_t] + list(xw), dma=True)
        ins.is_dma = True
        ins.dtile = st
        if st.dsem is None:
            st.dsem = self.gctx.enter_context(self.nc.semaphore("d_" + st.name))
            self.tiles.append(st)
        st.dcount += 16
        ins.dval = st.dcount
        return ins

    def mm(self, ps_t, out_ap, l_t, l_ap, r_t, r_ap, start, stop):
        def fn(e):
            return e.matmul(out_ap, lhsT=l_ap, rhs=r_ap, start=start, stop=stop)
        if start:
            return self.op("pe", fn, reads=[l_t, r_t], writes=[ps_t])
        return self.op("pe", fn, reads=[l_t, r_t], acc=[ps_t])

    def mmg(self, ps_t, out_ap, l_t, l_ap, r_t, r_ap, start, stop):
        def fn(e):
            return e.matmul(out_ap, lhsT=l_ap, rhs=r_ap, start=start, stop=stop)
        return self.op("pe", fn, reads=[l_t, r_t], acc=[ps_t])

    def tr(self, ps_t, out_ap, in_t, in_ap, id_t, id_ap):
        def fn(e):
            return e.transpose(out=out_ap, in_=in_ap, identity=id_ap)
        return self.op("pe", fn, reads=[in_t, id_t], acc=[ps_t])

    def act(self, out_t, out_ap, in_t, in_ap, func, bias=None, scale=None, rd=()):
        kw = {}
        if bias is not None:
            kw["bias"] = bias
        if scale is not None:
            kw["scale"] = scale

        def fn(e):
            return e.activation(out=out_ap, in_=in_ap, func=func, **kw)
        return self.op("act", fn, reads=[in_t] + list(rd), writes=[out_t])

    def tsc(self, eng, out_t, out_ap, in_t, in_ap, s1, s2, op0, op1=None, rd=()):
        def fn(e):
            if op1 is None:
                return e.tensor_scalar(out=out_ap, in0=in_ap, scalar1=s1, scalar2=None, op0=op0)
            return e.tensor_scalar(out=out_ap, in0=in_ap, scalar1=s1, scalar2=s2, op0=op0, op1=op1)
        return self.op(eng, fn, reads=[in_t] + list(rd), writes=[out_t])

    def stt(self, out_t, out_ap, a_t, a_ap, scalar, b_t, b_ap, op0, op1, rd=()):
        def fn(e):
            return e.scalar_tensor_tensor(out=out_ap, in0=a_ap, scalar=scalar, in1=b_ap, op0=op0, op1=op1)
        return self.op("dve", fn, reads=[a_t, b_t] + list(rd), writes=[out_t])

    def tt(self, eng, out_t, out_ap, a_t, a_ap, b_t, b_ap, op):
        def fn(e):
            return e.tensor_tensor(out=out_ap, in0=a_ap, in1=b_ap, op=op)
        return self.op(eng, fn, reads=[a_t, b_t], writes=[out_t])

    def cp(self, eng, out_t, out_ap, in_t, in_ap):
        if eng == "act":
            return self.act(out_t, out_ap, in_t, in_ap, AF.Copy)

        def fn(e):
            return e.tensor_copy(out=out_ap, in_=in_ap)
        return self.op(eng, fn, reads=[in_t], writes=[out_t])

    def memset(self, eng, t, ap, val):
        def fn(e):
            return e.memset(ap, val)
        return self.op(eng, fn, writes=[t])

    def flush(self, final=False):
        nc = self.nc
        esem = self.esem
        for e in ENGS:
            lst = self.prog[e]
            for ins in reversed(lst):
                if not ins.is_dma:
                    ins.marked = True
                    break
            c = self.base[e]
            for ins in lst:
                if ins.is_dma:
                    continue
                if ins.marked:
                    c += 1
                    ins.cnt = c
            self.base[e] = c
            self.n_instr += len(lst)

        def ev(d):
            if isinstance(d, tuple):
                return d[0].dsem, d[1]
            return esem[d.eng], d.cnt

        dma_tails = [(t.dsem, t.dcount) for t in self.tiles if t.dcount]
        eng_tails = [(esem[e], self.base[e]) for e in ENGS if self.base[e] > 0]

        with nc.Block() as block:
            def run(eng_obj, name):
                for ins in self.prog[name]:
                    for d in ins.waits:
                        s, v = ev(d)
                        eng_obj.wait_ge(s, v)
                    bi = ins.fn(eng_obj)
                    if ins.is_dma:
                        bi.then_inc(ins.dtile.dsem, 16)
                    elif ins.marked:
                        bi.then_inc(esem[name], 1)
                for s, v in eng_tails:
                    if s is not esem[name]:
                        eng_obj.wait_ge(s, v)
                for s, v in dma_tails:
                    eng_obj.wait_ge(s, v)

            @block.tensor
            def _(e):
                run(e, "pe")

            @block.scalar
            def _(e):
                run(e, "act")

            @block.vector
            def _(e):
                run(e, "dve")

            @block.gpsimd
            def _(e):
                run(e, "pool")

            @block.sync
            def _(e):
                run(e, "sp")

        self._reset()
        if self.pctx is not None:
            self.pctx.close()
            self.pctx = None


def lay_oc(W, cw=128):
    K, N = W.shape
    KC, OC = K // 128, N // cw
    return np.ascontiguousarray(W.reshape(KC, 128, OC, cw).transpose(2, 1, 0, 3).reshape(OC, 128, KC * cw))


def vec_cols(v):
    C = v.shape[0] // 128
    return np.ascontiguousarray(v.reshape(C, 128).T)


V_PRE_MIX, V_POST_MIX, V_PRE_MLP, V_POST_MLP, V_PLE = 0, 16, 32, 48, 64
V_ONORM = 80
V_QNORM = 88
V_KVNORM = 91
V_LBL = 93
NVEC = 117

C_ID = 0
C_MF = 128
C_MB = 256
C_SCAN = 384
C_INVF = 896
NCONST = 897


def make_consts():
    c = np.zeros((128, NCONST), np.float32)
    c[:, C_ID:C_ID + 128] = np.eye(128, dtype=np.float32)
    j = np.arange(128)[:, None]
    i = np.arange(128)[None, :]
    same = (j // 64) == (i // 64)
    c[:, C_MF:C_MF + 128] = (same & (i >= j)).astype(np.float32)
    c[:, C_MB:C_MB + 128] = (same & (i <= j)).astype(np.float32)
    sm = np.ones(512, np.float32)
    sm[::64] = 0.0
    c[:, C_SCAN:C_SCAN + 512] = sm[None, :]
    inv = (1.0 / (10000.0 ** (np.arange(0, 64, 2, dtype=np.float32) / 64.0))).astype(np.float32)
    c[0:32, C_INVF] = inv
    c[32:64, C_INVF] = inv
    return c


def prep_weights(inp):
    w = {}
    w["hg_w_in"] = lay_oc(inp["hg_w_in"][0])
    w["hg_w_out"] = lay_oc(inp["hg_w_out"][0])
    for l in range(2):
        w[f"w1_{l}"] = lay_oc(inp["w_mlp_in"][l])
        w[f"w2_{l}"] = lay_oc(inp["w_mlp_out"][l])
        w[f"wpp_{l}"] = lay_oc(inp["w_ple_proj"][l])
        w[f"wpg_{l}"] = lay_oc(inp["w_ple_gate"][l])
    win = inp["mla_w_in"][0]
    kr = win[:, 640:704]
    kr_sw = np.concatenate([kr[:, 32:64], kr[:, 0:32]], axis=1)
    w["mla_w_in"] = lay_oc(np.concatenate([win, kr_sw], axis=1))
    wuq = inp["mla_w_uq"][0].reshape(384, 16, 192)
    nope = wuq[:, :, 0:128]
    rope = wuq[:, :, 128:192]
    rope_sw = np.concatenate([rope[:, :, 32:64], rope[:, :, 0:32]], axis=2)
    wq = np.concatenate([nope, rope, rope_sw], axis=2).reshape(384, 16 * 256)
    w["mla_w_uq"] = lay_oc(wq, cw=256)
    w["mla_w_ukv"] = lay_oc(inp["mla_w_ukv"][0], cw=256)
    w["mla_w_o"] = lay_oc(inp["mla_w_o"][0])
    vecs = np.zeros((128, NVEC), np.float32)
    for l in range(2):
        vecs[:, V_PRE_MIX + 8 * l:V_PRE_MIX + 8 * l + 8] = vec_cols(inp["pre_mix_norm"][l])
        vecs[:, V_POST_MIX + 8 * l:V_POST_MIX + 8 * l + 8] = vec_cols(inp["post_mix_norm"][l])
        vecs[:, V_PRE_MLP + 8 * l:V_PRE_MLP + 8 * l + 8] = vec_cols(inp["pre_mlp_norm"][l])
        vecs[:, V_POST_MLP + 8 * l:V_POST_MLP + 8 * l + 8] = vec_cols(inp["post_mlp_norm"][l])
        vecs[:, V_PLE + 8 * l:V_PLE + 8 * l + 8] = vec_cols(inp["ple_norm"][l])
    vecs[:, V_ONORM:V_ONORM + 8] = inp["hg_o_norm"][0].T
    vecs[:, V_QNORM:V_QNORM + 3] = vec_cols(inp["mla_q_norm"][0])
    vecs[:, V_KVNORM:V_KVNORM + 2] = vec_cols(inp["mla_kv_norm"][0])
    for j in range(3):
        vecs[:, V_LBL + 8 * j:V_LBL + 8 * j + 8] = vec_cols(inp["hg_lb_logits"][j])
    w["vecs"] = vecs
    w["consts"] = make_consts()
    return w


WSHAPES = {
    "hg_w_in": [40, 128, 1024], "hg_w_out": [8, 128, 1024],
    "w1_0": [32, 128, 1024], "w2_0": [8, 128, 4096], "wpp_0": [8, 128, 256], "wpg_0": [8, 128, 1024],
    "w1_1": [32, 128, 1024], "w2_1": [8, 128, 4096], "wpp_1": [8, 128, 256], "wpg_1": [8, 128, 1024],
    "mla_w_in": [6, 128, 1024], "mla_w_uq": [16, 128, 768], "mla_w_ukv": [16, 128, 512],
    "mla_w_o": [8, 128, 2048],
}


def build(nseq=2, stop_after="F", dump=()):
    nc = bass.Bass("TRN2", target_bir_lowering=False)
    T = nseq * SEQ
    nblk = nseq * NBLK
    with ExitStack() as g:
        S = Sched(nc, g)
        x_d = S.dram("x", [nseq, SEQ, D], F32, kind="ExternalInput")
        p_d = S.dram("p", [2, nseq, SEQ, 256], F32, kind="ExternalInput")
        pos_d = S.dram("pos", [nseq, SEQ], I32, kind="ExternalInput")
        vecs_d = S.dram("vecs", [128, NVEC], F32, kind="ExternalInput")
        consts_d = S.dram("consts", [128, NCONST], F32, kind="ExternalInput")
        wf = {k: S.dram(k, shp, F32, kind="ExternalInput") for k, shp in WSHAPES.items()}
        out_d = S.dram("out", [nseq, SEQ, D], F32, kind="ExternalOutput")

        def scratch(name, shape, dt):
            return S.dram(name, shape, dt, kind="ExternalOutput" if name in dump else "Internal")

        wb = {k: S.dram(k + "_b", shp, BF16) for k, shp in WSHAPES.items()}
        HT = scratch("HT", [D, T], F32)
        AT = scratch("AT", [D, T], BF16)
        YT = scratch("YT", [2048, T], BF16)
        QN = scratch("QN", [16, 128, T], BF16)
        QR = scratch("QR", [16, 65, T], BF16)

        cst = S.sbuf("cst", [128, NCONST], F32, glob=True)
        vec = S.sbuf("vec", [128, NVEC], F32, glob=True)
        identb = S.sbuf("identb", [128, 128], BF16, glob=True)
        identf = S.sbuf("identf", [128, 128], F32, glob=True)
        onesb = S.sbuf("onesb", [128, 128], BF16, glob=True)
        epsb = S.sbuf("epsb", [128, 1], F32, glob=True)
        lbt = S.sbuf("lbt", [128, 16], F32, glob=True)
        ps = [S.psum(f"ps{i}", [128, 512], F32, glob=True) for i in range(8)]

        def hview(dr, t0, n=NB):
            return dr[:, t0:t0 + n].rearrange("(k p) t -> p k t", p=128)

        S.begin()
        S.dma("sp", cst, cst[:], consts_d, consts_d[:])
        S.dma("sp", vec, vec[:], vecs_d, vecs_d[:])
        toks = [Tile(f"tok{i}", None, "dr") for i in range(2)]
        ndw = 0
        import os as _os
        for k in WSHAPES:
            if "w" in _os.environ.get("DEV_SKIP", ""):
                break
            n0, _, el = WSHAPES[k]
            step = max(1, min(n0, 4096 // el))
            for o in range(0, n0, step):
                o2 = min(n0, o + step)
                S.dma("pool", wb[k], wb[k][o:o2], wf[k], wf[k][o:o2], group=True, xw=[toks[ndw % 2]], sem_tile=toks[ndw % 2])
                ndw += 1
        S.cp("dve", identb, identb[:], cst, cst[:, C_ID:C_ID + 128])
        S.cp("dve", identf, identf[:], cst, cst[:, C_ID:C_ID + 128])
        S.memset("dve", onesb, onesb[:], 1.0)
        S.memset("dve", epsb, epsb[:], EPS)
        ex = S.sbuf("ex", [128, 24], F32)
        sm = S.sbuf("sm", [128, 8], F32)
        S.act(ex, ex[:], vec, vec[:, V_LBL:V_LBL + 24], AF.Exp)
        S.tt("dve", sm, sm[:], ex, ex[:, 0:8], ex, ex[:, 8:16], ALU.add)
        S.tt("dve", sm, sm[:], sm, sm[:], ex, ex[:, 16:24], ALU.add)
        S.op("dve", lambda e: e.reciprocal(out=sm[:], in_=sm[:]), reads=[sm], writes=[sm])
        S.tt("dve", lbt, lbt[:, 0:8], ex, ex[:, 0:8], sm, sm[:], ALU.mult)
        S.tsc("dve", lbt, lbt[:, 8:16], lbt, lbt[:, 0:8], -1.0, 1.0, ALU.mult, ALU.add)
        if stop_after == "W":
            dbg = S.sbuf("dbg", [128, 16], F32)
            S.cp("dve", dbg, dbg[:], lbt, lbt[:])
            S.dma("sp", out_d, out_d[0, 0:128, 0:16], dbg, dbg[:])
        import os as _os
        if not _os.environ.get("DEV_NOFLUSHW"):
            S.flush()
        if stop_after == "W":
            return nc

        def rstd_from_sq(sq_t, sq_aps, nfeat, bank, rstd_t, rstd_ap, kparts=None):
            n = len(sq_aps)
            for c, ap in enumerate(sq_aps):
                kp = 128 if kparts is None else kparts[c]
                S.mm(bank, bank[:, :], onesb, onesb[0:kp, :], sq_t, ap, start=(c == 0), stop=(c == n - 1))
            S.act(rstd_t, rstd_ap, bank, bank[:, :], AF.Ln, bias=epsb[:, 0:1], scale=1.0 / nfeat, rd=[epsb])
            S.act(rstd_t, rstd_ap, rstd_t, rstd_ap, AF.Exp, scale=-0.5)

        if not _os.environ.get("DEV_NOFLUSHW"):
            S.begin()
        xin = [S.sbuf(f"xin{i}", [128, 4, D], F32) for i in range(2)]
        hTs = [S.sbuf(f"hT{i}", [128, 8, NB], F32) for i in range(2)]
        sqs = [S.sbuf(f"sq{i}", [128, 8, NB], BF16) for i in range(2)]
        aTs = [S.sbuf(f"aT{i}", [128, 8, NB], BF16) for i in range(2)]
        rstds = [S.sbuf(f"rstd{i}", [128, NB], F32) for i in range(2)]

        def load_x(bi):
            s, b = divmod(bi, NBLK)
            t = xin[bi % 2]
            if "x" in _os.environ.get("DEV_SKIP", ""):
                return
            S.dma("sp", t, t[:], x_d, x_d[s, b * NB:(b + 1) * NB, :].rearrange("(j p) d -> p j d", p=128))

        import os as _os
        nblkA = int(_os.environ.get("DEV_NBLK", nblk))
        skipA = _os.environ.get("DEV_SKIP", "")
        load_x(0)
        for bi in range(nblkA):
            if bi + 1 < nblkA:
                load_x(bi + 1)
            xt, hT, sq, aT, rstd = xin[bi % 2], hTs[bi % 2], sqs[bi % 2], aTs[bi % 2], rstds[bi % 2]
            for kc in range(8):
                bank = ps[kc % 4]
                for j in range(4):
                    if "t" in skipA:
                        continue
                    if j == 0:
                        S.op("pe", (lambda o, i_: (lambda e: e.transpose(out=o, in_=i_, identity=identf[:])))(bank[:, 0:128], xt[:, 0, kc * 128:(kc + 1) * 128]), reads=[xt, identf], writes=[bank])
                    else:
                        S.tr(bank, bank[:, j * 128:(j + 1) * 128], xt, xt[:, j, kc * 128:(kc + 1) * 128], identf, identf[:])
                if "c" not in skipA:
                    S.cp("dve", hT, hT[:, kc, :], bank, bank[:, :])
                if "s" not in skipA:
                    S.act(sq, sq[:, kc, :], bank, bank[:, :], AF.Square)
            if "h" not in skipA:
                S.dma("sp", HT, hview(HT, bi * NB), hT, hT[:], group=True)
            if "n" in skipA:
                continue
            rstd_from_sq(sq, [sq[:, c, :] for c in range(8)], D, ps[4 + bi % 2], rstd, rstd[:])
            for kc in range(8):
                S.stt(aT, aT[:, kc, :], hT, hT[:, kc, :], vec[:, V_PRE_MIX + kc:V_PRE_MIX + kc + 1], rstd, rstd[:], ALU.mult, ALU.mult, rd=[vec])
            if "a" not in skipA:
                S.dma("sp", AT, hview(AT, bi * NB), aT, aT[:], group=True)
        S.flush()
        if stop_after == "A":
            return nc

        S.begin()
        HSCALE = 128.0 ** -0.5
        wts = [S.sbuf(f"wB{i}", [128, 5, 1024], BF16) for i in range(2)]
        ats = [S.sbuf(f"atB{i}", [128, 8, NB], BF16) for i in range(2)]
        qdec = [S.sbuf(f"qdec{d}", [128, SEQ], BF16) for d in range(2)]
        kend_tok = [S.sbuf(f"kend{d}", [128, 32, 128], BF16) for d in range(2)]
        vtok = S.sbuf("vtok", [128, 32, 128], BF16)
        attnT = [S.sbuf(f"attnT{d}", [128, 32, 128], BF16) for d in range(2)]
        sgT = S.sbuf("sgT", [128, SEQ], BF16)
        egl = [S.sbuf(f"egl{d}", [128, 64], F32) for d in range(2)]
        Sbf = [S.sbuf(f"Sbf{d}", [128, 64, 128], BF16) for d in range(2)]
        Sst = [[S.sbuf(f"Sst{d}_{i}", [128, 128], F32) for i in range(2)] for d in range(2)]
        tA = [S.sbuf(f"tA{d}", [128, NB], F32) for d in range(2)]
        tG = [S.sbuf(f"tG{d}", [128, NB], F32) for d in range(2)]
        tP = [S.sbuf(f"tP{d}", [128, NB], F32) for d in range(2)]
        tX = [S.sbuf(f"tX{d}", [128, NB], F32) for d in range(2)]
        tEa = [S.sbuf(f"tEa{d}", [128, NB], F32) for d in range(2)]
        tEb = [S.sbuf(f"tEb{d}", [128, NB], F32) for d in range(2)]
        tEe = [S.sbuf(f"tEe{d}", [128, NB], F32) for d in range(2)]
        kinvT = [S.sbuf(f"kinvT{d}", [128, NB], BF16) for d in range(2)]
        kendT = [S.sbuf(f"kendT{d}", [128, NB], BF16) for d in range(2)]
        qsb = S.sbuf("qsb", [128, NB], F32)
        osq = [S.sbuf(f"osq{i}", [128, NB], BF16) for i in range(2)]
        orstd = S.sbuf("orstd", [128, NB], F32)
        otmp = S.sbuf("otmp", [128, NB], F32)
        yblk = [S.sbuf(f"yblk{i}", [128, NB], BF16) for i in range(2)]
        scanm = cst[:, C_SCAN:C_SCAN + NB]

        def c3(ap):
            return ap.rearrange("p (c t) -> p c t", t=64)

        it = 0
        abi = 0
        for s in range(nseq):
            for h in range(8):
                wt = wts[it % 2]
                for j, oc in enumerate([h, 8 + h, 16 + h, 24 + h, 32 + h]):
                    S.dma("sp", wt, wt[:, j, :], wb["hg_w_in"], wb["hg_w_in"][oc], group=True)
                def load_at(b):
                    at = ats[(abi0 + b) % 2]
                    S.dma("sp", at, at[:], AT, hview(AT, s * SEQ + b * NB))

                def proj(b):
                    at = ats[(abi0 + b) % 2]
                    for j, bank in ((1, ps[0]), (2, ps[1]), (0, ps[2]), (4, ps[3])):
                        for kc in range(8):
                            S.mm(bank, bank[:, :], wt, wt[:, j, kc * 128:(kc + 1) * 128], at, at[:, kc, :], start=(kc == 0), stop=(kc == 7))
                    for jj in range(4):
                        for kc in range(8):
                            S.mmg(ps[4], ps[4][:, jj * 128:(jj + 1) * 128], at, at[:, kc, jj * 128:(jj + 1) * 128], wt, wt[:, 3, kc * 128:(kc + 1) * 128], start=(kc == 0), stop=(kc == 7))

                abi0 = abi
                load_at(0)
                proj(0)
                for b in range(NBLK):
                    abi += 1
                    if b + 1 < NBLK:
                        load_at(b + 1)
                    blk = slice(b * NB, (b + 1) * NB)
                    for d in range(2):
                        S.act(tA[d], tA[d][:], ps[d], ps[d][:, :], AF.Sigmoid, scale=-1.0)
                    S.act(sgT, sgT[:, blk], ps[3], ps[3][:, :], AF.Silu)
                    S.cp("dve", vtok, vtok[:, b * 4:(b + 1) * 4, :], ps[4], ps[4][:, :].rearrange("p (j v) -> p j v", v=128))
                    S.cp("dve", qsb, qsb[:], ps[2], ps[2][:, :])
                    if b + 1 < NBLK:
                        proj(b + 1)
                    for d in range(2):
                        S.tsc("dve", tA[d], tA[d][:], tA[d], tA[d][:], lbt[:, 8 + h:9 + h], None, ALU.mult, rd=[lbt])
                    for d in range(2):
                        S.act(tG[d], tG[d][:], tA[d], tA[d][:], AF.Ln, bias=1.0, scale=-1.0)
                    for d in range(2):
                        S.op("dve", (lambda o, m, g_: (lambda e: e.tensor_tensor_scan(out=o, data0=m, data1=g_, initial=0.0, op0=ALU.mult, op1=ALU.add)))(tP[d][:], scanm, tG[d][:]), reads=[cst, tG[d]], writes=[tP[d]])
                    Tb0 = c3(tP[0][:])[:, :, 63:64].to_broadcast([128, 8, 64])
                    Tb1 = c3(tP[1][:])[:, :, 63:64].to_broadcast([128, 8, 64])
                    S.tt("dve", tX[0], c3(tX[0][:]), tP[0], Tb0, tP[0], c3(tP[0][:]), ALU.subtract)
                    S.tt("dve", tEe[1], tEe[1][:], tP[1], tP[1][:], tG[1], tG[1][:], ALU.subtract)
                    S.tt("dve", tX[1], c3(tX[1][:]), tP[1], Tb1, tEe[1], c3(tEe[1][:]), ALU.subtract)
                    Gt = (tP[0], tX[1])
                    S.act(tEa[0], tEa[0][:], Gt[0], Gt[0][:], AF.Exp)
                    S.act(tEb[0], tEb[0][:], Gt[0], Gt[0][:], AF.Exp, scale=-1.0)
                    S.act(tEe[0], tEe[0][:], tX[0], tX[0][:], AF.Exp)
                    S.act(tEe[1], tEe[1][:], tEe[1], tEe[1][:], AF.Exp)
                    S.act(tEa[1], tEa[1][:], Gt[1], Gt[1][:], AF.Exp)
                    S.act(tEb[1], tEb[1][:], Gt[1], Gt[1][:], AF.Exp, scale=-1.0)
                    for d in range(2):
                        S.stt(qdec[d], qdec[d][:, blk], qsb, qsb[:], HSCALE, tEa[d], tEa[d][:], ALU.mult, ALU.mult)
                        S.tt("dve", kinvT[d], kinvT[d][:], tA[d], tA[d][:], tEb[d], tEb[d][:], ALU.mult)
                        S.tt("pool", kendT[d], kendT[d][:], tA[d], tA[d][:], tEe[d], tEe[d][:], ALU.mult)
                        col = 63 if d == 0 else 0
                        S.cp("pool", egl[d], egl[d][:, b * 8:(b + 1) * 8], tEa[d], c3(tEa[d][:])[:, :, col])
                    for d in range(2):
                        bk = ps[6 + d]
                        for jj in range(4):
                            S.mmg(bk, bk[:, jj * 128:(jj + 1) * 128], kendT[d], kendT[d][:, jj * 128:(jj + 1) * 128], identb, identb[:], start=True, stop=True)
                        S.cp("act", kend_tok[d], kend_tok[d][:, b * 4:(b + 1) * 4, :], bk, bk[:, :].rearrange("p (j v) -> p j v", v=128))
                    for d in range(2):
                        for jj in range(4):
                            tk = slice(jj * 128, (jj + 1) * 128)
                            tq = slice(b * NB + jj * 128, b * NB + (jj + 1) * 128)
                            S.mmg(ps[5], ps[5][:, tk], kinvT[d], kinvT[d][:, tk], qdec[d], qdec[d][:, tq], start=True, stop=True)
                        mcol = C_MF if d == 0 else C_MB
                        S.tt("dve", attnT[d], attnT[d][:, b * 4:(b + 1) * 4, :], ps[5], ps[5][:, :].rearrange("p (j v) -> p j v", v=128),
                             cst, cst[:, mcol:mcol + 128].unsqueeze(1).to_broadcast([128, 4, 128]), ALU.mult)
                zi = [0, 0]
                for d in range(2):
                    S.memset("pool", Sst[d][0], Sst[d][0][:], 0.0)
                for step in range(64):
                    for d in range(2):
                        c = step if d == 0 else 63 - step
                        ub = ps[2 * (step % 2) + d]
                        usl = slice(0, 128)
                        rows = slice((c % 2) * 64, (c % 2) * 64 + 64)
                        S.mmg(ub, ub[:, usl], kend_tok[d], kend_tok[d][rows, c // 2, :], vtok, vtok[rows, c // 2, :], start=True, stop=True)
                        zc = Sst[d][zi[d] % 2]
                        zn = Sst[d][(zi[d] + 1) % 2]
                        zi[d] += 1
                        S.cp("act" if d == 0 else "pool", Sbf[d], Sbf[d][:, c, :], zc, zc[:])
                        S.stt(zn, zn[:], zc, zc[:], egl[d][:, c:c + 1], ub, ub[:, usl], ALU.mult, ALU.add, rd=[egl[d]])
                def out_mm(b):
                    ob = ps[4 + b % 2]
                    for jj in range(4):
                        tt_ = b * 4 + jj
                        osl = slice(jj * 128, (jj + 1) * 128)
                        S.mmg(ob, ob[:, osl], vtok, vtok[:, tt_, :], attnT[0], attnT[0][:, tt_, :], start=True, stop=False)
                        S.mmg(ob, ob[:, osl], vtok, vtok[:, tt_, :], attnT[1], attnT[1][:, tt_, :], start=False, stop=False)
                        for cc in range(2):
                            c = tt_ * 2 + cc
                            csl = slice(jj * 128 + cc * 64, jj * 128 + cc * 64 + 64)
                            qsl = slice(c * 64, (c + 1) * 64)
                            S.mmg(ob, ob[:, csl], Sbf[0], Sbf[0][:, c, :], qdec[0], qdec[0][:, qsl], start=False, stop=False)
                            S.mmg(ob, ob[:, csl], Sbf[1], Sbf[1][:, c, :], qdec[1], qdec[1][:, qsl], start=False, stop=(cc == 1))
                    S.act(osq[b % 2], osq[b % 2][:], ob, ob[:, :], AF.Square)

                def out_norm(b):
                    ob = ps[4 + b % 2]
                    rstd_from_sq(osq[b % 2], [osq[b % 2][:]], 128, ps[6 + b % 2], orstd, orstd[:])
                    S.stt(otmp, otmp[:], ob, ob[:, :], vec[:, V_ONORM + h:V_ONORM + h + 1], orstd, orstd[:], ALU.mult, ALU.mult, rd=[vec])
                    yb = yblk[b % 2]
                    S.tt("pool", yb, yb[:], otmp, otmp[:], sgT, sgT[:, b * NB:(b + 1) * NB], ALU.mult)
                    S.dma("sp", YT, YT[h * 128:(h + 1) * 128, s * SEQ + b * NB:s * SEQ + (b + 1) * NB], yb, yb[:], group=True)

                for b in range(NBLK):
                    out_mm(b)
                    if b >= 1:
                        out_norm(b - 1)
                out_norm(NBLK - 1)
                it += 1
        S.flush()
        if stop_after == "B":
            return nc

        def post_mixer(layer, wmix, kcm, last):
            S.begin()
            nring = 5
            ring = [S.sbuf(f"wr{i}", [128, 4096], BF16) for i in range(nring)]
            yin = S.sbuf("yin", [128, kcm, NB], BF16)
            hTl = [S.sbuf(f"hTl{i}", [128, 8, NB], F32) for i in range(2)]
            pin = S.sbuf("pin", [128, 4, 256], F32)
            pT = S.sbuf("pT", [128, 2, NB], BF16)
            mt = S.sbuf("mt", [128, 8, NB], F32)
            mtX = S.sbuf("mtX", [128, 8, NB], F32)
            sqc = S.sbuf("sqc", [128, 8, NB], BF16)
            rs = S.sbuf("rs", [128, NB], F32)
            a2 = S.sbuf("a2", [128, 8, NB], BF16)
            hbf = S.sbuf("hbf", [128, 8, NB], BF16)
            hid = S.sbuf("hid", [128, 32, NB], BF16)
            rl = [S.sbuf(f"rl{i}", [128, NB], F32) for i in range(2)]
            gsb = [S.sbuf(f"gsb{i}", [128, NB], F32) for i in range(2)]
            tmpn = [S.sbuf(f"tmpn{i}", [128, NB], F32) for i in range(2)]
            if not last:
                aTn = S.sbuf("aTn", [128, 8, NB], BF16)
            else:
                osg = [S.sbuf(f"osg{i}", [128, NB], F32) for i in range(2)]
            per = 4096 // (kcm * 128)

            order = [("C1", 0), ("C3", 0)]
            for b in range(nblk):
                if b + 1 < nblk:
                    order.append(("C1", b + 1))
                order.append(("C4", b))
                if b + 1 < nblk:
                    order.append(("C3", b + 1))
                order.append(("C5", b))
            chunks = []
            for st, b in order:
                if st == "C1":
                    for o in range(0, 8, per):
                        chunks.append((wmix, o, per, kcm * 128))
                elif st == "C3":
                    for o in range(0, 32, 4):
                        chunks.append((wb[f"w1_{layer}"], o, 4, 1024))
                elif st == "C4":
                    for o in range(8):
                        chunks.append((wb[f"w2_{layer}"], o, 1, 4096))
                else:
                    chunks.append((wb[f"wpp_{layer}"], 0, 8, 256))
                    for o in range(0, 8, 4):
                        chunks.append((wb[f"wpg_{layer}"], o, 4, 1024))
            loaded = {}
            gpos = [0]

            def wload(gi):
                if gi >= len(chunks) or gi in loaded:
                    return
                wt_, o0, n, el = chunks[gi]
                r = ring[gi % nring]
                S.dma("sp", r, r[:, 0:n * el].rearrange("p (o k) -> p o k", o=n), wt_, wt_[o0:o0 + n].rearrange("o p k -> p o k"))
                loaded[gi] = r

            def wget():
                gi = gpos[0]
                gpos[0] += 1
                for a in range(gi, gi + nring - 2):
                    wload(a)
                return loaded.pop(gi)

            chains = []

            def pop_one():
                while chains:
                    try:
                        next(chains[0][2])
                        chains[0][3] += 1
                        return
                    except StopIteration:
                        chains.pop(0)

            def ensure_steps(kind, b, n):
                while True:
                    tgt = [c for c in chains if c[0] == kind and c[1] == b]
                    if not tgt or tgt[0][3] >= n:
                        return
                    pop_one()

            def force(kind, b):
                while any(c[0] == kind and c[1] == b for c in chains):
                    pop_one()

            def drain():
                while chains:
                    pop_one()

            def ld_h(b):
                S.dma("sp", hTl[b % 2], hTl[b % 2][:], HT, hview(HT, b * NB))

            def ld_y(b):
                S.dma("sp", yin, yin[:], YT, hview(YT[0:kcm * 128], b * NB))

            def ld_p(b):
                s_, bb = divmod(b, NBLK)
                S.dma("sp", pin, pin[:], p_d, p_d[layer, s_, bb * NB:(bb + 1) * NB, :].rearrange("(j p) d -> p j d", p=128))

            def squares(src_t, src_ap):
                for c in range(8):
                    S.act(sqc, sqc[:, c, :], src_t, src_ap(c), AF.Square)

            def rstd_now():
                rstd_from_sq(sqc, [sqc[:, c, :] for c in range(8)], D, ps[7], rs, rs[:])

            def resid_apply(src_t, gcol, hT):
                for c in range(8):
                    tn = tmpn[c % 2]
                    S.tt("pool" if c % 2 == 0 else "dve", tn, tn[:], src_t, src_t[:, c, :], rs, rs[:], ALU.mult)
                    S.stt(hT, hT[:, c, :], tn, tn[:], vec[:, gcol + c:gcol + c + 1], hT, hT[:, c, :], ALU.mult, ALU.add, rd=[vec])

            def bf_apply(hT, gcol, dst):
                for c in range(8):
                    S.stt(dst, dst[:, c, :], hT, hT[:, c, :], vec[:, gcol + c:gcol + c + 1], rs, rs[:], ALU.mult, ALU.mult, rd=[vec])

            def chain_X(b):
                hT = hTl[b % 2]
                ld_h(b)
                squares(mtX, lambda c: mtX[:, c, :])
                yield
                rstd_now()
                resid_apply(mtX, V_POST_MIX + 8 * layer, hT)
                squares(hT, lambda c: hT[:, c, :])
                yield
                rstd_now()
                bf_apply(hT, V_PRE_MLP + 8 * layer, a2)

            def chain_F(b):
                hT = hTl[b % 2]
                squares(mt, lambda c: mt[:, c, :])
                yield
                rstd_now()
                resid_apply(mt, V_POST_MLP + 8 * layer, hT)
                for c in range(8):
                    S.cp("act", hbf, hbf[:, c, :], hT, hT[:, c, :])

            def chain_T(b):
                hT = hTl[b % 2]
                squares(mt, lambda c: mt[:, c, :])
                yield
                rstd_now()
                resid_apply(mt, V_PLE + 8 * layer, hT)
                if not last:
                    S.dma("sp", HT, hview(HT, b * NB), hT, hT[:], group=True)
                    squares(hT, lambda c: hT[:, c, :])
                    yield
                    rstd_now()
                    bf_apply(hT, V_PRE_MIX + 8 * (layer + 1), aTn)
                    S.dma("sp", AT, hview(AT, b * NB), aTn, aTn[:], group=True)
                else:
                    yield
                    s_, bb = divmod(b, NBLK)
                    k = 0
                    for j in range(4):
                        for half in range(2):
                            bank = ps[4 + k % 2]
                            for q in range(4):
                                kc = half * 4 + q
                                if q == 0:
                                    S.op("pe", (lambda o, i_: (lambda e: e.transpose(out=o, in_=i_, identity=identf[:])))(bank[:, 0:128], hT[:, kc, j * 128:(j + 1) * 128]), reads=[hT, identf], writes=[bank])
                                else:
                                    S.tr(bank, bank[:, q * 128:(q + 1) * 128], hT, hT[:, kc, j * 128:(j + 1) * 128], identf, identf[:])
                            og = osg[k % 2]
                            S.cp("dve", og, og[:], bank, bank[:, :])
                            S.dma("sp", out_d, out_d[s_, bb * NB + j * 128:bb * NB + (j + 1) * 128, half * 512:(half + 1) * 512], og, og[:])
                            k += 1
                            if k % 4 == 0:
                                yield

            def start_chain(kind, b, gen):
                chains.append([kind, b, gen, 0])

            def st_C1(b):
                oc = 0
                for _ in range(8 // per):
                    r = wget()
                    for o in range(per):
                        bank = ps[oc % 4]
                        for kc in range(kcm):
                            S.mm(bank, bank[:, :], r, r[:, o * kcm * 128 + kc * 128:o * kcm * 128 + (kc + 1) * 128], yin, yin[:, kc, :], start=(kc == 0), stop=(kc == kcm - 1))
                        S.cp("act", mtX, mtX[:, oc, :], bank, bank[:, :])
                        oc += 1
                if b + 1 < nblk:
                    ld_y(b + 1)
                start_chain("X", b, chain_X(b))
                if b >= 2:
                    ensure_steps("T", b - 2, 2)

            def st_C3(b):
                force("X", b)
                oc = 0
                for _ in range(8):
                    r = wget()
                    for o in range(4):
                        bank = ps[oc % 4]
                        for kc in range(8):
                            S.mm(bank, bank[:, :], r, r[:, o * 1024 + kc * 128:o * 1024 + (kc + 1) * 128], a2, a2[:, kc, :], start=(kc == 0), stop=(kc == 7))
                        rr = rl[oc % 2]
                        S.act(rr, rr[:], bank, bank[:, :], AF.Relu)
                        S.tt("pool", hid, hid[:, oc, :], rr, rr[:], rr, rr[:], ALU.mult)
                        oc += 1
                        if oc % 4 == 0:
                            pop_one()

            def st_C4(b):
                if b >= 1:
                    ensure_steps("T", b - 1, 2)
                for oc in range(8):
                    r = wget()
                    bank = ps[oc % 4]
                    for kc in range(32):
                        S.mm(bank, bank[:, :], r, r[:, kc * 128:(kc + 1) * 128], hid, hid[:, kc, :], start=(kc == 0), stop=(kc == 31))
                    S.cp("act", mt, mt[:, oc, :], bank, bank[:, :])
                    if oc % 2 == 0:
                        pop_one()
                start_chain("F", b, chain_F(b))
                force_first("F", b)

            def force_first(kind, b):
                tgt = [c for c in chains if c[0] == kind and c[1] == b][0]
                while chains and chains[0] is not tgt:
                    pop_one()
                if chains and chains[0] is tgt:
                    pop_one()

            def st_pT(b):
                for kc in range(2):
                    bank = ps[6]
                    for j in range(4):
                        if j == 0:
                            S.op("pe", (lambda o, i_: (lambda e: e.transpose(out=o, in_=i_, identity=identf[:])))(bank[:, 0:128], pin[:, 0, kc * 128:(kc + 1) * 128]), reads=[pin, identf], writes=[bank])
                        else:
                            S.tr(bank, bank[:, j * 128:(j + 1) * 128], pin, pin[:, j, kc * 128:(kc + 1) * 128], identf, identf[:])
                    S.cp("dve", pT, pT[:, kc, :], bank, bank[:, :])

            def st_C5(b):
                force("F", b)
                if b >= 1:
                    force("T", b - 1)
                st_pT(b)
                if b + 1 < nblk:
                    ld_p(b + 1)
                rp = wget()
                oc = 0
                for _ in range(2):
                    r = wget()
                    for o in range(4):
                        bg = ps[oc % 2]
                        be = ps[2 + oc % 2]
                        for kc in range(8):
                            S.mm(bg, bg[:, :], r, r[:, o * 1024 + kc * 128:o * 1024 + (kc + 1) * 128], hbf, hbf[:, kc, :], start=(kc == 0), stop=(kc == 7))
                        for kc in range(2):
                            S.mm(be, be[:, :], rp, rp[:, oc * 256 + kc * 128:oc * 256 + (kc + 1) * 128], pT, pT[:, kc, :], start=(kc == 0), stop=(kc == 1))
                        gg = gsb[oc % 2]
                        S.act(gg, gg[:], bg, bg[:, :], AF.Sigmoid)
                        S.tt("dve", mt, mt[:, oc, :], be, be[:, :], gg, gg[:], ALU.mult)
                        oc += 1
                start_chain("T", b, chain_T(b))
                force_first("T", b)

            ld_y(0)
            ld_p(0)
            for st, b in order:
                {"C1": st_C1, "C3": st_C3, "C4": st_C4, "C5": st_C5}[st](b)
            drain()
            S.flush()

        post_mixer(0, wb["hg_w_out"], 8, last=False)
        if stop_after == "C":
            return nc

        CKV = scratch("CKV", [256, T], BF16)
        KR = scratch("KR", [64, T], BF16)
        KSS = scratch("KSS", [1, T], F32)
        S.begin()
        ASCALE = 192.0 ** -0.5
        win_t = S.sbuf("win_t", [128, 6, 1024], BF16)
        wq_t = S.sbuf("wq_t", [128, 16, 768], BF16)
        S.dma("sp", win_t, win_t[:], wb["mla_w_in"], wb["mla_w_in"][:].rearrange("o p k -> p o k"))
        S.dma("sp", wq_t, wq_t[:], wb["mla_w_uq"], wb["mla_w_uq"][:].rearrange("o p k -> p o k"))
        atD = [S.sbuf(f"atD{i}", [128, 8, NB], BF16) for i in range(2)]
        posi = S.sbuf("posi", [64, NB], I32)
        ang = S.sbuf("ang", [64, NB], F32)
        angn = S.sbuf("angn", [64, NB], F32)
        angi = S.sbuf("angi", [64, NB], I32)
        cosT = S.sbuf("cosT", [64, NB], F32)
        sinT = S.sbuf("sinT", [64, NB], F32)
        cq_sb = S.sbuf("cq_sb", [128, 3, NB], F32)
        cq_sq = S.sbuf("cq_sq", [128, 3, NB], BF16)
        cqn = S.sbuf("cqn", [128, 3, NB], BF16)
        kv_sb = S.sbuf("kv_sb", [128, 2, NB], F32)
        kv_sq = S.sbuf("kv_sq", [128, 2, NB], BF16)
        ckv_b = [S.sbuf(f"ckv_b{i}", [128, 2, NB], BF16) for i in range(2)]
        rsD = S.sbuf("rsD", [128, NB], F32)
        kr_b = [S.sbuf(f"kr_b{i}", [64, NB], BF16) for i in range(2)]
        kr_sq = S.sbuf("kr_sq", [64, NB], BF16)
        kss_b = [S.sbuf(f"kss_b{i}", [65, NB], F32) for i in range(2)]
        r1 = [S.sbuf(f"r1_{i}", [64, NB], F32) for i in range(2)]
        r2 = [S.sbuf(f"r2_{i}", [64, NB], F32) for i in range(2)]
        qn_b = [S.sbuf(f"qn_b{i}", [128, NB], BF16) for i in range(3)]
        qn_sq = [S.sbuf(f"qn_sq{i}", [128, NB], BF16) for i in range(2)]
        qr_sq = [S.sbuf(f"qr_sq{i}", [64, NB], BF16) for i in range(2)]
        qr_b = [S.sbuf(f"qr_b{i}", [65, NB], BF16) for i in range(3)]
        qnl = [S.sbuf(f"qnl{i}", [65, NB], F32) for i in range(2)]
        TWO_PI_HI = 6.28125
        TWO_PI_LO = 0.0019353071795864769
        hi = 0
        for bi in range(nblk):
            s, b = divmod(bi, NBLK)
            t0 = bi * NB
            at = atD[bi % 2]
            S.dma("sp", at, at[:], AT, hview(AT, t0))
            S.dma("sp", posi, posi[:], pos_d, pos_d[s:s + 1, b * NB:(b + 1) * NB].partition_broadcast(64))
            S.cp("dve", ang, ang[:], posi, posi[:])
            S.tsc("dve", ang, ang[:], ang, ang[:], cst[0:64, C_INVF:C_INVF + 1], None, ALU.mult, rd=[cst])
            for tab, shift in ((sinT, 0.0), (cosT, float(np.pi / 2))):
                S.tsc("dve", angn, angn[:], ang, ang[:], shift, float(1.0 / (2 * np.pi)), ALU.add, ALU.mult)
                S.cp("dve", angi, angi[:], angn, angn[:])
                S.cp("dve", angn, angn[:], angi, angi[:])
                S.tsc("dve", tab, tab[:], ang, ang[:], shift, None, ALU.add)
                S.stt(tab, tab[:], angn, angn[:], -TWO_PI_HI, tab, tab[:], ALU.mult, ALU.add)
                S.stt(tab, tab[:], angn, angn[:], -TWO_PI_LO, tab, tab[:], ALU.mult, ALU.add)
                S.tsc("dve", tab, tab[:], tab, tab[:], float(np.pi), float(-np.pi), ALU.min, ALU.max)
                S.act(tab, tab[:], tab, tab[:], AF.Sin)
            S.tsc("dve", sinT, sinT[0:32, :], sinT, sinT[0:32, :], -1.0, None, ALU.mult)
            for oc in range(3):
                bank = ps[oc % 4]
                for kc in range(8):
                    S.mm(bank, bank[:, :], win_t, win_t[:, oc, kc * 128:(kc + 1) * 128], at, at[:, kc, :], start=(kc == 0), stop=(kc == 7))
                S.cp("dve", cq_sb, cq_sb[:, oc, :], bank, bank[:, :])
                S.act(cq_sq, cq_sq[:, oc, :], bank, bank[:, :], AF.Square)
            rstd_from_sq(cq_sq, [cq_sq[:, c, :] for c in range(3)], 384, ps[7], rsD, rsD[:])
            for c in range(3):
                S.stt(cqn, cqn[:, c, :], cq_sb, cq_sb[:, c, :], vec[:, V_QNORM + c:V_QNORM + c + 1], rsD, rsD[:], ALU.mult, ALU.mult, rd=[vec])
            for oc in range(2):
                bank = ps[oc % 4]
                for kc in range(8):
                    S.mm(bank, bank[:, :], win_t, win_t[:, 3 + oc, kc * 128:(kc + 1) * 128], at, at[:, kc, :], start=(kc == 0), stop=(kc == 7))
                S.cp("dve", kv_sb, kv_sb[:, oc, :], bank, bank[:, :])
                S.act(kv_sq, kv_sq[:, oc, :], bank, bank[:, :], AF.Square)
            rstd_from_sq(kv_sq, [kv_sq[:, c, :] for c in range(2)], 256, ps[7], rsD, rsD[:])
            ck = ckv_b[bi % 2]
            for c in range(2):
                S.stt(ck, ck[:, c, :], kv_sb, kv_sb[:, c, :], vec[:, V_KVNORM + c:V_KVNORM + c + 1], rsD, rsD[:], ALU.mult, ALU.mult, rd=[vec])
            S.dma("sp", CKV, hview(CKV, t0), ck, ck[:], group=True)
            for half, bank in ((0, ps[2]), (1, ps[3])):
                for kc in range(8):
                    S.mm(bank, bank[0:64, :], win_t, win_t[:, 5, kc * 128 + half * 64:kc * 128 + half * 64 + 64], at, at[:, kc, :], start=(kc == 0), stop=(kc == 7))
            krb = kr_b[bi % 2]
            S.tt("dve", r1[0], r1[0][:], ps[2], ps[2][0:64, :], cosT, cosT[:], ALU.mult)
            S.tt("dve", r2[0], r2[0][:], ps[3], ps[3][0:64, :], sinT, sinT[:], ALU.mult)
            S.tt("pool", krb, krb[:], r1[0], r1[0][:], r2[0], r2[0][:], ALU.add)
            S.act(kr_sq, kr_sq[:], ps[2], ps[2][0:64, :], AF.Square)
            S.dma("sp", KR, KR[:, t0:t0 + NB], krb, krb[:], group=True)
            S.mm(ps[7], ps[7][0:65, :], onesb, onesb[0:64, 0:65], kr_sq, kr_sq[:], start=True, stop=True)
            kb = kss_b[bi % 2]
            S.cp("dve", kb, kb[64:65, :], ps[7], ps[7][64:65, :])
            S.dma("sp", KSS, KSS[0:1, t0:t0 + NB], kb, kb[64:65, :], group=True)
            for h in range(16):
                bn, br, bs = ps[(3 * hi) % 6], ps[(3 * hi + 1) % 6], ps[(3 * hi + 2) % 6]
                for kc in range(3):
                    S.mm(bn, bn[:, :], wq_t, wq_t[:, h, kc * 256:kc * 256 + 128], cqn, cqn[:, kc, :], start=(kc == 0), stop=(kc == 2))
                for kc in range(3):
                    S.mm(br, br[0:64, :], wq_t, wq_t[:, h, kc * 256 + 128:kc * 256 + 192], cqn, cqn[:, kc, :], start=(kc == 0), stop=(kc == 2))
                for kc in range(3):
                    S.mm(bs, bs[0:64, :], wq_t, wq_t[:, h, kc * 256 + 192:kc * 256 + 256], cqn, cqn[:, kc, :], start=(kc == 0), stop=(kc == 2))
                qn = qn_b[hi % 3]
                qr = qr_b[hi % 3]
                S.cp("act", qn, qn[:], bn, bn[:, :])
                S.act(qn_sq[hi % 2], qn_sq[hi % 2][:], bn, bn[:, :], AF.Square)
                S.act(qr_sq[hi % 2], qr_sq[hi % 2][:], br, br[0:64, :], AF.Square)
                S.tt("dve", r1[hi % 2], r1[hi % 2][:], br, br[0:64, :], cosT, cosT[:], ALU.mult)
                S.tt("dve", r2[hi % 2], r2[hi % 2][:], bs, bs[0:64, :], sinT, sinT[:], ALU.mult)
                S.tt("pool", qr, qr[0:64, :], r1[hi % 2], r1[hi % 2][:], r2[hi % 2], r2[hi % 2][:], ALU.add)
                bq = ps[6 + hi % 2]
                S.mm(bq, bq[0:65, :], onesb, onesb[:, 0:65], qn_sq[hi % 2], qn_sq[hi % 2][:], start=True, stop=False)
                S.mm(bq, bq[0:65, :], onesb, onesb[0:64, 0:65], qr_sq[hi % 2], qr_sq[hi % 2][:], start=False, stop=True)
                ql = qnl[hi % 2]
                S.act(ql, ql[64:65, :], bq, bq[64:65, :], AF.Ln)
                S.act(ql, ql[64:65, :], ql, ql[64:65, :], AF.Exp, scale=0.5)
                S.tsc("dve", qr, qr[64:65, :], ql, ql[64:65, :], -1.0, None, ALU.mult)
                S.dma("sp", QN, QN[h, :, t0:t0 + NB], qn, qn[:], group=True)
                S.dma("sp", QR, QR[h, :, t0:t0 + NB], qr, qr[:], group=True)
                hi += 1
        S.flush()
        if stop_after == "D":
            return nc

        S.begin()
        wkv_t = S.sbuf("wkv_t", [128, 16, 512], BF16)
        S.dma("sp", wkv_t, wkv_t[:], wb["mla_w_ukv"], wb["mla_w_ukv"][:].rearrange("o p k -> p o k"))
        ckv_s = S.sbuf("ckv_s", [128, 2, SEQ], BF16)
        krs = [S.sbuf(f"krs{i}", [65, SEQ], BF16) for i in range(2)]
        kss_s = S.sbuf("kss_s", [65, SEQ], F32)
        knT = [S.sbuf(f"knT{i}", [128, SEQ], BF16) for i in range(2)]
        kn_sq = S.sbuf("kn_sq", [128, NB], BF16)
        vh = [S.sbuf(f"vh{i}", [128, 32, 132], BF16) for i in range(2)]
        ktot = S.sbuf("ktot", [65, SEQ], F32)
        kmax = S.sbuf("kmax", [65, 2], F32)
        qnE = [S.sbuf(f"qnE{i}", [128, NB], BF16) for i in range(2)]
        qrE = [S.sbuf(f"qrE{i}", [65, NB], BF16) for i in range(2)]
        oT_b = [S.sbuf(f"oT_b{i}", [128, NB], BF16) for i in range(2)]
        for i in range(2):
            S.memset("pool", vh[i], vh[i][:], 1.0)
        pT4 = [S.sbuf(f"pT4_{i}", [128, NB], BF16) for i in range(4)]
        accD = [[S.sbuf(f"accD{i}_{j}", [128, NB], F32) for j in range(4)] for i in range(2)]
        accP = [S.sbuf(f"accP{i}", [128, NB], F32) for i in range(2)]
        onesf = S.sbuf("onesf", [128, 128], F32)
        recE = [S.sbuf(f"recE{i}", [128, NB], F32) for i in range(2)]
        S.memset("dve", onesf, onesf[:], 1.0)

        def prep_head(h, slot):
            kn, v, kr = knT[slot], vh[slot], krs[slot]
            for b in range(NBLK):
                bank = ps[6 + b % 2]
                for kc in range(2):
                    S.mm(bank, bank[:, :], wkv_t, wkv_t[:, h, kc * 256:kc * 256 + 128], ckv_s, ckv_s[:, kc, b * NB:(b + 1) * NB], start=(kc == 0), stop=(kc == 1))
                S.cp("dve", kn, kn[:, b * NB:(b + 1) * NB], bank, bank[:, :])
                S.tt("pool", kn_sq, kn_sq[:], kn, kn[:, b * NB:(b + 1) * NB], kn, kn[:, b * NB:(b + 1) * NB], ALU.mult)
                bq = ps[6 + (b + 1) % 2]
                S.mm(bq, bq[0:65, :], onesb, onesb[:, 0:65], kn_sq, kn_sq[:], start=True, stop=True)
                S.tt("dve", ktot, ktot[64:65, b * NB:(b + 1) * NB], bq, bq[64:65, :], kss_s, kss_s[64:65, b * NB:(b + 1) * NB], ALU.add)
            S.op("dve", (lambda o, i_: (lambda e: e.reduce_max(out=o, in_=i_, axis=mybir.AxisListType.X)))(kmax[64:65, 0:1], ktot[64:65, :]), reads=[ktot], writes=[kmax])
            S.act(kmax, kmax[64:65, 1:2], kmax, kmax[64:65, 0:1], AF.Ln)
            S.act(kmax, kmax[64:65, 1:2], kmax, kmax[64:65, 1:2], AF.Exp, scale=0.5)
            S.memset("pool", kr, kr[64:65, :], 1.0)
            S.tsc("dve", kr, kr[64:65, :], kr, kr[64:65, :], kmax[64:65, 1:2], None, ALU.mult, rd=[kmax])
            for tt_ in range(32):
                bank = ps[6 + (tt_ // 4) % 2]
                sl = slice((tt_ % 4) * 128, (tt_ % 4 + 1) * 128)
                for kc in range(2):
                    S.mmg(bank, bank[:, sl], ckv_s, ckv_s[:, kc, tt_ * 128:(tt_ + 1) * 128], wkv_t, wkv_t[:, h, kc * 256 + 128:kc * 256 + 256], start=(kc == 0), stop=(kc == 1))
                if tt_ % 4 == 3:
                    S.cp("dve", v, v[:, tt_ - 3:tt_ + 1, 0:128], bank, bank[:, :].rearrange("p (j v) -> p j v", v=128))

        def load_q(s, h, qb, slot):
            t0 = s * SEQ + qb * NB
            S.dma("sp", qnE[slot], qnE[slot][:], QN, QN[h, :, t0:t0 + NB])
            S.dma("sp", qrE[slot], qrE[slot][:], QR, QR[h, :, t0:t0 + NB])

        LOOK = 2
        hcount = 0
        for s in range(nseq):
            S.dma("sp", ckv_s, ckv_s[:], CKV, hview(CKV, s * SEQ, SEQ))
            S.dma("sp", kss_s, kss_s[64:65, :], KSS, KSS[0:1, s * SEQ:(s + 1) * SEQ])
            for i in range(2):
                S.dma("sp", krs[i], krs[i][0:64, :], KR, KR[:, s * SEQ:(s + 1) * SEQ])
            items = [(h, qb, kt) for h in range(16) for qb in range(NBLK) for kt in range(32)]
            nit = len(items)

            def hslot(h):
                return (hcount + h) % 2

            def qslot(h, qb):
                return (h * NBLK + qb) % 2

            def qk(i_):
                h, qb, kt = items[i_]
                sb_ = ps[i_ % 4]
                ks = slice(kt * 128, (kt + 1) * 128)
                kn, kr = knT[hslot(h)], krs[hslot(h)]
                qn, qr = qnE[qslot(h, qb)], qrE[qslot(h, qb)]
                S.mm(sb_, sb_[:, :], kn, kn[:, ks], qn, qn[:], start=True, stop=False)
                S.mm(sb_, sb_[:, :], kr, kr[0:65, ks], qr, qr[0:65, :], start=False, stop=True)

            deferred = []
            prep_head(0, hslot(0))
            load_q(s, 0, 0, qslot(0, 0))
            for j in range(LOOK):
                qk(j)
            for i_, it in enumerate(items):
                h, qb, kt = it
                qbi = h * NBLK + qb
                if kt == 0:
                    nh, nqb = (h, qb + 1) if qb + 1 < NBLK else (h + 1, 0)
                    if nh < 16:
                        load_q(s, nh, nqb, qslot(nh, nqb))
                if kt == 4 and qb == 3 and h + 1 < 16:
                    prep_head(h + 1, hslot(h + 1))
                if i_ + LOOK < nit:
                    qk(i_ + LOOK)
                sb_ = ps[i_ % 4]
                pt = pT4[i_ % 4]
                S.act(pt, pt[:], sb_, sb_[:, :], AF.Exp, scale=ASCALE)
                v = vh[hslot(h)]
                ob = ps[4 + qbi % 2]
                S.mm(ob, ob[:, :], v, v[:, kt, 0:128], pt, pt[:], start=(kt == 0), stop=(kt == 31))
                if False:
                    pass
                else:
                    ad_ = accD[qbi % 2][kt % 4]
                    if kt < 4:
                        S.cp("dve", ad_, ad_[:], pt, pt[:])
                    else:
                        S.tt("dve", ad_, ad_[:], ad_, ad_[:], pt, pt[:], ALU.add)
                if kt == 3 and deferred:
                    for f in deferred:
                        f()
                    deferred = []
                if kt == 31:
                    def fin(h=h, qb=qb, s=s, qbi=qbi):
                        t0 = s * SEQ + qb * NB
                        ad_, ap_, ob = accD[qbi % 2], accP[qbi % 2], ps[4 + qbi % 2]
                        rec = recE[qbi % 2]
                        oT = oT_b[qbi % 2]
                        bd = ps[6 + qbi % 2]
                        S.tt("pool", ad_[0], ad_[0][:], ad_[0], ad_[0][:], ad_[1], ad_[1][:], ALU.add)
                        S.tt("pool", ad_[2], ad_[2][:], ad_[2], ad_[2][:], ad_[3], ad_[3][:], ALU.add)
                        S.tt("pool", ad_[0], ad_[0][:], ad_[0], ad_[0][:], ad_[2], ad_[2][:], ALU.add)
                        S.mm(bd, bd[:, :], onesf, onesf[:], ad_[0], ad_[0][:], start=True, stop=True)
                        S.act(rec, rec[:], bd, bd[:, :], AF.Ln)
                        S.act(rec, rec[:], rec, rec[:], AF.Exp, scale=-1.0)
                        S.tt("dve", oT, oT[:], ob, ob[:, :], rec, rec[:], ALU.mult)
                        S.dma("sp", YT, YT[h * 128:(h + 1) * 128, t0:t0 + NB], oT, oT[:], group=True)
                    deferred.append(fin)
            for f in deferred:
                f()
            hcount += 16
        S.flush()
        if stop_after == "E":
            return nc

        post_mixer(1, wb["mla_w_o"], 16, last=True)
    return nc


_CACHE = {}


def kernel(x, p, positions, pre_mix_norm, post_mix_norm, pre_mlp_norm, post_mlp_norm,
           w_mlp_in, w_mlp_out, w_ple_proj, w_ple_gate, ple_norm,
           hg_lb_logits, hg_w_in, hg_o_norm, hg_w_out,
           mla_w_in, mla_q_norm, mla_w_uq, mla_kv_norm, mla_w_ukv, mla_w_o):
    inp = dict(x=x, p=p, positions=positions, pre_mix_norm=pre_mix_norm, post_mix_norm=post_mix_norm,
               pre_mlp_norm=pre_mlp_norm, post_mlp_norm=post_mlp_norm, w_mlp_in=w_mlp_in,
               w_mlp_out=w_mlp_out, w_ple_proj=w_ple_proj, w_ple_gate=w_ple_gate, ple_norm=ple_norm,
               hg_lb_logits=hg_lb_logits, hg_w_in=hg_w_in, hg_o_norm=hg_o_norm, hg_w_out=hg_w_out,
               mla_w_in=mla_w_in, mla_q_norm=mla_q_norm, mla_w_uq=mla_w_uq, mla_kv_norm=mla_kv_norm,
               mla_w_ukv=mla_w_ukv, mla_w_o=mla_w_o)
    inp = {k: np.asarray(v) for k, v in inp.items()}
    w = prep_weights(inp)
    nseq = 2
    if "nc" not in _CACHE:
        _CACHE["nc"] = build(nseq=nseq)
    nc = _CACHE["nc"]
    in_maps = []
    for c in range(NCORES):
        m = dict(w)
        m["x"] = np.ascontiguousarray(inp["x"][c * nseq:(c + 1) * nseq], dtype=np.float32)
        m["p"] = np.ascontiguousarray(inp["p"][:, c * nseq:(c + 1) * nseq], dtype=np.float32)
        m["pos"] = np.ascontiguousarray(inp["positions"][c * nseq:(c + 1) * nseq], dtype=np.int32)
        in_maps.append(m)
    res = run_bass_kernel_spmd(nc, in_maps, core_ids=list(range(NCORES)))
    return np.concatenate([r["out"] for r in res.results], axis=0).astype(np.float32)
```

```python
from contextlib import ExitStack
import numpy as np
import concourse.bass as bass
import concourse.mybir as mybir
from concourse.bass_utils import run_bass_kernel_spmd

F32 = mybir.dt.float32
BF16 = mybir.dt.bfloat16
I32 = mybir.dt.int32
AF = mybir.ActivationFunctionType
ALU = mybir.AluOpType

ENGS = ("pe", "act", "dve", "pool", "sp")

D = 1024
SEQ = 4096
NB = 512
NBLK = SEQ // NB
EPS = 1e-6
NCORES = 8


class Tile:
    __slots__ = ("name", "h", "writers", "readers", "dsem", "dcount", "space", "dwaited")

    def __init__(self, name, h, space):
        self.name = name
        self.h = h
        self.space = space
        self.writers = []
        self.readers = []
        self.dsem = None
        self.dcount = 0

    def __getitem__(self, idx):
        return self.h[idx]


class Instr:
    __slots__ = ("eng", "fn", "idx", "waits", "marked", "is_dma", "dtile", "dval", "vc", "cnt")

    def __init__(self, eng, fn, idx):
        self.eng = eng
        self.fn = fn
        self.idx = idx
        self.waits = []
        self.marked = False
        self.is_dma = False
        self.dtile = None
        self.dval = 0
        self.vc = None
        self.cnt = 0


class Sched:
    def __init__(self, nc, gctx):
        self.nc = nc
        self.gctx = gctx
        self.pctx = None
        self.esem = {e: gctx.enter_context(nc.semaphore("s_" + e)) for e in ENGS}
        self.base = {e: 0 for e in ENGS}
        self.tiles = []
        self.live = []
        self._reset()
        self.n_instr = 0

    def _reset(self):
        self.prog = {e: [] for e in ENGS}
        self.vc = {e: {p: -1 for p in ENGS} for e in ENGS}
        for e in ENGS:
            self.vc[e]["dma"] = {}
        for t in self.live:
            t.writers = []
            t.readers = []
        self.live = []

    def begin(self):
        self.pctx = ExitStack()

    def sbuf(self, name, shape, dt, glob=False):
        c = self.gctx if glob else self.pctx
        self.uid = getattr(self, "uid", 0) + 1
        name = f"{name}_{self.uid}"
        h = c.enter_context(self.nc.sbuf_tensor(name, list(shape), dt))
        return Tile(name, h, "sb")

    def psum(self, name, shape, dt=F32, glob=False):
        c = self.gctx if glob else self.pctx
        h = c.enter_context(self.nc.psum_tensor(name, list(shape), dt))
        return Tile(name, h, "ps")

    def dram(self, name, shape, dt, kind="Internal"):
        h = self.nc.dram_tensor(name, list(shape), dt, kind=kind)
        return Tile(name, h.ap(), "dr")

    def _dep(self, ins, d):
        if d is ins:
            return
        e = ins.eng
        if d.is_dma:
            known = self.vc[e]["dma"]
            tot = d.dtile.dcount
            if known.get(id(d.dtile), 0) >= tot:
                return
            known[id(d.dtile)] = tot
            ins.waits.append((d.dtile, tot))
            return
        p = d.eng
        if p == "pe" and e == "pe":
            return
        if self.vc[e][p] >= d.idx:
            return
        ins.waits.append(d)
        d.marked = True
        self.vc[e][p] = d.idx
        for k, v in d.vc.items():
            if k == "dma":
                known = self.vc[e]["dma"]
                for kk, vv in v.items():
                    if known.get(kk, 0) < vv:
                        known[kk] = vv
            elif self.vc[e][k] < v:
                self.vc[e][k] = v

    def op(self, eng, fn, reads=(), writes=(), acc=()):
        ins = Instr(eng, fn, len(self.prog[eng]))
        for t in reads:
            for w in t.writers:
                self._dep(ins, w)
            if t.space == "ps":
                for r in t.readers:
                    if r.eng != eng:
                        self._dep(ins, r)
        for t in writes:
            for w in t.writers:
                self._dep(ins, w)
            for r in t.readers:
                self._dep(ins, r)
        for t in acc:
            for r in t.readers:
                self._dep(ins, r)
        snap = {}
        for k, v in self.vc[eng].items():
            snap[k] = dict(v) if k == "dma" else v
        ins.vc = snap
        for t in reads:
            if not t.readers and not t.writers:
                self.live.append(t)
            t.readers.append(ins)
        for t in writes:
            if not t.readers and not t.writers:
                self.live.append(t)
            t.writers = [ins]
            t.readers = []
        for t in acc:
            if not t.readers and not t.writers:
                self.live.append(t)
            if t.readers:
                t.writers = [ins]
                t.readers = []
            else:
                t.writers.append(ins)
        self.prog[eng].append(ins)
        return ins

    def dma(self, eng, out_t, out_ap, in_t, in_ap, group=False, xw=(), sem_tile=None):
        st = out_t if out_t.space == "sb" else (in_t if in_t.space == "sb" else out_t)
        if sem_tile is not None:
            st = sem_tile

        def fn(e, out_ap=out_ap, in_ap=in_ap):
            return e.dma_start(out=out_ap, in_=in_ap)

        if group:
            ins = self.op(eng, fn, reads=[in_t], acc=[out_t], writes=list(xw))
        else:
            ins = self.op(eng, fn, reads=[in_t], writes=[out_t] + list(xw))
        ins.is_dma = True
        ins.dtile = st
        if st.dsem is None:
            st.dsem = self.gctx.enter_context(self.nc.semaphore("d_" + st.name))
            self.tiles.append(st)
        st.dcount += 16
        ins.dval = st.dcount
        return ins

    def mm(self, ps_t, out_ap, l_t, l_ap, r_t, r_ap, start, stop):
        def fn(e):
            return e.matmul(out_ap, lhsT=l_ap, rhs=r_ap, start=start, stop=stop)
        if start:
            return self.op("pe", fn, reads=[l_t, r_t], writes=[ps_t])
        return self.op("pe", fn, reads=[l_t, r_t], acc=[ps_t])

    def mmg(self, ps_t, out_ap, l_t, l_ap, r_t, r_ap, start, stop):
        def fn(e):
            return e.matmul(out_ap, lhsT=l_ap, rhs=r_ap, start=start, stop=stop)
        return self.op("pe", fn, reads=[l_t, r_t], acc=[ps_t])

    def tr(self, ps_t, out_ap, in_t, in_ap, id_t, id_ap):
        def fn(e):
            return e.transpose(out=out_ap, in_=in_ap, identity=id_ap)
        return self.op("pe", fn, reads=[in_t, id_t], acc=[ps_t])

    def act(self, out_t, out_ap, in_t, in_ap, func, bias=None, scale=None, rd=()):
        kw = {}
        if bias is not None:
            kw["bias"] = bias
        if scale is not None:
            kw["scale"] = scale

        def fn(e):
            return e.activation(out=out_ap, in_=in_ap, func=func, **kw)
        return self.op("act", fn, reads=[in_t] + list(rd), writes=[out_t])

    def tsc(self, eng, out_t, out_ap, in_t, in_ap, s1, s2, op0, op1=None, rd=()):
        def fn(e):
            if op1 is None:
                return e.tensor_scalar(out=out_ap, in0=in_ap, scalar1=s1, scalar2=None, op0=op0)
            return e.tensor_scalar(out=out_ap, in0=in_ap, scalar1=s1, scalar2=s2, op0=op0, op1=op1)
        return self.op(eng, fn, reads=[in_t] + list(rd), writes=[out_t])

    def stt(self, out_t, out_ap, a_t, a_ap, scalar, b_t, b_ap, op0, op1, rd=()):
        def fn(e):
            return e.scalar_tensor_tensor(out=out_ap, in0=a_ap, scalar=scalar, in1=b_ap, op0=op0, op1=op1)
        return self.op("dve", fn, reads=[a_t, b_t] + list(rd), writes=[out_t])

    def tt(self, eng, out_t, out_ap, a_t, a_ap, b_t, b_ap, op):
        def fn(e):
            return e.tensor_tensor(out=out_ap, in0=a_ap, in1=b_ap, op=op)
        return self.op(eng, fn, reads=[a_t, b_t], writes=[out_t])

    def cp(self, eng, out_t, out_ap, in_t, in_ap):
        if eng == "act":
            return self.act(out_t, out_ap, in_t, in_ap, AF.Copy)

        def fn(e):
            return e.tensor_copy(out=out_ap, in_=in_ap)
        return self.op(eng, fn, reads=[in_t], writes=[out_t])

    def memset(self, eng, t, ap, val):
        def fn(e):
            return e.memset(ap, val)
        return self.op(eng, fn, writes=[t])

    def flush(self, final=False):
        nc = self.nc
        esem = self.esem
        for e in ENGS:
            lst = self.prog[e]
            for ins in reversed(lst):
                if not ins.is_dma:
                    ins.marked = True
                    break
            c = self.base[e]
            for ins in lst:
                if ins.is_dma:
                    continue
                if ins.marked:
                    c += 1
                    ins.cnt = c
            self.base[e] = c
            self.n_instr += len(lst)

        def ev(d):
            if isinstance(d, tuple):
                return d[0].dsem, d[1]
            return esem[d.eng], d.cnt

        dma_tails = [(t.dsem, t.dcount) for t in self.tiles if t.dcount]
        eng_tails = [(esem[e], self.base[e]) for e in ENGS if self.base[e] > 0]

        with nc.Block() as block:
            def run(eng_obj, name):
                for ins in self.prog[name]:
                    for d in ins.waits:
                        s, v = ev(d)
                        eng_obj.wait_ge(s, v)
                    bi = ins.fn(eng_obj)
                    if ins.is_dma:
                        bi.then_inc(ins.dtile.dsem, 16)
                    elif ins.marked:
                        bi.then_inc(esem[name], 1)
                for s, v in eng_tails:
                    if s is not esem[name]:
                        eng_obj.wait_ge(s, v)
                for s, v in dma_tails:
                    eng_obj.wait_ge(s, v)

            @block.tensor
            def _(e):
                run(e, "pe")

            @block.scalar
            def _(e):
                run(e, "act")

            @block.vector
            def _(e):
                run(e, "dve")

            @block.gpsimd
            def _(e):
                run(e, "pool")

            @block.sync
            def _(e):
                run(e, "sp")

        self._reset()
        if self.pctx is not None:
            self.pctx.close()
            self.pctx = None


def lay_oc(W, cw=128):
    K, N = W.shape
    KC, OC = K // 128, N // cw
    return np.ascontiguousarray(W.reshape(KC, 128, OC, cw).transpose(2, 1, 0, 3).reshape(OC, 128, KC * cw))


def vec_cols(v):
    C = v.shape[0] // 128
    return np.ascontiguousarray(v.reshape(C, 128).T)


V_PRE_MIX, V_POST_MIX, V_PRE_MLP, V_POST_MLP, V_PLE = 0, 16, 32, 48, 64
V_ONORM = 80
V_QNORM = 88
V_KVNORM = 91
V_LBL = 93
NVEC = 117

C_ID = 0
C_MF = 128
C_MB = 256
C_SCAN = 384
C_INVF = 896
NCONST = 897


def make_consts():
    c = np.zeros((128, NCONST), np.float32)
    c[:, C_ID:C_ID + 128] = np.eye(128, dtype=np.float32)
    j = np.arange(128)[:, None]
    i = np.arange(128)[None, :]
    same = (j // 64) == (i // 64)
    c[:, C_MF:C_MF + 128] = (same & (i >= j)).astype(np.float32)
    c[:, C_MB:C_MB + 128] = (same & (i <= j)).astype(np.float32)
    sm = np.ones(512, np.float32)
    sm[::64] = 0.0
    c[:, C_SCAN:C_SCAN + 512] = sm[None, :]
    inv = (1.0 / (10000.0 ** (np.arange(0, 64, 2, dtype=np.float32) / 64.0))).astype(np.float32)
    c[0:32, C_INVF] = inv
    c[32:64, C_INVF] = inv
    return c


def prep_weights(inp):
    w = {}
    w["hg_w_in"] = lay_oc(inp["hg_w_in"][0])
    w["hg_w_out"] = lay_oc(inp["hg_w_out"][0])
    for l in range(2):
        w[f"w1_{l}"] = lay_oc(inp["w_mlp_in"][l])
        w[f"w2_{l}"] = lay_oc(inp["w_mlp_out"][l])
        w[f"wpp_{l}"] = lay_oc(inp["w_ple_proj"][l])
        w[f"wpg_{l}"] = lay_oc(inp["w_ple_gate"][l])
    win = inp["mla_w_in"][0]
    kr = win[:, 640:704]
    kr_sw = np.concatenate([kr[:, 32:64], kr[:, 0:32]], axis=1)
    w["mla_w_in"] = lay_oc(np.concatenate([win, kr_sw], axis=1))
    wuq = inp["mla_w_uq"][0].reshape(384, 16, 192)
    nope = wuq[:, :, 0:128]
    rope = wuq[:, :, 128:192]
    rope_sw = np.concatenate([rope[:, :, 32:64], rope[:, :, 0:32]], axis=2)
    wq = np.concatenate([nope, rope, rope_sw], axis=2).reshape(384, 16 * 256)
    w["mla_w_uq"] = lay_oc(wq, cw=256)
    w["mla_w_ukv"] = lay_oc(inp["mla_w_ukv"][0], cw=256)
    w["mla_w_o"] = lay_oc(inp["mla_w_o"][0])
    vecs = np.zeros((128, NVEC), np.float32)
    for l in range(2):
        vecs[:, V_PRE_MIX + 8 * l:V_PRE_MIX + 8 * l + 8] = vec_cols(inp["pre_mix_norm"][l])
        vecs[:, V_POST_MIX + 8 * l:V_POST_MIX + 8 * l + 8] = vec_cols(inp["post_mix_norm"][l])
        vecs[:, V_PRE_MLP + 8 * l:V_PRE_MLP + 8 * l + 8] = vec_cols(inp["pre_mlp_norm"][l])
        vecs[:, V_POST_MLP + 8 * l:V_POST_MLP + 8 * l + 8] = vec_cols(inp["post_mlp_norm"][l])
        vecs[:, V_PLE + 8 * l:V_PLE + 8 * l + 8] = vec_cols(inp["ple_norm"][l])
    vecs[:, V_ONORM:V_ONORM + 8] = inp["hg_o_norm"][0].T
    vecs[:, V_QNORM:V_QNORM + 3] = vec_cols(inp["mla_q_norm"][0])
    vecs[:, V_KVNORM:V_KVNORM + 2] = vec_cols(inp["mla_kv_norm"][0])
    for j in range(3):
        vecs[:, V_LBL + 8 * j:V_LBL + 8 * j + 8] = vec_cols(inp["hg_lb_logits"][j])
    w["vecs"] = vecs
    w["consts"] = make_consts()
    return w


WSHAPES = {
    "hg_w_in": [40, 128, 1024], "hg_w_out": [8, 128, 1024],
    "w1_0": [32, 128, 1024], "w2_0": [8, 128, 4096], "wpp_0": [8, 128, 256], "wpg_0": [8, 128, 1024],
    "w1_1": [32, 128, 1024], "w2_1": [8, 128, 4096], "wpp_1": [8, 128, 256], "wpg_1": [8, 128, 1024],
    "mla_w_in": [6, 128, 1024], "mla_w_uq": [16, 128, 768], "mla_w_ukv": [16, 128, 512],
    "mla_w_o": [8, 128, 2048],
}


def build(nseq=2, stop_after="F", dump=()):
    nc = bass.Bass("TRN2", target_bir_lowering=False)
    T = nseq * SEQ
    nblk = nseq * NBLK
    with ExitStack() as g:
        S = Sched(nc, g)
        x_d = S.dram("x", [nseq, SEQ, D], F32, kind="ExternalInput")
        p_d = S.dram("p", [2, nseq, SEQ, 256], F32, kind="ExternalInput")
        pos_d = S.dram("pos", [nseq, SEQ], I32, kind="ExternalInput")
        vecs_d = S.dram("vecs", [128, NVEC], F32, kind="ExternalInput")
        consts_d = S.dram("consts", [128, NCONST], F32, kind="ExternalInput")
        wf = {k: S.dram(k, shp, F32, kind="ExternalInput") for k, shp in WSHAPES.items()}
        out_d = S.dram("out", [nseq, SEQ, D], F32, kind="ExternalOutput")

        def scratch(name, shape, dt):
            return S.dram(name, shape, dt, kind="ExternalOutput" if name in dump else "Internal")

        wb = {k: S.dram(k + "_b", shp, BF16) for k, shp in WSHAPES.items()}
        HT = scratch("HT", [D, T], F32)
        AT = scratch("AT", [D, T], BF16)
        YT = scratch("YT", [2048, T], BF16)
        QN = scratch("QN", [16, 128, T], BF16)
        QR = scratch("QR", [16, 65, T], BF16)

        cst = S.sbuf("cst", [128, NCONST], F32, glob=True)
        vec = S.sbuf("vec", [128, NVEC], F32, glob=True)
        identb = S.sbuf("identb", [128, 128], BF16, glob=True)
        identf = S.sbuf("identf", [128, 128], F32, glob=True)
        onesb = S.sbuf("onesb", [128, 128], BF16, glob=True)
        epsb = S.sbuf("epsb", [128, 1], F32, glob=True)
        lbt = S.sbuf("lbt", [128, 16], F32, glob=True)
        ps = [S.psum(f"ps{i}", [128, 512], F32, glob=True) for i in range(8)]

        def hview(dr, t0, n=NB):
            return dr[:, t0:t0 + n].rearrange("(k p) t -> p k t", p=128)

        S.begin()
        S.dma("sp", cst, cst[:], consts_d, consts_d[:])
        S.dma("sp", vec, vec[:], vecs_d, vecs_d[:])
        toks = [Tile(f"tok{i}", None, "dr") for i in range(2)]
        ndw = 0
        import os as _os
        for k in WSHAPES:
            if "w" in _os.environ.get("DEV_SKIP", ""):
                break
            n0, _, el = WSHAPES[k]
            step = max(1, min(n0, 4096 // el))
            for o in range(0, n0, step):
                o2 = min(n0, o + step)
                S.dma("pool", wb[k], wb[k][o:o2], wf[k], wf[k][o:o2], group=True, xw=[toks[ndw % 2]], sem_tile=toks[ndw % 2])
                ndw += 1
        S.cp("dve", identb, identb[:], cst, cst[:, C_ID:C_ID + 128])
        S.cp("dve", identf, identf[:], cst, cst[:, C_ID:C_ID + 128])
        S.memset("dve", onesb, onesb[:], 1.0)
        S.memset("dve", epsb, epsb[:], EPS)
        ex = S.sbuf("ex", [128, 24], F32)
        sm = S.sbuf("sm", [128, 8], F32)
        S.act(ex, ex[:], vec, vec[:, V_LBL:V_LBL + 24], AF.Exp)
        S.tt("dve", sm, sm[:], ex, ex[:, 0:8], ex, ex[:, 8:16], ALU.add)
        S.tt("dve", sm, sm[:], sm, sm[:], ex, ex[:, 16:24], ALU.add)
        S.op("dve", lambda e: e.reciprocal(out=sm[:], in_=sm[:]), reads=[sm], writes=[sm])
        S.tt("dve", lbt, lbt[:, 0:8], ex, ex[:, 0:8], sm, sm[:], ALU.mult)
        S.tsc("dve", lbt, lbt[:, 8:16], lbt, lbt[:, 0:8], -1.0, 1.0, ALU.mult, ALU.add)
        if stop_after == "W":
            dbg = S.sbuf("dbg", [128, 16], F32)
            S.cp("dve", dbg, dbg[:], lbt, lbt[:])
            S.dma("sp", out_d, out_d[0, 0:128, 0:16], dbg, dbg[:])
        import os as _os
        if not _os.environ.get("DEV_NOFLUSHW"):
            S.flush()
        if stop_after == "W":
            return nc

        def rstd_from_sq(sq_t, sq_aps, nfeat, bank, rstd_t, rstd_ap, kparts=None):
            n = len(sq_aps)
            for c, ap in enumerate(sq_aps):
                kp = 128 if kparts is None else kparts[c]
                S.mm(bank, bank[:, :], onesb, onesb[0:kp, :], sq_t, ap, start=(c == 0), stop=(c == n - 1))
            S.act(rstd_t, rstd_ap, bank, bank[:, :], AF.Ln, bias=epsb[:, 0:1], scale=1.0 / nfeat, rd=[epsb])
            S.act(rstd_t, rstd_ap, rstd_t, rstd_ap, AF.Exp, scale=-0.5)

        if not _os.environ.get("DEV_NOFLUSHW"):
            S.begin()
        xin = [S.sbuf(f"xin{i}", [128, 4, D], F32) for i in range(2)]
        hTs = [S.sbuf(f"hT{i}", [128, 8, NB], F32) for i in range(2)]
        sqs = [S.sbuf(f"sq{i}", [128, 8, NB], BF16) for i in range(2)]
        aTs = [S.sbuf(f"aT{i}", [128, 8, NB], BF16) for i in range(2)]
        rstds = [S.sbuf(f"rstd{i}", [128, NB], F32) for i in range(2)]

        def load_x(bi):
            s, b = divmod(bi, NBLK)
            t = xin[bi % 2]
            if "x" in _os.environ.get("DEV_SKIP", ""):
                return
            S.dma("sp", t, t[:], x_d, x_d[s, b * NB:(b + 1) * NB, :].rearrange("(j p) d -> p j d", p=128))

        import os as _os
        nblkA = int(_os.environ.get("DEV_NBLK", nblk))
        skipA = _os.environ.get("DEV_SKIP", "")
        load_x(0)
        for bi in range(nblkA):
            if bi + 1 < nblkA:
                load_x(bi + 1)
            xt, hT, sq, aT, rstd = xin[bi % 2], hTs[bi % 2], sqs[bi % 2], aTs[bi % 2], rstds[bi % 2]
            for kc in range(8):
                bank = ps[kc % 4]
                for j in range(4):
                    if "t" in skipA:
                        continue
                    if j == 0:
                        S.op("pe", (lambda o, i_: (lambda e: e.transpose(out=o, in_=i_, identity=identf[:])))(bank[:, 0:128], xt[:, 0, kc * 128:(kc + 1) * 128]), reads=[xt, identf], writes=[bank])
                    else:
                        S.tr(bank, bank[:, j * 128:(j + 1) * 128], xt, xt[:, j, kc * 128:(kc + 1) * 128], identf, identf[:])
                if "c" not in skipA:
                    S.cp("dve", hT, hT[:, kc, :], bank, bank[:, :])
                if "s" not in skipA:
                    S.act(sq, sq[:, kc, :], bank, bank[:, :], AF.Square)
            if "h" not in skipA:
                S.dma("sp", HT, hview(HT, bi * NB), hT, hT[:], group=True)
            if "n" in skipA:
                continue
            rstd_from_sq(sq, [sq[:, c, :] for c in range(8)], D, ps[4 + bi % 2], rstd, rstd[:])
            for kc in range(8):
                S.stt(aT, aT[:, kc, :], hT, hT[:, kc, :], vec[:, V_PRE_MIX + kc:V_PRE_MIX + kc + 1], rstd, rstd[:], ALU.mult, ALU.mult, rd=[vec])
            if "a" not in skipA:
                S.dma("sp", AT, hview(AT, bi * NB), aT, aT[:], group=True)
        S.flush()
        if stop_after == "A":
            return nc

        S.begin()
        HSCALE = 128.0 ** -0.5
        wts = [S.sbuf(f"wB{i}", [128, 5, 1024], BF16) for i in range(2)]
        ats = [S.sbuf(f"atB{i}", [128, 8, NB], BF16) for i in range(2)]
        qdec = [S.sbuf(f"qdec{d}", [128, SEQ], BF16) for d in range(2)]
        kend_tok = [S.sbuf(f"kend{d}", [128, 32, 128], BF16) for d in range(2)]
        vtok = S.sbuf("vtok", [128, 32, 128], BF16)
        attnT = [S.sbuf(f"attnT{d}", [128, 32, 128], BF16) for d in range(2)]
        sgT = S.sbuf("sgT", [128, SEQ], BF16)
        egl = [S.sbuf(f"egl{d}", [128, 64], F32) for d in range(2)]
        Sbf = [S.sbuf(f"Sbf{d}", [128, 64, 128], BF16) for d in range(2)]
        Sst = [[S.sbuf(f"Sst{d}_{i}", [128, 128], F32) for i in range(2)] for d in range(2)]
        tA = [S.sbuf(f"tA{d}", [128, NB], F32) for d in range(2)]
        tG = [S.sbuf(f"tG{d}", [128, NB], F32) for d in range(2)]
        tP = [S.sbuf(f"tP{d}", [128, NB], F32) for d in range(2)]
        tX = [S.sbuf(f"tX{d}", [128, NB], F32) for d in range(2)]
        tEa = [S.sbuf(f"tEa{d}", [128, NB], F32) for d in range(2)]
        tEb = [S.sbuf(f"tEb{d}", [128, NB], F32) for d in range(2)]
        tEe = [S.sbuf(f"tEe{d}", [128, NB], F32) for d in range(2)]
        kinvT = [S.sbuf(f"kinvT{d}", [128, NB], BF16) for d in range(2)]
        kendT = [S.sbuf(f"kendT{d}", [128, NB], BF16) for d in range(2)]
        qsb = S.sbuf("qsb", [128, NB], F32)
        osq = [S.sbuf(f"osq{i}", [128, NB], BF16) for i in range(2)]
        orstd = S.sbuf("orstd", [128, NB], F32)
        otmp = S.sbuf("otmp", [128, NB], F32)
        yblk = [S.sbuf(f"yblk{i}", [128, NB], BF16) for i in range(2)]
        scanm = cst[:, C_SCAN:C_SCAN + NB]

        def c3(ap):
            return ap.rearrange("p (c t) -> p c t", t=64)

        it = 0
        abi = 0
        for s in range(nseq):
            for h in range(8):
                wt = wts[it % 2]
                for j, oc in enumerate([h, 8 + h, 16 + h, 24 + h, 32 + h]):
                    S.dma("sp", wt, wt[:, j, :], wb["hg_w_in"], wb["hg_w_in"][oc], group=True)
                def load_at(b):
                    at = ats[(abi0 + b) % 2]
                    S.dma("sp", at, at[:], AT, hview(AT, s * SEQ + b * NB))

                def proj(b):
                    at = ats[(abi0 + b) % 2]
                    for j, bank in ((1, ps[0]), (2, ps[1]), (0, ps[2]), (4, ps[3])):
                        for kc in range(8):
                            S.mm(bank, bank[:, :], wt, wt[:, j, kc * 128:(kc + 1) * 128], at, at[:, kc, :], start=(kc == 0), stop=(kc == 7))
                    for jj in range(4):
                        for kc in range(8):
                            S.mmg(ps[4], ps[4][:, jj * 128:(jj + 1) * 128], at, at[:, kc, jj * 128:(jj + 1) * 128], wt, wt[:, 3, kc * 128:(kc + 1) * 128], start=(kc == 0), stop=(kc == 7))

                abi0 = abi
                load_at(0)
                proj(0)
                for b in range(NBLK):
                    abi += 1
                    if b + 1 < NBLK:
                        load_at(b + 1)
                    blk = slice(b * NB, (b + 1) * NB)
                    for d in range(2):
                        S.act(tA[d], tA[d][:], ps[d], ps[d][:, :], AF.Sigmoid, scale=-1.0)
                    S.act(sgT, sgT[:, blk], ps[3], ps[3][:, :], AF.Silu)
                    S.cp("dve", vtok, vtok[:, b * 4:(b + 1) * 4, :], ps[4], ps[4][:, :].rearrange("p (j v) -> p j v", v=128))
                    S.cp("dve", qsb, qsb[:], ps[2], ps[2][:, :])
                    if b + 1 < NBLK:
                        proj(b + 1)
                    for d in range(2):
                        S.tsc("dve", tA[d], tA[d][:], tA[d], tA[d][:], lbt[:, 8 + h:9 + h], None, ALU.mult, rd=[lbt])
                    for d in range(2):
                        S.act(tG[d], tG[d][:], tA[d], tA[d][:], AF.Ln, bias=1.0, scale=-1.0)
                    for d in range(2):
                        S.op("dve", (lambda o, m, g_: (lambda e: e.tensor_tensor_scan(out=o, data0=m, data1=g_, initial=0.0, op0=ALU.mult, op1=ALU.add)))(tP[d][:], scanm, tG[d][:]), reads=[cst, tG[d]], writes=[tP[d]])
                    Tb0 = c3(tP[0][:])[:, :, 63:64].to_broadcast([128, 8, 64])
                    Tb1 = c3(tP[1][:])[:, :, 63:64].to_broadcast([128, 8, 64])
                    S.tt("dve", tX[0], c3(tX[0][:]), tP[0], Tb0, tP[0], c3(tP[0][:]), ALU.subtract)
                    S.tt("dve", tEe[1], tEe[1][:], tP[1], tP[1][:], tG[1], tG[1][:], ALU.subtract)
                    S.tt("dve", tX[1], c3(tX[1][:]), tP[1], Tb1, tEe[1], c3(tEe[1][:]), ALU.subtract)
                    Gt = (tP[0], tX[1])
                    S.act(tEa[0], tEa[0][:], Gt[0], Gt[0][:], AF.Exp)
                    S.act(tEb[0], tEb[0][:], Gt[0], Gt[0][:], AF.Exp, scale=-1.0)
                    S.act(tEe[0], tEe[0][:], tX[0], tX[0][:], AF.Exp)
                    S.act(tEe[1], tEe[1][:], tEe[1], tEe[1][:], AF.Exp)
                    S.act(tEa[1], tEa[1][:], Gt[1], Gt[1][:], AF.Exp)
                    S.act(tEb[1], tEb[1][:], Gt[1], Gt[1][:], AF.Exp, scale=-1.0)
                    for d in range(2):
                        S.stt(qdec[d], qdec[d][:, blk], qsb, qsb[:], HSCALE, tEa[d], tEa[d][:], ALU.mult, ALU.mult)
                        S.tt("dve", kinvT[d], kinvT[d][:], tA[d], tA[d][:], tEb[d], tEb[d][:], ALU.mult)
                        S.tt("pool", kendT[d], kendT[d][:], tA[d], tA[d][:], tEe[d], tEe[d][:], ALU.mult)
                        col = 63 if d == 0 else 0
                        S.cp("pool", egl[d], egl[d][:, b * 8:(b + 1) * 8], tEa[d], c3(tEa[d][:])[:, :, col])
                    for d in range(2):
                        bk = ps[6 + d]
                        for jj in range(4):
                            S.mmg(bk, bk[:, jj * 128:(jj + 1) * 128], kendT[d], kendT[d][:, jj * 128:(jj + 1) * 128], identb, identb[:], start=True, stop=True)
                        S.cp("act", kend_tok[d], kend_tok[d][:, b * 4:(b + 1) * 4, :], bk, bk[:, :].rearrange("p (j v) -> p j v", v=128))
                    for d in range(2):
                        for jj in range(4):
                            tk = slice(jj * 128, (jj + 1) * 128)
                            tq = slice(b * NB + jj * 128, b * NB + (jj + 1) * 128)
                            S.mmg(ps[5], ps[5][:, tk], kinvT[d], kinvT[d][:, tk], qdec[d], qdec[d][:, tq], start=True, stop=True)
                        mcol = C_MF if d == 0 else C_MB
                        S.tt("dve", attnT[d], attnT[d][:, b * 4:(b + 1) * 4, :], ps[5], ps[5][:, :].rearrange("p (j v) -> p j v", v=128),
                             cst, cst[:, mcol:mcol + 128].unsqueeze(1).to_broadcast([128, 4, 128]), ALU.mult)
                zi = [0, 0]
                for d in range(2):
                    S.memset("pool", Sst[d][0], Sst[d][0][:], 0.0)
                for step in range(64):
                    for d in range(2):
                        c = step if d == 0 else 63 - step
                        ub = ps[2 * (step % 2) + d]
                        usl = slice(0, 128)
                        rows = slice((c % 2) * 64, (c % 2) * 64 + 64)
                        S.mmg(ub, ub[:, usl], kend_tok[d], kend_tok[d][rows, c // 2, :], vtok, vtok[rows, c // 2, :], start=True, stop=True)
                        zc = Sst[d][zi[d] % 2]
                        zn = Sst[d][(zi[d] + 1) % 2]
                        zi[d] += 1
                        S.cp("act" if d == 0 else "pool", Sbf[d], Sbf[d][:, c, :], zc, zc[:])
                        S.stt(zn, zn[:], zc, zc[:], egl[d][:, c:c + 1], ub, ub[:, usl], ALU.mult, ALU.add, rd=[egl[d]])
                def out_mm(b):
                    ob = ps[4 + b % 2]
                    for jj in range(4):
                        tt_ = b * 4 + jj
                        osl = slice(jj * 128, (jj + 1) * 128)
                        S.mmg(ob, ob[:, osl], vtok, vtok[:, tt_, :], attnT[0], attnT[0][:, tt_, :], start=True, stop=False)
                        S.mmg(ob, ob[:, osl], vtok, vtok[:, tt_, :], attnT[1], attnT[1][:, tt_, :], start=False, stop=False)
                        for cc in range(2):
                            c = tt_ * 2 + cc
                            csl = slice(jj * 128 + cc * 64, jj * 128 + cc * 64 + 64)
                            qsl = slice(c * 64, (c + 1) * 64)
                            S.mmg(ob, ob[:, csl], Sbf[0], Sbf[0][:, c, :], qdec[0], qdec[0][:, qsl], start=False, stop=False)
                            S.mmg(ob, ob[:, csl], Sbf[1], Sbf[1][:, c, :], qdec[1], qdec[1][:, qsl], start=False, stop=(cc == 1))
                    S.act(osq[b % 2], osq[b % 2][:], ob, ob[:, :], AF.Square)

                def out_norm(b):
                    ob = ps[4 + b % 2]
                    rstd_from_sq(osq[b % 2], [osq[b % 2][:]], 128, ps[6 + b % 2], orstd, orstd[:])
                    S.stt(otmp, otmp[:], ob, ob[:, :], vec[:, V_ONORM + h:V_ONORM + h + 1], orstd, orstd[:], ALU.mult, ALU.mult, rd=[vec])
                    yb = yblk[b % 2]
                    S.tt("pool", yb, yb[:], otmp, otmp[:], sgT, sgT[:, b * NB:(b + 1) * NB], ALU.mult)
                    S.dma("sp", YT, YT[h * 128:(h + 1) * 128, s * SEQ + b * NB:s * SEQ + (b + 1) * NB], yb, yb[:], group=True)

                for b in range(NBLK):
                    out_mm(b)
                    if b >= 1:
                        out_norm(b - 1)
                out_norm(NBLK - 1)
                it += 1
        S.flush()
        if stop_after == "B":
            return nc

        def post_mixer(layer, wmix, kcm, last):
            S.begin()
            nring = 5
            ring = [S.sbuf(f"wr{i}", [128, 4096], BF16) for i in range(nring)]
            yin = S.sbuf("yin", [128, kcm, NB], BF16)
            hTl = [S.sbuf(f"hTl{i}", [128, 8, NB], F32) for i in range(2)]
            pin = S.sbuf("pin", [128, 4, 256], F32)
            pT = S.sbuf("pT", [128, 2, NB], BF16)
            mt = S.sbuf("mt", [128, 8, NB], F32)
            mtX = S.sbuf("mtX", [128, 8, NB], F32)
            sqc = S.sbuf("sqc", [128, 8, NB], BF16)
            rs = S.sbuf("rs", [128, NB], F32)
            a2 = S.sbuf("a2", [128, 8, NB], BF16)
            hbf = S.sbuf("hbf", [128, 8, NB], BF16)
            hid = S.sbuf("hid", [128, 32, NB], BF16)
            rl = [S.sbuf(f"rl{i}", [128, NB], F32) for i in range(2)]
            gsb = [S.sbuf(f"gsb{i}", [128, NB], F32) for i in range(2)]
            tmpn = [S.sbuf(f"tmpn{i}", [128, NB], F32) for i in range(2)]
            if not last:
                aTn = S.sbuf("aTn", [128, 8, NB], BF16)
            else:
                osg = [S.sbuf(f"osg{i}", [128, NB], F32) for i in range(2)]
            per = 4096 // (kcm * 128)

            order = [("C1", 0), ("C3", 0)]
            for b in range(nblk):
                if b + 1 < nblk:
                    order.append(("C1", b + 1))
                order.append(("C4", b))
                if b + 1 < nblk:
                    order.append(("C3", b + 1))
                order.append(("C5", b))
            chunks = []
            for st, b in order:
                if st == "C1":
                    for o in range(0, 8, per):
                        chunks.append((wmix, o, per, kcm * 128))
                elif st == "C3":
                    for o in range(0, 32, 4):
                        chunks.append((wb[f"w1_{layer}"], o, 4, 1024))
                elif st == "C4":
                    for o in range(8):
                        chunks.append((wb[f"w2_{layer}"], o, 1, 4096))
                else:
                    chunks.append((wb[f"wpp_{layer}"], 0, 8, 256))
                    for o in range(0, 8, 4):
                        chunks.append((wb[f"wpg_{layer}"], o, 4, 1024))
            loaded = {}
            gpos = [0]

            def wload(gi):
                if gi >= len(chunks) or gi in loaded:
                    return
                wt_, o0, n, el = chunks[gi]
                r = ring[gi % nring]
                S.dma("sp", r, r[:, 0:n * el].rearrange("p (o k) -> p o k", o=n), wt_, wt_[o0:o0 + n].rearrange("o p k -> p o k"))
                loaded[gi] = r

            def wget():
                gi = gpos[0]
                gpos[0] += 1
                for a in range(gi, gi + nring - 2):
                    wload(a)
                return loaded.pop(gi)

            chains = []

            def pop_one():
                while chains:
                    try:
                        next(chains[0][2])
                        chains[0][3] += 1
                        return
                    except StopIteration:
                        chains.pop(0)

            def ensure_steps(kind, b, n):
                while True:
                    tgt = [c for c in chains if c[0] == kind and c[1] == b]
                    if not tgt or tgt[0][3] >= n:
                        return
                    pop_one()

            def force(kind, b):
                while any(c[0] == kind and c[1] == b for c in chains):
                    pop_one()

            def drain():
                while chains:
                    pop_one()

            def ld_h(b):
                S.dma("sp", hTl[b % 2], hTl[b % 2][:], HT, hview(HT, b * NB))

            def ld_y(b):
                S.dma("sp", yin, yin[:], YT, hview(YT[0:kcm * 128], b * NB))

            def ld_p(b):
                s_, bb = divmod(b, NBLK)
                S.dma("sp", pin, pin[:], p_d, p_d[layer, s_, bb * NB:(bb + 1) * NB, :].rearrange("(j p) d -> p j d", p=128))

            def squares(src_t, src_ap):
                for c in range(8):
                    S.act(sqc, sqc[:, c, :], src_t, src_ap(c), AF.Square)

            def rstd_now():
                rstd_from_sq(sqc, [sqc[:, c, :] for c in range(8)], D, ps[7], rs, rs[:])

            def resid_apply(src_t, gcol, hT):
                for c in range(8):
                    tn = tmpn[c % 2]
                    S.tt("pool" if c % 2 == 0 else "dve", tn, tn[:], src_t, src_t[:, c, :], rs, rs[:], ALU.mult)
                    S.stt(hT, hT[:, c, :], tn, tn[:], vec[:, gcol + c:gcol + c + 1], hT, hT[:, c, :], ALU.mult, ALU.add, rd=[vec])

            def bf_apply(hT, gcol, dst):
                for c in range(8):
                    S.stt(dst, dst[:, c, :], hT, hT[:, c, :], vec[:, gcol + c:gcol + c + 1], rs, rs[:], ALU.mult, ALU.mult, rd=[vec])

            def chain_X(b):
                hT = hTl[b % 2]
                ld_h(b)
                squares(mtX, lambda c: mtX[:, c, :])
                yield
                rstd_now()
                resid_apply(mtX, V_POST_MIX + 8 * layer, hT)
                squares(hT, lambda c: hT[:, c, :])
                yield
                rstd_now()
                bf_apply(hT, V_PRE_MLP + 8 * layer, a2)

            def chain_F(b):
                hT = hTl[b % 2]
                squares(mt, lambda c: mt[:, c, :])
                yield
                rstd_now()
                resid_apply(mt, V_POST_MLP + 8 * layer, hT)
                for c in range(8):
                    S.cp("act", hbf, hbf[:, c, :], hT, hT[:, c, :])

            def chain_T(b):
                hT = hTl[b % 2]
                squares(mt, lambda c: mt[:, c, :])
                yield
                rstd_now()
                resid_apply(mt, V_PLE + 8 * layer, hT)
                if not last:
                    S.dma("sp", HT, hview(HT, b * NB), hT, hT[:], group=True)
                    squares(hT, lambda c: hT[:, c, :])
                    yield
                    rstd_now()
                    bf_apply(hT, V_PRE_MIX + 8 * (layer + 1), aTn)
                    S.dma("sp", AT, hview(AT, b * NB), aTn, aTn[:], group=True)
                else:
                    yield
                    s_, bb = divmod(b, NBLK)
                    k = 0
                    for j in range(4):
                        for half in range(2):
                            bank = ps[4 + k % 2]
                            for q in range(4):
                                kc = half * 4 + q
                                if q == 0:
                                    S.op("pe", (lambda o, i_: (lambda e: e.transpose(out=o, in_=i_, identity=identf[:])))(bank[:, 0:128], hT[:, kc, j * 128:(j + 1) * 128]), reads=[hT, identf], writes=[bank])
                                else:
                                    S.tr(bank, bank[:, q * 128:(q + 1) * 128], hT, hT[:, kc, j * 128:(j + 1) * 128], identf, identf[:])
                            og = osg[k % 2]
                            S.cp("dve", og, og[:], bank, bank[:, :])
                            S.dma("sp", out_d, out_d[s_, bb * NB + j * 128:bb * NB + (j + 1) * 128, half * 512:(half + 1) * 512], og, og[:])
                            k += 1
                            if k % 4 == 0:
                                yield

            def start_chain(kind, b, gen):
                chains.append([kind, b, gen, 0])

            def st_C1(b):
                oc = 0
                for _ in range(8 // per):
                    r = wget()
                    for o in range(per):
                        bank = ps[oc % 6]
                        for kc in range(kcm):
                            S.mm(bank, bank[:, :], r, r[:, o * kcm * 128 + kc * 128:o * kcm * 128 + (kc + 1) * 128], yin, yin[:, kc, :], start=(kc == 0), stop=(kc == kcm - 1))
                        S.cp("act", mtX, mtX[:, oc, :], bank, bank[:, :])
                        oc += 1
                if b + 1 < nblk:
                    ld_y(b + 1)
                start_chain("X", b, chain_X(b))
                if b >= 2:
                    ensure_steps("T", b - 2, 2)

            def st_C3(b):
                force("X", b)
                oc = 0
                for _ in range(8):
                    r = wget()
                    for o in range(4):
                        bank = ps[oc % 6]
                        for kc in range(8):
                            S.mm(bank, bank[:, :], r, r[:, o * 1024 + kc * 128:o * 1024 + (kc + 1) * 128], a2, a2[:, kc, :], start=(kc == 0), stop=(kc == 7))
                        rr = rl[oc % 2]
                        S.act(rr, rr[:], bank, bank[:, :], AF.Relu)
                        S.tt("pool", hid, hid[:, oc, :], rr, rr[:], rr, rr[:], ALU.mult)
                        oc += 1
                        if oc % 4 == 0:
                            pop_one()

            def st_C4(b):
                if b >= 1:
                    ensure_steps("T", b - 1, 2)
                for oc in range(8):
                    r = wget()
                    bank = ps[oc % 6]
                    for kc in range(32):
                        S.mm(bank, bank[:, :], r, r[:, kc * 128:(kc + 1) * 128], hid, hid[:, kc, :], start=(kc == 0), stop=(kc == 31))
                    S.cp("act", mt, mt[:, oc, :], bank, bank[:, :])
                    if oc % 2 == 0:
                        pop_one()
                start_chain("F", b, chain_F(b))
                force_first("F", b)

            def force_first(kind, b):
                tgt = [c for c in chains if c[0] == kind and c[1] == b][0]
                while chains and chains[0] is not tgt:
                    pop_one()
                if chains and chains[0] is tgt:
                    pop_one()

            def st_pT(b):
                for kc in range(2):
                    bank = ps[6]
                    for j in range(4):
                        if j == 0:
                            S.op("pe", (lambda o, i_: (lambda e: e.transpose(out=o, in_=i_, identity=identf[:])))(bank[:, 0:128], pin[:, 0, kc * 128:(kc + 1) * 128]), reads=[pin, identf], writes=[bank])
                        else:
                            S.tr(bank, bank[:, j * 128:(j + 1) * 128], pin, pin[:, j, kc * 128:(kc + 1) * 128], identf, identf[:])
                    S.cp("dve", pT, pT[:, kc, :], bank, bank[:, :])

            def st_C5(b):
                force("F", b)
                if b >= 1:
                    force("T", b - 1)
                st_pT(b)
                if b + 1 < nblk:
                    ld_p(b + 1)
                rp = wget()
                oc = 0
                for _ in range(2):
                    r = wget()
                    for o in range(4):
                        bg = ps[oc % 2]
                        be = ps[2 + oc % 2]
                        for kc in range(8):
                            S.mm(bg, bg[:, :], r, r[:, o * 1024 + kc * 128:o * 1024 + (kc + 1) * 128], hbf, hbf[:, kc, :], start=(kc == 0), stop=(kc == 7))
                        for kc in range(2):
                            S.mm(be, be[:, :], rp, rp[:, oc * 256 + kc * 128:oc * 256 + (kc + 1) * 128], pT, pT[:, kc, :], start=(kc == 0), stop=(kc == 1))
                        gg = gsb[oc % 2]
                        S.act(gg, gg[:], bg, bg[:, :], AF.Sigmoid)
                        S.tt("dve", mt, mt[:, oc, :], be, be[:, :], gg, gg[:], ALU.mult)
                        oc += 1
                start_chain("T", b, chain_T(b))
                force_first("T", b)

            ld_y(0)
            ld_p(0)
            for st, b in order:
                {"C1": st_C1, "C3": st_C3, "C4": st_C4, "C5": st_C5}[st](b)
            drain()
            S.flush()

        post_mixer(0, wb["hg_w_out"], 8, last=False)
        if stop_after == "C":
            return nc

        CKV = scratch("CKV", [256, T], BF16)
        KR = scratch("KR", [64, T], BF16)
        KSS = scratch("KSS", [1, T], F32)
        S.begin()
        ASCALE = 192.0 ** -0.5
        win_t = S.sbuf("win_t", [128, 6, 1024], BF16)
        wq_t = S.sbuf("wq_t", [128, 16, 768], BF16)
        S.dma("sp", win_t, win_t[:], wb["mla_w_in"], wb["mla_w_in"][:].rearrange("o p k -> p o k"))
        S.dma("sp", wq_t, wq_t[:], wb["mla_w_uq"], wb["mla_w_uq"][:].rearrange("o p k -> p o k"))
        atD = [S.sbuf(f"atD{i}", [128, 8, NB], BF16) for i in range(2)]
        posi = S.sbuf("posi", [64, NB], I32)
        ang = S.sbuf("ang", [64, NB], F32)
        angn = S.sbuf("angn", [64, NB], F32)
        angi = S.sbuf("angi", [64, NB], I32)
        cosT = S.sbuf("cosT", [64, NB], F32)
        sinT = S.sbuf("sinT", [64, NB], F32)
        cq_sb = S.sbuf("cq_sb", [128, 3, NB], F32)
        cq_sq = S.sbuf("cq_sq", [128, 3, NB], BF16)
        cqn = S.sbuf("cqn", [128, 3, NB], BF16)
        kv_sb = S.sbuf("kv_sb", [128, 2, NB], F32)
        kv_sq = S.sbuf("kv_sq", [128, 2, NB], BF16)
        ckv_b = [S.sbuf(f"ckv_b{i}", [128, 2, NB], BF16) for i in range(2)]
        rsD = S.sbuf("rsD", [128, NB], F32)
        kr_b = [S.sbuf(f"kr_b{i}", [64, NB], BF16) for i in range(2)]
        kr_sq = S.sbuf("kr_sq", [64, NB], BF16)
        kss_b = [S.sbuf(f"kss_b{i}", [65, NB], F32) for i in range(2)]
        r1 = [S.sbuf(f"r1_{i}", [64, NB], F32) for i in range(2)]
        r2 = [S.sbuf(f"r2_{i}", [64, NB], F32) for i in range(2)]
        qn_b = [S.sbuf(f"qn_b{i}", [128, NB], BF16) for i in range(3)]
        qn_sq = [S.sbuf(f"qn_sq{i}", [128, NB], BF16) for i in range(2)]
        qr_sq = [S.sbuf(f"qr_sq{i}", [64, NB], BF16) for i in range(2)]
        qr_b = [S.sbuf(f"qr_b{i}", [65, NB], BF16) for i in range(3)]
        qnl = [S.sbuf(f"qnl{i}", [65, NB], F32) for i in range(2)]
        TWO_PI_HI = 6.28125
        TWO_PI_LO = 0.0019353071795864769
        hi = 0
        for bi in range(nblk):
            s, b = divmod(bi, NBLK)
            t0 = bi * NB
            at = atD[bi % 2]
            S.dma("sp", at, at[:], AT, hview(AT, t0))
            S.dma("sp", posi, posi[:], pos_d, pos_d[s:s + 1, b * NB:(b + 1) * NB].partition_broadcast(64))
            S.cp("dve", ang, ang[:], posi, posi[:])
            S.tsc("dve", ang, ang[:], ang, ang[:], cst[0:64, C_INVF:C_INVF + 1], None, ALU.mult, rd=[cst])
            for tab, shift in ((sinT, 0.0), (cosT, float(np.pi / 2))):
                S.tsc("dve", angn, angn[:], ang, ang[:], shift, float(1.0 / (2 * np.pi)), ALU.add, ALU.mult)
                S.cp("dve", angi, angi[:], angn, angn[:])
                S.cp("dve", angn, angn[:], angi, angi[:])
                S.tsc("dve", tab, tab[:], ang, ang[:], shift, None, ALU.add)
                S.stt(tab, tab[:], angn, angn[:], -TWO_PI_HI, tab, tab[:], ALU.mult, ALU.add)
                S.stt(tab, tab[:], angn, angn[:], -TWO_PI_LO, tab, tab[:], ALU.mult, ALU.add)
                S.tsc("dve", tab, tab[:], tab, tab[:], float(np.pi), float(-np.pi), ALU.min, ALU.max)
                S.act(tab, tab[:], tab, tab[:], AF.Sin)
            S.tsc("dve", sinT, sinT[0:32, :], sinT, sinT[0:32, :], -1.0, None, ALU.mult)
            for oc in range(3):
                bank = ps[oc % 4]
                for kc in range(8):
                    S.mm(bank, bank[:, :], win_t, win_t[:, oc, kc * 128:(kc + 1) * 128], at, at[:, kc, :], start=(kc == 0), stop=(kc == 7))
                S.cp("dve", cq_sb, cq_sb[:, oc, :], bank, bank[:, :])
                S.act(cq_sq, cq_sq[:, oc, :], bank, bank[:, :], AF.Square)
            rstd_from_sq(cq_sq, [cq_sq[:, c, :] for c in range(3)], 384, ps[7], rsD, rsD[:])
            for c in range(3):
                S.stt(cqn, cqn[:, c, :], cq_sb, cq_sb[:, c, :], vec[:, V_QNORM + c:V_QNORM + c + 1], rsD, rsD[:], ALU.mult, ALU.mult, rd=[vec])
            for oc in range(2):
                bank = ps[oc % 4]
                for kc in range(8):
                    S.mm(bank, bank[:, :], win_t, win_t[:, 3 + oc, kc * 128:(kc + 1) * 128], at, at[:, kc, :], start=(kc == 0), stop=(kc == 7))
                S.cp("dve", kv_sb, kv_sb[:, oc, :], bank, bank[:, :])
                S.act(kv_sq, kv_sq[:, oc, :], bank, bank[:, :], AF.Square)
            rstd_from_sq(kv_sq, [kv_sq[:, c, :] for c in range(2)], 256, ps[7], rsD, rsD[:])
            ck = ckv_b[bi % 2]
            for c in range(2):
                S.stt(ck, ck[:, c, :], kv_sb, kv_sb[:, c, :], vec[:, V_KVNORM + c:V_KVNORM + c + 1], rsD, rsD[:], ALU.mult, ALU.mult, rd=[vec])
            S.dma("sp", CKV, hview(CKV, t0), ck, ck[:], group=True)
            for half, bank in ((0, ps[2]), (1, ps[3])):
                for kc in range(8):
                    S.mm(bank, bank[0:64, :], win_t, win_t[:, 5, kc * 128 + half * 64:kc * 128 + half * 64 + 64], at, at[:, kc, :], start=(kc == 0), stop=(kc == 7))
            krb = kr_b[bi % 2]
            S.tt("dve", r1[0], r1[0][:], ps[2], ps[2][0:64, :], cosT, cosT[:], ALU.mult)
            S.tt("dve", r2[0], r2[0][:], ps[3], ps[3][0:64, :], sinT, sinT[:], ALU.mult)
            S.tt("pool", krb, krb[:], r1[0], r1[0][:], r2[0], r2[0][:], ALU.add)
            S.act(kr_sq, kr_sq[:], ps[2], ps[2][0:64, :], AF.Square)
            S.dma("sp", KR, KR[:, t0:t0 + NB], krb, krb[:], group=True)
            S.mm(ps[7], ps[7][0:65, :], onesb, onesb[0:64, 0:65], kr_sq, kr_sq[:], start=True, stop=True)
            kb = kss_b[bi % 2]
            S.cp("dve", kb, kb[64:65, :], ps[7], ps[7][64:65, :])
            S.dma("sp", KSS, KSS[0:1, t0:t0 + NB], kb, kb[64:65, :], group=True)
            for h in range(16):
                bn, br, bs = ps[(3 * hi) % 6], ps[(3 * hi + 1) % 6], ps[(3 * hi + 2) % 6]
                for kc in range(3):
                    S.mm(bn, bn[:, :], wq_t, wq_t[:, h, kc * 256:kc * 256 + 128], cqn, cqn[:, kc, :], start=(kc == 0), stop=(kc == 2))
                for kc in range(3):
                    S.mm(br, br[0:64, :], wq_t, wq_t[:, h, kc * 256 + 128:kc * 256 + 192], cqn, cqn[:, kc, :], start=(kc == 0), stop=(kc == 2))
                for kc in range(3):
                    S.mm(bs, bs[0:64, :], wq_t, wq_t[:, h, kc * 256 + 192:kc * 256 + 256], cqn, cqn[:, kc, :], start=(kc == 0), stop=(kc == 2))
                qn = qn_b[hi % 3]
                qr = qr_b[hi % 3]
                S.cp("act", qn, qn[:], bn, bn[:, :])
                S.act(qn_sq[hi % 2], qn_sq[hi % 2][:], bn, bn[:, :], AF.Square)
                S.act(qr_sq[hi % 2], qr_sq[hi % 2][:], br, br[0:64, :], AF.Square)
                S.tt("dve", r1[hi % 2], r1[hi % 2][:], br, br[0:64, :], cosT, cosT[:], ALU.mult)
                S.tt("dve", r2[hi % 2], r2[hi % 2][:], bs, bs[0:64, :], sinT, sinT[:], ALU.mult)
                S.tt("pool", qr, qr[0:64, :], r1[hi % 2], r1[hi % 2][:], r2[hi % 2], r2[hi % 2][:], ALU.add)
                bq = ps[6 + hi % 2]
                S.mm(bq, bq[0:65, :], onesb, onesb[:, 0:65], qn_sq[hi % 2], qn_sq[hi % 2][:], start=True, stop=False)
                S.mm(bq, bq[0:65, :], onesb, onesb[0:64, 0:65], qr_sq[hi % 2], qr_sq[hi % 2][:], start=False, stop=True)
                ql = qnl[hi % 2]
                S.act(ql, ql[64:65, :], bq, bq[64:65, :], AF.Ln)
                S.act(ql, ql[64:65, :], ql, ql[64:65, :], AF.Exp, scale=0.5)
                S.tsc("dve", qr, qr[64:65, :], ql, ql[64:65, :], -1.0, None, ALU.mult)
                S.dma("sp", QN, QN[h, :, t0:t0 + NB], qn, qn[:], group=True)
                S.dma("sp", QR, QR[h, :, t0:t0 + NB], qr, qr[:], group=True)
                hi += 1
        S.flush()
        if stop_after == "D":
            return nc

        S.begin()
        wkv_t = S.sbuf("wkv_t", [128, 16, 512], BF16)
        S.dma("sp", wkv_t, wkv_t[:], wb["mla_w_ukv"], wb["mla_w_ukv"][:].rearrange("o p k -> p o k"))
        ckv_s = S.sbuf("ckv_s", [128, 2, SEQ], BF16)
        krs = [S.sbuf(f"krs{i}", [65, SEQ], BF16) for i in range(2)]
        kss_s = S.sbuf("kss_s", [65, SEQ], F32)
        knT = [S.sbuf(f"knT{i}", [128, SEQ], BF16) for i in range(2)]
        kn_sq = S.sbuf("kn_sq", [128, NB], BF16)
        vh = [S.sbuf(f"vh{i}", [128, 32, 132], BF16) for i in range(2)]
        ktot = S.sbuf("ktot", [65, SEQ], F32)
        kmax = S.sbuf("kmax", [65, 2], F32)
        qnE = [S.sbuf(f"qnE{i}", [128, NB], BF16) for i in range(2)]
        qrE = [S.sbuf(f"qrE{i}", [65, NB], BF16) for i in range(2)]
        oT_b = [S.sbuf(f"oT_b{i}", [128, NB], BF16) for i in range(2)]
        for i in range(2):
            S.memset("pool", vh[i], vh[i][:], 1.0)
        pT4 = [S.sbuf(f"pT4_{i}", [128, NB], BF16) for i in range(4)]
        accD = [[S.sbuf(f"accD{i}_{j}", [128, NB], F32) for j in range(4)] for i in range(2)]
        accP = [S.sbuf(f"accP{i}", [128, NB], F32) for i in range(2)]
        onesf = S.sbuf("onesf", [128, 128], F32)
        recE = [S.sbuf(f"recE{i}", [128, NB], F32) for i in range(2)]
        S.memset("dve", onesf, onesf[:], 1.0)

        def prep_head(h, slot):
            kn, v, kr = knT[slot], vh[slot], krs[slot]
            for b in range(NBLK):
                bank = ps[6 + b % 2]
                for kc in range(2):
                    S.mm(bank, bank[:, :], wkv_t, wkv_t[:, h, kc * 256:kc * 256 + 128], ckv_s, ckv_s[:, kc, b * NB:(b + 1) * NB], start=(kc == 0), stop=(kc == 1))
                S.cp("dve", kn, kn[:, b * NB:(b + 1) * NB], bank, bank[:, :])
                S.tt("pool", kn_sq, kn_sq[:], kn, kn[:, b * NB:(b + 1) * NB], kn, kn[:, b * NB:(b + 1) * NB], ALU.mult)
                bq = ps[6 + (b + 1) % 2]
                S.mm(bq, bq[0:65, :], onesb, onesb[:, 0:65], kn_sq, kn_sq[:], start=True, stop=True)
                S.tt("dve", ktot, ktot[64:65, b * NB:(b + 1) * NB], bq, bq[64:65, :], kss_s, kss_s[64:65, b * NB:(b + 1) * NB], ALU.add)
            S.op("dve", (lambda o, i_: (lambda e: e.reduce_max(out=o, in_=i_, axis=mybir.AxisListType.X)))(kmax[64:65, 0:1], ktot[64:65, :]), reads=[ktot], writes=[kmax])
            S.act(kmax, kmax[64:65, 1:2], kmax, kmax[64:65, 0:1], AF.Ln)
            S.act(kmax, kmax[64:65, 1:2], kmax, kmax[64:65, 1:2], AF.Exp, scale=0.5)
            S.memset("pool", kr, kr[64:65, :], 1.0)
            S.tsc("dve", kr, kr[64:65, :], kr, kr[64:65, :], kmax[64:65, 1:2], None, ALU.mult, rd=[kmax])
            for tt_ in range(32):
                bank = ps[6 + (tt_ // 4) % 2]
                sl = slice((tt_ % 4) * 128, (tt_ % 4 + 1) * 128)
                for kc in range(2):
                    S.mmg(bank, bank[:, sl], ckv_s, ckv_s[:, kc, tt_ * 128:(tt_ + 1) * 128], wkv_t, wkv_t[:, h, kc * 256 + 128:kc * 256 + 256], start=(kc == 0), stop=(kc == 1))
                if tt_ % 4 == 3:
                    S.cp("dve", v, v[:, tt_ - 3:tt_ + 1, 0:128], bank, bank[:, :].rearrange("p (j v) -> p j v", v=128))

        def load_q(s, h, qb, slot):
            t0 = s * SEQ + qb * NB
            S.dma("sp", qnE[slot], qnE[slot][:], QN, QN[h, :, t0:t0 + NB])
            S.dma("sp", qrE[slot], qrE[slot][:], QR, QR[h, :, t0:t0 + NB])

        LOOK = 3
        hcount = 0
        for s in range(nseq):
            S.dma("sp", ckv_s, ckv_s[:], CKV, hview(CKV, s * SEQ, SEQ))
            S.dma("sp", kss_s, kss_s[64:65, :], KSS, KSS[0:1, s * SEQ:(s + 1) * SEQ])
            for i in range(2):
                S.dma("sp", krs[i], krs[i][0:64, :], KR, KR[:, s * SEQ:(s + 1) * SEQ])
            items = [(h, qb, kt) for h in range(16) for qb in range(NBLK) for kt in range(32)]
            nit = len(items)

            def hslot(h):
                return (hcount + h) % 2

            def qslot(h, qb):
                return (h * NBLK + qb) % 2

            def qk(i_):
                h, qb, kt = items[i_]
                sb_ = ps[i_ % 4]
                ks = slice(kt * 128, (kt + 1) * 128)
                kn, kr = knT[hslot(h)], krs[hslot(h)]
                qn, qr = qnE[qslot(h, qb)], qrE[qslot(h, qb)]
                S.mm(sb_, sb_[:, :], kn, kn[:, ks], qn, qn[:], start=True, stop=False)
                S.mm(sb_, sb_[:, :], kr, kr[0:65, ks], qr, qr[0:65, :], start=False, stop=True)

            deferred = []
            prep_head(0, hslot(0))
            load_q(s, 0, 0, qslot(0, 0))
            for j in range(LOOK):
                qk(j)
            for i_, it in enumerate(items):
                h, qb, kt = it
                qbi = h * NBLK + qb
                if kt == 0:
                    nh, nqb = (h, qb + 1) if qb + 1 < NBLK else (h + 1, 0)
                    if nh < 16:
                        load_q(s, nh, nqb, qslot(nh, nqb))
                if kt == 4 and qb == 3 and h + 1 < 16:
                    prep_head(h + 1, hslot(h + 1))
                if i_ + LOOK < nit:
                    qk(i_ + LOOK)
                sb_ = ps[i_ % 4]
                pt = pT4[i_ % 4]
                S.act(pt, pt[:], sb_, sb_[:, :], AF.Exp, scale=ASCALE)
                v = vh[hslot(h)]
                ob = ps[4 + qbi % 2]
                S.mm(ob, ob[:, :], v, v[:, kt, 0:128], pt, pt[:], start=(kt == 0), stop=(kt == 31))
                if False:
                    pass
                else:
                    ad_ = accD[qbi % 2][kt % 4]
                    if kt < 4:
                        S.cp("dve", ad_, ad_[:], pt, pt[:])
                    else:
                        S.tt("dve", ad_, ad_[:], ad_, ad_[:], pt, pt[:], ALU.add)
                if kt == 3 and deferred:
                    for f in deferred:
                        f()
                    deferred = []
                if kt == 31:
                    def fin(h=h, qb=qb, s=s, qbi=qbi):
                        t0 = s * SEQ + qb * NB
                        ad_, ap_, ob = accD[qbi % 2], accP[qbi % 2], ps[4 + qbi % 2]
                        rec = recE[qbi % 2]
                        oT = oT_b[qbi % 2]
                        bd = ps[6 + qbi % 2]
                        S.tt("pool", ad_[0], ad_[0][:], ad_[0], ad_[0][:], ad_[1], ad_[1][:], ALU.add)
                        S.tt("pool", ad_[2], ad_[2][:], ad_[2], ad_[2][:], ad_[3], ad_[3][:], ALU.add)
                        S.tt("pool", ad_[0], ad_[0][:], ad_[0], ad_[0][:], ad_[2], ad_[2][:], ALU.add)
                        S.mm(bd, bd[:, :], onesf, onesf[:], ad_[0], ad_[0][:], start=True, stop=True)
                        S.act(rec, rec[:], bd, bd[:, :], AF.Ln)
                        S.act(rec, rec[:], rec, rec[:], AF.Exp, scale=-1.0)
                        S.tt("dve", oT, oT[:], ob, ob[:, :], rec, rec[:], ALU.mult)
                        S.dma("sp", YT, YT[h * 128:(h + 1) * 128, t0:t0 + NB], oT, oT[:], group=True)
                    deferred.append(fin)
            for f in deferred:
                f()
            hcount += 16
        S.flush()
        if stop_after == "E":
            return nc

        post_mixer(1, wb["mla_w_o"], 16, last=True)
    return nc


_CACHE = {}


def kernel(x, p, positions, pre_mix_norm, post_mix_norm, pre_mlp_norm, post_mlp_norm,
           w_mlp_in, w_mlp_out, w_ple_proj, w_ple_gate, ple_norm,
           hg_lb_logits, hg_w_in, hg_o_norm, hg_w_out,
           mla_w_in, mla_q_norm, mla_w_uq, mla_kv_norm, mla_w_ukv, mla_w_o):
    inp = dict(x=x, p=p, positions=positions, pre_mix_norm=pre_mix_norm, post_mix_norm=post_mix_norm,
               pre_mlp_norm=pre_mlp_norm, post_mlp_norm=post_mlp_norm, w_mlp_in=w_mlp_in,
               w_mlp_out=w_mlp_out, w_ple_proj=w_ple_proj, w_ple_gate=w_ple_gate, ple_norm=ple_norm,
               hg_lb_logits=hg_lb_logits, hg_w_in=hg_w_in, hg_o_norm=hg_o_norm, hg_w_out=hg_w_out,
               mla_w_in=mla_w_in, mla_q_norm=mla_q_norm, mla_w_uq=mla_w_uq, mla_kv_norm=mla_kv_norm,
               mla_w_ukv=mla_w_ukv, mla_w_o=mla_w_o)
    inp = {k: np.asarray(v) for k, v in inp.items()}
    w = prep_weights(inp)
    nseq = 2
    if "nc" not in _CACHE:
        _CACHE["nc"] = build(nseq=nseq)
    nc = _CACHE["nc"]
    in_maps = []
    for c in range(NCORES):
        m = dict(w)
        m["x"] = np.ascontiguousarray(inp["x"][c * nseq:(c + 1) * nseq], dtype=np.float32)
        m["p"] = np.ascontiguousarray(inp["p"][:, c * nseq:(c + 1) * nseq], dtype=np.float32)
        m["pos"] = np.ascontiguousarray(inp["positions"][c * nseq:(c + 1) * nseq], dtype=np.int32)
        in_maps.append(m)
    res = run_bass_kernel_spmd(nc, in_maps, core_ids=list(range(NCORES)))
    return np.concatenate([r["out"] for r in res.results], axis=0).astype(np.float32)
```
